# Optimizing a Trainium2 kernel written in Bass

```python
import math
import jax, jax.numpy as jnp
from jax import lax
import numpy as np

D_MODEL = 2048
BATCH = 2
SEQ = 16384
DEPTH = 1
DEC_BATCH = 4
DEC_SEQ = 2048
PAST_LEN = 128

H_A = 8
HD_A = 64
DIL_CONFIGS = ((128, 1), (512, 4), (2048, 16))
N_GROUPS_B = 3
H_B = 4
HD_B = 128
D_FF = 5632
CONV_W = 3
Q_BLOCK = 128
NORM_EPS = 1e-6
SUBLN_EPS = 1e-5
NEG_INF = -1e30

QA_W = H_A * 2 * HD_A
VA_W = H_A * 2 * HD_A
QB_W = N_GROUPS_B * H_B * HD_B
OA_W = VA_W
OB_W = H_B * HD_B
SPLIT_SIZES = [QA_W, QA_W, VA_W, QB_W, QB_W, QB_W, D_MODEL, D_MODEL]
SPLIT_IDX = [int(v) for v in np.cumsum(SPLIT_SIZES)[:-1]]
N_IN = int(sum(SPLIT_SIZES))

kernel_name = 'hybrid_diff_dilated_encoder'


def alibi_slopes(n):
    return jnp.asarray(2.0 ** (-8.0 * (np.arange(n) + 1) / n), dtype=jnp.float32)


def rmsnorm(x, g, eps):
    xf = x.astype(jnp.float32)
    y = xf * lax.rsqrt(jnp.mean(xf * xf, axis=-1, keepdims=True) + eps)
    return (y * g.astype(jnp.float32)).astype(x.dtype)


def diff_attention(q, k, v, lam, slopes):
    B, S = q.shape[:2]
    nb = S // Q_BLOCK
    scale = HD_A ** -0.5
    kpos = jnp.arange(S, dtype=jnp.float32)
    qblocks = q.reshape(B, nb, Q_BLOCK, H_A, 2, HD_A).swapaxes(0, 1)
    starts = jnp.arange(nb, dtype=jnp.float32) * Q_BLOCK

    def one_block(args):
        qb, start = args
        s = jnp.einsum('bqhcd,bkhcd->bhcqk', qb, k, preferred_element_type=jnp.float32) * scale
        qpos = start + jnp.arange(Q_BLOCK, dtype=jnp.float32)
        dist = jnp.abs(qpos[:, None] - kpos[None, :])
        s = s - slopes[None, :, None, None, None] * dist
        p = jax.nn.softmax(s, axis=-1)
        w = p[:, :, 0] - lam * p[:, :, 1]
        return jnp.einsum('bhqk,bkhe->bqhe', w.astype(v.dtype), v)

    out = lax.map(one_block, (qblocks, starts))
    return out.swapaxes(0, 1).reshape(B, S, H_A, 2 * HD_A)


def to_strided(x, r):
    B, S = x.shape[:2]
    rest = x.shape[2:]
    return x.reshape((B, S // r, r) + rest).swapaxes(1, 2).reshape((B * r, S // r) + rest)


def from_strided(x, r, B):
    N, L = x.shape[:2]
    rest = x.shape[2:]
    return x.reshape((B, r, L) + rest).swapaxes(1, 2).reshape((B, L * r) + rest)


def banded_attention(q, k, v, slopes, dil, half):
    N, L, H, D = q.shape
    qb = min(Q_BLOCK, L)
    nb = -(-L // qb)
    Lp = nb * qb
    kw = qb + 2 * half
    scale = D ** -0.5
    qp = jnp.pad(q, ((0, 0), (0, Lp - L), (0, 0), (0, 0))).reshape(N, nb, qb, H, D)
    pad_k = ((0, 0), (half, Lp - L + half), (0, 0), (0, 0))
    kp = jnp.pad(k, pad_k)
    vp = jnp.pad(v, pad_k)
    kidx = jnp.arange(nb)[:, None] * qb + jnp.arange(kw)[None, :]
    kb = kp[:, kidx]
    vb = vp[:, kidx]
    s = jnp.einsum('nbqhd,nbkhd->nbhqk', qp, kb, preferred_element_type=jnp.float32) * scale
    qpos = jnp.arange(nb)[:, None] * qb + jnp.arange(qb)[None, :]
    kpos = kidx - half
    rel = kpos[:, None, :] - qpos[:, :, None]
    valid = (jnp.abs(rel) <= half) & (kpos[:, None, :] >= 0) & (kpos[:, None, :] < L)
    dist = (dil * jnp.abs(rel)).astype(jnp.float32)
    bias = -(slopes[:, None, None, None] * dist[None]).transpose(1, 0, 2, 3)[None]
    s = jnp.where(valid[None, :, None], s + bias, NEG_INF)
    lse = jax.nn.logsumexp(s, axis=-1)
    p = jnp.exp(s - lse[..., None])
    o = jnp.einsum('nbhqk,nbkhd->nbqhd', p.astype(v.dtype), vb).reshape(N, Lp, H, D)[:, :L]
    lse = lse.transpose(0, 1, 3, 2).reshape(N, Lp, H)[:, :L]
    return o, lse


def dilated_mixer(q, k, v, slopes_b):
    B, S = q.shape[:2]
    outs, lses = [], []
    for g, (window, dil) in enumerate(DIL_CONFIGS):
        half = (window // 2) // dil
        o, lse = banded_attention(to_strided(q[:, :, g], dil), to_strided(k[:, :, g], dil),
                                  to_strided(v[:, :, g], dil), slopes_b[g], dil, half)
        outs.append(from_strided(o, dil, B))
        lses.append(from_strided(lse, dil, B))
    alpha = jax.nn.softmax(jnp.stack(lses), axis=0)
    out = jnp.einsum('gbsh,gbshd->bshd', alpha, jnp.stack(outs).astype(jnp.float32))
    return out.astype(q.dtype).reshape(B, S, OB_W)


def dwconv_centred(u, w, b):
    up = jnp.pad(u, ((0, 0), (1, 1), (0, 0)))
    return up[:, :-2] * w[0] + up[:, 1:-1] * w[1] + up[:, 2:] * w[2] + b


def encoder_layer(x, layer_idx, g_mix_norm, w_in, g_qa, g_ka, lam_q1, lam_k1, lam_q2, lam_k2,
                  g_subln, g_qb, g_kb, w_pa, w_pb, w_o, g_ffn_norm, w_up, conv_w, conv_b, w_down):
    B, S, _ = x.shape
    h = rmsnorm(x, g_mix_norm, NORM_EPS)
    z = h @ w_in
    qa, ka, va, qb, kb, vb, gate_a, gate_b = jnp.split(z, SPLIT_IDX, axis=-1)

    qa = rmsnorm(qa.reshape(B, S, H_A, 2, HD_A), g_qa, NORM_EPS)
    ka = rmsnorm(ka.reshape(B, S, H_A, 2, HD_A), g_ka, NORM_EPS)
    va = va.reshape(B, S, H_A, 2 * HD_A)
    lam_init = 0.8 - 0.6 * math.exp(-0.3 * layer_idx)
    lam = (jnp.exp(jnp.sum(lam_q1.astype(jnp.float32) * lam_k1.astype(jnp.float32)))
           - jnp.exp(jnp.sum(lam_q2.astype(jnp.float32) * lam_k2.astype(jnp.float32))) + lam_init)
    oa = diff_attention(qa, ka, va, lam, alibi_slopes(H_A))
    oa = (rmsnorm(oa, g_subln, SUBLN_EPS) * (1.0 - lam_init)).reshape(B, S, OA_W)

    qb = rmsnorm(qb.reshape(B, S, N_GROUPS_B, H_B, HD_B), g_qb, NORM_EPS)
    kb = rmsnorm(kb.reshape(B, S, N_GROUPS_B, H_B, HD_B), g_kb, NORM_EPS)
    vb = vb.reshape(B, S, N_GROUPS_B, H_B, HD_B)
    ob = dilated_mixer(qb, kb, vb, alibi_slopes(N_GROUPS_B * H_B).reshape(N_GROUPS_B, H_B))

    merged = jax.nn.sigmoid(gate_a) * (oa @ w_pa) + jax.nn.sigmoid(gate_b) * (ob @ w_pb)
    x = x + merged @ w_o

    h2 = rmsnorm(x, g_ffn_norm, NORM_EPS)
    u = dwconv_centred(h2 @ w_up, conv_w, conv_b)
    ug, uv = jnp.split(u, [D_FF], axis=-1)
    return x + (jax.nn.gelu(ug, approximate=False) * uv) @ w_down


def setup_inputs(seed: int = 0) -> dict:
    key = jax.random.key(seed)
    ks = jax.random.split(key, 24)
    f32 = jnp.float32
    nrm = lambda k, shape, s: jax.random.normal(k, shape, f32) * s
    gain = lambda k, shape: 1.0 + 0.02 * jax.random.normal(k, shape, f32)
    return {
        'x_prompt': jax.random.normal(ks[0], (BATCH, SEQ, D_MODEL), f32),
        'x_sample': jax.random.normal(ks[1], (DEC_BATCH, DEC_SEQ, D_MODEL), f32),
        'g_mix_norm': gain(ks[2], (DEPTH, D_MODEL)),
        'w_in': nrm(ks[3], (DEPTH, D_MODEL, N_IN), D_MODEL ** -0.5),
        'g_qa': gain(ks[4], (DEPTH, HD_A)),
        'g_ka': gain(ks[5], (DEPTH, HD_A)),
        'lam_q1': nrm(ks[6], (DEPTH, HD_A), 0.1),
        'lam_k1': nrm(ks[7], (DEPTH, HD_A), 0.1),
        'lam_q2': nrm(ks[8], (DEPTH, HD_A), 0.1),
        'lam_k2': nrm(ks[9], (DEPTH, HD_A), 0.1),
        'g_subln': gain(ks[10], (DEPTH, 2 * HD_A)),
        'g_qb': gain(ks[11], (DEPTH, HD_B)),
        'g_kb': gain(ks[12], (DEPTH, HD_B)),
        'w_pa': nrm(ks[13], (DEPTH, OA_W, D_MODEL), OA_W ** -0.5),
        'w_pb': nrm(ks[14], (DEPTH, OB_W, D_MODEL), OB_W ** -0.5),
        'w_o': nrm(ks[15], (DEPTH, D_MODEL, D_MODEL), D_MODEL ** -0.5),
        'g_ffn_norm': gain(ks[16], (DEPTH, D_MODEL)),
        'w_up': nrm(ks[17], (DEPTH, D_MODEL, 2 * D_FF), D_MODEL ** -0.5),
        'conv_w': nrm(ks[18], (DEPTH, CONV_W, 2 * D_FF), CONV_W ** -0.5),
        'conv_b': nrm(ks[19], (DEPTH, 2 * D_FF), 0.01),
        'w_down': nrm(ks[20], (DEPTH, D_FF, D_MODEL), D_FF ** -0.5),
    }


def reference(x_prompt, x_sample, g_mix_norm, w_in, g_qa, g_ka, lam_q1, lam_k1, lam_q2, lam_k2,
              g_subln, g_qb, g_kb, w_pa, w_pb, w_o, g_ffn_norm, w_up, conv_w, conv_b, w_down):
    y_prompt = x_prompt
    y_sample = x_sample
    for l in range(DEPTH):
        p = (g_mix_norm[l], w_in[l], g_qa[l], g_ka[l], lam_q1[l], lam_k1[l], lam_q2[l], lam_k2[l],
             g_subln[l], g_qb[l], g_kb[l], w_pa[l], w_pb[l], w_o[l], g_ffn_norm[l], w_up[l],
             conv_w[l], conv_b[l], w_down[l])
        y_prompt = encoder_layer(y_prompt, l, *p)
        y_sample = encoder_layer(y_sample, l, *p)
    return (y_prompt, y_sample)
```

```python
import numpy as np
from contextlib import ExitStack
import concourse.bass as bass
import concourse.mybir as mybir
from concourse.bass_utils import run_bass_kernel_spmd

F32 = mybir.dt.float32
BF16 = mybir.dt.bfloat16
AF = mybir.ActivationFunctionType
ALU = mybir.AluOpType
AX = mybir.AxisListType

D = 2048
SEQ = 16384
SSEQ = 2048
N_IN = 11776
DFF = 5632
NCTX = SEQ + SSEQ
OWN0 = 1536
OWN1 = OWN0 + 4096
BWIN = 7168
NB = BWIN + SSEQ
NQ = 4096 + 2 + SSEQ
QL, QR, QS0 = 4096, 4097, 4098
NCV = 4098 + 2050
CVS0 = 4098
SLOPE_A = [2.0 ** (-(h + 1)) for h in range(8)]
SLOPE_B = [2.0 ** (-8.0 * (i + 1) / 12.0) for i in range(12)]
DIL = [1, 4, 16]
BIGD = 1.0e7
DB_C0 = [64 * d + 512 for d in DIL]
DB_W = [DB_C0[g] + 64 * DIL[g] + 640 for g in range(3)]
DB_OFF = [0, DB_W[0], DB_W[0] + DB_W[1]]
DB_TOT = sum(DB_W)

PV_GQA = 0
PV_GKA = 1
PV_GQB = 2
PV_GKB = 3
PV_GSUB = 4
PV_CW = 5
PV_CB = PV_CW + 264
PV_LAM = PV_CB + 88
PV_HM = PV_LAM + 256
NPV = PV_HM + 2

ENGS = ("pe", "act", "dve", "pool", "sp")


class Buf:
    __slots__ = ("t", "w", "r", "ds", "name")

    def __init__(self, t, name=""):
        self.t = t
        self.w = None
        self.r = {}
        self.ds = None
        self.name = name


class Prog:
    def __init__(self, nc, es):
        self.nc = nc
        self.es = es
        self.E = {"pe": nc.tensor, "act": nc.scalar, "dve": nc.vector,
                  "pool": nc.gpsimd, "sp": nc.sync}
        self.sem = {}
        self.cnt = {}
        self.waited = {e: {} for e in ENGS}
        for e in ("pe", "act", "dve", "pool"):
            self._mk("E_" + e)
        self.free_ds = {"sw": [], "hw": []}
        self.nds = 0

    def _mk(self, name):
        self.sem[name] = self.es.enter_context(self.nc.semaphore(name))
        self.cnt[name] = 0

    def get_ds(self, kind):
        if self.free_ds[kind]:
            return self.free_ds[kind].pop()
        name = "D%s%d" % (kind, self.nds)
        self.nds += 1
        self._mk(name)
        return name

    def release(self, bufs):
        for b in bufs:
            if b.ds is not None:
                for kind, nm in b.ds.items():
                    self.free_ds[kind].append(nm)
                b.ds = None

    def wait(self, eng, tok):
        if tok is None:
            return
        s, v = tok
        if eng == "pe" and s == "E_pe":
            return
        if self.waited[eng].get(s, 0) >= v:
            return
        self.waited[eng][s] = v
        self.E[eng].wait_ge(self.sem[s], v)

    def _deps(self, eng, reads, writes):
        for b in reads:
            self.wait(eng, b.w)
        for b in writes:
            self.wait(eng, b.w)
            for t in b.r.values():
                self.wait(eng, t)

    def op(self, eng, reads, writes, fn):
        self._deps(eng, reads, writes)
        ins = fn(self.E[eng])
        s = "E_" + eng
        self.cnt[s] += 1
        ins.then_inc(self.sem[s], 1)
        tok = (s, self.cnt[s])
        for b in reads:
            b.r[eng] = tok
        for b in writes:
            b.w = tok
            b.r = {}
        return tok

    def dma(self, q, out_b, in_b, out_ap, in_ap, track=None):
        reads = [in_b] if in_b is not None else []
        writes = [out_b] if out_b is not None else []
        self._deps(q, reads, writes)
        tb = track if track is not None else (out_b if out_b is not None else in_b)
        if tb.ds is None:
            tb.ds = {}
        kind = "sw" if q == "pool" else "hw"
        if kind not in tb.ds:
            tb.ds[kind] = self.get_ds(kind)
        s = tb.ds[kind]
        self.cnt[s] += 16
        try:
            ins = self.E[q].dma_start(out=out_ap, in_=in_ap)
        except ValueError:
            ins = self.E[q].dma_start(out=out_ap, in_=in_ap, allow_slow_non_contiguous=True)
        ins.then_inc(self.sem[s], 16)
        tok = (s, self.cnt[s])
        if in_b is not None:
            in_b.r["dma_" + s] = tok
        if out_b is not None:
            out_b.w = tok
            out_b.r = {}
        return tok

    def barrier(self):
        for e in ENGS:
            for s, c in self.cnt.items():
                if c > 0:
                    self.wait(e, (s, c))


def _rot(lst):
    i = [0]

    def nxt():
        b = lst[i[0] % len(lst)]
        i[0] += 1
        return b
    return nxt


def build(dbg=False):
    nc = bass.Bass("TRN2", target_bir_lowering=False)
    es = ExitStack()
    P = Prog(nc, es)
    es.enter_context(nc.allow_low_precision(reason="bf16 matmul operands by design, fp32 accumulation"))

    def din(name, shape):
        return nc.dram_tensor(name, list(shape), F32, kind="ExternalInput").ap()

    def dscr(name, shape, dt):
        if dbg:
            return nc.dram_tensor(name, list(shape), dt, kind="ExternalOutput").ap()
        return nc.dram_tensor(name, list(shape), dt).ap()

    xp = din("xp", [SEQ, D])
    xs = din("xs", [SSEQ, D])
    w_in = din("w_in", [D, N_IN])
    w_pa = din("w_pa", [1024, D])
    w_pb = din("w_pb", [512, D])
    w_o = din("w_o", [D, D])
    w_up = din("w_up", [D, 2 * DFF])
    w_dn = din("w_dn", [DFF, D])
    pv = din("pv", [128, NPV])
    gmixb = din("gmixb", [128, D])
    gffnb = din("gffnb", [128, D])
    kaug = din("kaug", [8, 4, NCTX])
    qaugp = din("qaugp", [4, NQ])
    qaugm = din("qaugm", [4, NQ])
    absa = din("absa", [128, 896])
    dbt = din("dbt", [128, DB_TOT])
    valb = din("valb", [128, 72])
    ident = din("ident", [128, 128])
    yp = nc.dram_tensor("yp", [4096, D], F32, kind="ExternalOutput").ap()
    ys = nc.dram_tensor("ys", [SSEQ, D], F32, kind="ExternalOutput").ap()

    wb_in = nc.dram_tensor("wb_in", [128, 16, N_IN], BF16).ap()
    wb_pa = nc.dram_tensor("wb_pa", [128, 8, D], BF16).ap()
    wb_pb = nc.dram_tensor("wb_pb", [128, 4, D], BF16).ap()
    wb_o = nc.dram_tensor("wb_o", [128, 16, D], BF16).ap()
    wb_up = nc.dram_tensor("wb_up", [128, 16, 2 * DFF], BF16).ap()
    wb_dn = nc.dram_tensor("wb_dn", [128, 44, D], BF16).ap()
    kaug_b = nc.dram_tensor("kaug_b", [8, 4, NCTX], BF16).ap()
    qaugp_b = nc.dram_tensor("qaugp_b", [4, NQ], BF16).ap()
    qaugm_b = nc.dram_tensor("qaugm_b", [4, NQ], BF16).ap()
    QAd = dscr("QAd", [8, 128, NQ], BF16)
    KAd = dscr("KAd", [8, 128, NCTX], BF16)
    VAd = dscr("VAd", [8, NCTX, 128], BF16)
    QBd = dscr("QBd", [12, 128, NQ], BF16)
    KBd = dscr("KBd", [12, 128, NB], BF16)
    VBd = dscr("VBd", [12, NB, 128], BF16)
    GTd = dscr("GTd", [32, 128, NQ], BF16)
    OAd = dscr("OAd", [8, 128, NQ], BF16)
    OBd = dscr("OBd", [4, 128, NQ], BF16)
    X1d = dscr("X1d", [NCV, D], F32)
    H2d = dscr("H2d", [128, 16, NCV], BF16)

    def sb(name, shape, dt):
        return es.enter_context(nc.sbuf_tensor(name, list(shape), dt))

    PDt = [es.enter_context(nc.psum_tensor("pd%d" % i, [128, 1024], F32)) for i in range(4)]
    PSt = [PDt[i // 2][:, (i % 2) * 512:(i % 2 + 1) * 512] for i in range(8)]
    PS = [Buf(t, "ps%d" % i) for i, t in enumerate(PSt)]

    pvt = sb("pvt", [128, NPV], F32)
    cst = sb("cst", [128, 16], F32)
    idt = sb("idt", [128, 128], BF16)
    idf = sb("idf", [128, 128], F32)
    ones = sb("ones", [128, 128], BF16)
    blk = sb("blk", [128, 128], BF16)
    lamt = sb("lamt", [128, 64], F32)
    lams = sb("lams", [128, 4], F32)
    CB = Buf(None, "consts")

    WB = {}

    def cast(name, dst, src, ktn):
        b = Buf(None, name)
        for kt in range(ktn):
            P.dma("pool", b, None, dst[:, kt, :], src[kt * 128:(kt + 1) * 128, :])
        WB[name] = b

    TB = Buf(None, "tables")
    P.dma("pool", CB, None, idt[:], ident[:, :])
    P.dma("pool", TB, None, kaug_b[:, :, :], kaug[:, :, :])
    P.dma("pool", TB, None, qaugp_b[:, :], qaugp[:, :])
    P.dma("pool", TB, None, qaugm_b[:, :], qaugm[:, :])
    bkv = Buf(None, "in_kv")
    for kt in range(16):
        P.dma("pool", bkv, None, wb_in[:, kt, 1024:3072], w_in[kt * 128:(kt + 1) * 128, 1024:3072])
    WB["in_kv"] = bkv
    brest = Buf(None, "in")
    for kt in range(16):
        P.dma("pool", brest, None, wb_in[:, kt, 0:1024], w_in[kt * 128:(kt + 1) * 128, 0:1024])
        P.dma("pool", brest, None, wb_in[:, kt, 3072:N_IN], w_in[kt * 128:(kt + 1) * 128, 3072:N_IN])
    WB["in"] = brest
    cast("pa", wb_pa, w_pa, 8)
    cast("pb", wb_pb, w_pb, 4)
    cast("o", wb_o, w_o, 16)
    cast("up", wb_up, w_up, 16)
    cast("dn", wb_dn, w_dn, 44)

    P.dma("sp", CB, None, pvt[:], pv[:, :])
    P.dma("sp", CB, None, idf[:], ident[:, :])

    def c_op(eng, fn):
        P.op(eng, [CB], [CB], fn)

    c_op("dve", lambda e: e.memset(cst[:, 0:1], 1e-6))
    c_op("dve", lambda e: e.memset(cst[:, 1:2], 1e-5))
    c_op("dve", lambda e: e.memset(cst[:, 8:9], 0.0))
    c_op("dve", lambda e: e.memset(ones[:], 1.0))
    c_op("dve", lambda e: e.memset(blk[:], 0.0))
    c_op("dve", lambda e: e.memset(blk[0:64, 0:64], 1.0))
    c_op("dve", lambda e: e.memset(blk[64:128, 64:128], 1.0))
    c_op("dve", lambda e: e.tensor_scalar(cst[:, 3:4], pvt[:, PV_GQA:PV_GQA + 1], 0.125, None, ALU.mult))
    c_op("dve", lambda e: e.tensor_copy(cst[:, 4:5], pvt[:, PV_GKA:PV_GKA + 1]))
    c_op("dve", lambda e: e.tensor_scalar(cst[:, 5:6], pvt[:, PV_GQB:PV_GQB + 1], 128.0 ** -0.5, None, ALU.mult))
    c_op("dve", lambda e: e.tensor_copy(cst[:, 6:7], pvt[:, PV_GKB:PV_GKB + 1]))
    c_op("dve", lambda e: e.tensor_scalar(cst[:, 7:8], pvt[:, PV_GSUB:PV_GSUB + 1], 0.8, None, ALU.mult))
    for j in range(2):
        a0 = PV_LAM + 128 * j
        c_op("dve", lambda e, a0=a0: e.tensor_tensor(lamt[:], pvt[:, a0:a0 + 64], pvt[:, a0 + 64:a0 + 128], ALU.mult))
        c_op("dve", lambda e, j=j: e.tensor_reduce(lams[:, j:j + 1], lamt[:], AX.X, ALU.add))
        c_op("act", lambda e, j=j: e.activation(lams[:, 2 + j:3 + j], lams[:, j:j + 1], AF.Exp))
    c_op("dve", lambda e: e.tensor_tensor(lams[:, 0:1], lams[:, 3:4], lams[:, 2:3], ALU.subtract))
    c_op("dve", lambda e: e.tensor_scalar(cst[:, 2:3], lams[:, 0:1], -0.2, None, ALU.add))
    EPS6 = cst[:, 0:1]
    EPS5 = cst[:, 1:2]
    NEGLAM = cst[:, 2:3]

    def norm_tail(N, ps_z, ps_bs, sqb, lnb, onesmat, inv_n, eps_ap, g_ap, out_b, z_sbuf=None):
        zb = z_sbuf if z_sbuf is not None else ps_z
        P.op("act", [zb], [sqb], lambda e: e.activation(sqb.t[:, 0:N], zb.t[:, 0:N], AF.Square))
        P.op("pe", [sqb, CB], [ps_bs], lambda e: e.matmul(ps_bs.t[:, 0:N], onesmat, sqb.t[:, 0:N], start=True, stop=True))
        P.op("act", [ps_bs, CB], [lnb], lambda e: e.activation(lnb.t[:, 0:N], ps_bs.t[:, 0:N], AF.Ln, bias=eps_ap, scale=inv_n))
        P.op("act", [lnb], [lnb], lambda e: e.activation(lnb.t[:, 0:N], lnb.t[:, 0:N], AF.Exp, scale=-0.5))
        P.op("dve", [zb, lnb, CB], [out_b], lambda e: e.scalar_tensor_tensor(
            out_b.t[:, 0:N], zb.t[:, 0:N], g_ap, lnb.t[:, 0:N], ALU.mult, ALU.mult))

    with ExitStack() as ph:
        def psb(name, shape, dt):
            return ph.enter_context(nc.sbuf_tensor(name, list(shape), dt))
        xt = [Buf(psb("xt%d" % i, [128, D], F32)) for i in range(3)]
        junk = psb("junk", [128, D], BF16)
        ssb = [Buf(psb("ssb%d" % i, [128, 8], F32)) for i in range(2)]
        xsb = [Buf(psb("xsb%d" % i, [128, D], BF16)) for i in range(2)]
        hTt = [psb("hT%d" % i, [128, 16, 512], BF16) for i in range(2)]
        hT = [[Buf(t) for _ in range(2)] for t in hTt]
        gmt = psb("gmt", [128, D], F32)
        GM = Buf(None)
        P.dma("sp", GM, None, gmt[:], gmixb[:, :])
        wch = [Buf(psb("wch%d" % i, [128, 16, 512], BF16)) for i in range(4)]
        zst = [Buf(psb("zst%d" % i, [128, 512], BF16)) for i in range(4)]
        vst = [Buf(psb("vst%d" % i, [128, 512], BF16)) for i in range(3)]
        sqb = [Buf(psb("sqb%d" % i, [128, 512], BF16)) for i in range(2)]
        lnb = [Buf(psb("lnb%d" % i, [128, 512], F32)) for i in range(2)]
        gex = [Buf(psb("gex%d" % i, [128, 512], F32)) for i in range(2)]
        nx_xt, nx_ss, nx_xs, nx_hT = _rot(xt), _rot(ssb), _rot(xsb), _rot(hT)
        nx_zst, nx_vst, nx_sq, nx_ln, nx_gex = _rot(zst), _rot(vst), _rot(sqb), _rot(lnb), _rot(gex)
        ptb = [(PS[i], PSt[i].bitcast(BF16)) for i in range(2)]
        nx_pt = _rot(ptb)
        nx_z = _rot(PS[2:6])
        nx_bs = _rot(PS[6:8])
        evac_i = [0]
        wslot = _rot(wch)

        def load_chunk(ci, slot):
            P.dma("sp", slot, WB["in_kv" if ci in (2, 3, 4, 5) else "in"], slot.t[:], wb_in[:, :, ci * 512:(ci + 1) * 512])

        import os
        _lim = os.environ.get("KDBG", "")

        def proj_tile(xsrc, r0, ctx0, b0, q0, qc, chunks, resident):
            Hs = nx_hT()
            Ht = Hs[0].t
            ss = nx_ss()
            for b in range(4):
                X = nx_xt()
                P.dma("sp", X, None, X.t[:], xsrc[r0 + b * 128:r0 + (b + 1) * 128, :])
                P.op("act", [X], [ss], lambda e, X=X, b=b: e.activation(
                    junk[:], X.t[:], AF.Square, accum_out=ss.t[:, b:b + 1]))
                P.op("act", [ss, CB], [ss], lambda e, b=b: e.activation(ss.t[:, 4 + b:5 + b], ss.t[:, b:b + 1], AF.Ln, bias=EPS6, scale=1.0 / D))
                P.op("act", [ss], [ss], lambda e, b=b: e.activation(ss.t[:, 4 + b:5 + b], ss.t[:, 4 + b:5 + b], AF.Exp, scale=-0.5))
                S = nx_xs()
                P.op("dve", [X, ss, GM], [S], lambda e, X=X, S=S, b=b: e.scalar_tensor_tensor(
                    S.t[:], X.t[:], ss.t[:, 4 + b:5 + b], gmt[:], ALU.mult, ALU.mult))
                for kh in range(2):
                    pb, pvw = nx_pt()

                    def tr(e, S=S, pvw=pvw, kh=kh):
                        for k in range(8):
                            kt = kh * 8 + k
                            ins = e.transpose(pvw[:, k * 128:(k + 1) * 128], S.t[:, kt * 128:(kt + 1) * 128], idt[:])
                        return ins
                    P.op("pe", [S, CB], [pb], tr)
                    src = pvw[:, :].rearrange("p (k t) -> p k t", k=8)
                    dst = Ht[:, kh * 8:kh * 8 + 8, b * 128:(b + 1) * 128]
                    if kh % 2:
                        P.op("dve", [pb], [Hs[kh]], lambda e, dst=dst, src=src: e.tensor_copy(dst, src))
                    else:
                        P.op("act", [pb], [Hs[kh]], lambda e, dst=dst, src=src: e.activation(dst, src, AF.Copy))
            if _lim == "t":
                return
            loaded = {}

            def ensure(k):
                if k < len(chunks) and k not in loaded:
                    cj = chunks[k]
                    if cj in resident:
                        loaded[k] = resident[cj]
                    else:
                        sl = wslot()
                        load_chunk(cj, sl)
                        loaded[k] = sl
            for j, ci in enumerate(chunks):
                ensure(j)
                ensure(j + 1)
                ensure(j + 2)
                W = loaded.pop(j)
                if ci in (4, 5, 12, 13, 14):
                    for b in range(4):
                        Z = nx_z()

                        def mm(e, W=W, Z=Z, b=b):
                            for kt in range(16):
                                ins = e.matmul(Z.t[:, :], Ht[:, kt, b * 128:(b + 1) * 128], W.t[:, kt, :], start=(kt == 0), stop=(kt == 15))
                            return ins
                        P.op("pe", Hs + [W], [Z], mm)
                        V = nx_vst()
                        evac_i[0] += 1
                        if evac_i[0] % 2:
                            P.op("dve", [Z], [V], lambda e, V=V, Z=Z: e.tensor_copy(V.t[:], Z.t[:]))
                        else:
                            P.op("act", [Z], [V], lambda e, V=V, Z=Z: e.activation(V.t[:], Z.t[:], AF.Copy))
                        if ci in (4, 5):
                            h0 = 4 * (ci - 4)
                            dst = VAd[h0:h0 + 4, ctx0 + b * 128:ctx0 + (b + 1) * 128, :].rearrange("h t e -> t h e")
                        else:
                            h0 = 4 * (ci - 12)
                            dst = VBd[h0:h0 + 4, b0 + b * 128:b0 + (b + 1) * 128, :].rearrange("h t e -> t h e")
                        P.dma("sp", None, V, dst, V.t[:].rearrange("p (h e) -> p h e", h=4))
                    continue
                for f in range(4):
                    ft = ci * 4 + f
                    isq = ft < 8 or 24 <= ft < 36 or ft >= 60
                    c0, c1 = qc if isq else (0, 512)
                    N = c1 - c0
                    Z = nx_z()

                    def mm(e, W=W, Z=Z, f=f, c0=c0, c1=c1, N=N):
                        for kt in range(16):
                            ins = e.matmul(Z.t[:, 0:N], W.t[:, kt, f * 128:(f + 1) * 128], Ht[:, kt, c0:c1], start=(kt == 0), stop=(kt == 15))
                        return ins
                    P.op("pe", Hs + [W], [Z], mm)
                    O = nx_zst()
                    if ft >= 60:
                        G = nx_gex()
                        P.op("act", [Z], [G], lambda e, G=G, Z=Z, N=N: e.activation(G.t[:, 0:N], Z.t[:, 0:N], AF.Exp, scale=-1.0))
                        P.op("dve", [G], [G], lambda e, G=G, N=N: e.tensor_scalar(G.t[:, 0:N], G.t[:, 0:N], 1.0, None, ALU.add))
                        P.op("dve", [G], [O], lambda e, G=G, O=O, N=N: e.reciprocal(O.t[:, 0:N], G.t[:, 0:N]))
                        dst = GTd[ft - 60, :, q0:q0 + N]
                    else:
                        if ft < 8:
                            om, inv, g, dst = blk[:], 1.0 / 64, cst[:, 3:4], QAd[ft, :, q0:q0 + N]
                        elif ft < 16:
                            om, inv, g, dst = blk[:], 1.0 / 64, cst[:, 4:5], KAd[ft - 8, :, ctx0:ctx0 + 512]
                        elif ft < 36:
                            om, inv, g, dst = ones[:], 1.0 / 128, cst[:, 5:6], QBd[ft - 24, :, q0:q0 + N]
                        else:
                            om, inv, g, dst = ones[:], 1.0 / 128, cst[:, 6:7], KBd[ft - 36, :, b0:b0 + 512]
                        norm_tail(N, Z, nx_bs(), nx_sq(), nx_ln(), om, inv, EPS6, g, O)
                    P.dma("sp", None, O, dst, O.t[:, 0:N])

        KV = [2, 3, 4, 5]
        KVB = [2, 3, 4, 5, 9, 10, 11, 12, 13, 14]
        ALLC = list(range(23))
        res = {}
        for ci in KV:
            s = wslot()
            load_chunk(ci, s)
            res[ci] = s
        if _lim == "setup":
            P.barrier()
            return nc, es
        for T in range(14, 32):
            proj_tile(xp, T * 512, T * 512, None, None, None, KV, res)
            if _lim in ("1a", "t"):
                P.barrier()
                return nc, es
        for T in range(0, 14):
            if 3 <= T <= 10:
                proj_tile(xp, T * 512, T * 512, T * 512, (T - 3) * 512, (0, 512), ALLC, {})
            elif T == 2:
                proj_tile(xp, T * 512, T * 512, T * 512, QL, (511, 512), ALLC, {})
            elif T == 11:
                proj_tile(xp, T * 512, T * 512, T * 512, QR, (0, 1), ALLC, {})
            else:
                proj_tile(xp, T * 512, T * 512, T * 512, None, None, KVB, {})
        for T in range(4):
            proj_tile(xs, T * 512, SEQ + T * 512, BWIN + T * 512, QS0 + T * 512, (0, 512), ALLC, {})
        P.barrier()
        P.release(xt + wch + zst + vst + [GM])

    if dbg == 1:
        return nc, es

    import os
    _lim2 = os.environ.get("KDBG2", "")
    QT = [("p", t, t * 512, 512) for t in range(8)] + [("h", 0, QL, 2)] + [("s", t, QS0 + t * 512, 512) for t in range(4)]
    with ExitStack() as ph:
        def psb(name, shape, dt):
            return ph.enter_context(nc.sbuf_tensor(name, list(shape), dt))
        KT = [Buf(psb("KT%d" % m, [68, NCTX], BF16)) for m in range(2)]
        VT = Buf(psb("VT", [128, 144, 128], BF16))
        QSb = [Buf(psb("QS%d" % i, [68, 6, 512], BF16)) for i in range(2)]
        PT = [Buf(psb("PT%d" % i, [128, 2, 512], BF16)) for i in range(4)]
        SBf = [Buf(psb("SBf%d" % i, [128, 2, 512], F32)) for i in range(2)]
        absat = psb("absat", [128, 896], F32)
        lr = [Buf(psb("lr%d" % i, [128, 512], F32)) for i in range(2)]
        o12 = [Buf(psb("o12%d" % i, [128, 512], F32)) for i in range(2)]
        ob = Buf(psb("ob", [128, 512], F32))
        sq2 = Buf(psb("sq2", [128, 512], BF16))
        ln2 = Buf(psb("ln2", [128, 512], F32))
        oast = [Buf(psb("oast%d" % i, [128, 512], BF16)) for i in range(2)]
        nx_QS, nx_PT, nx_SB, nx_oast = _rot(QSb), _rot(PT), _rot(SBf), _rot(oast)
        nx_sc = _rot(PS[0:4])
        nx_scp = _rot([(PS[0], PS[1], PDt[0]), (PS[2], PS[3], PDt[1])])
        OB_, LB_ = PS[4:6], PS[6:8]
        AT = Buf(None)
        P.dma("sp", AT, None, absat[:], absa[:, :])
        for q in QSb:
            P.op("dve", [], [q], lambda e, q=q: e.memset(q.t[64:68, 4:6, :], 0.0))

        for h in range(8):
            if False:
                continue
            for m in range(2):
                P.dma("sp", KT[m], None, KT[m].t[0:64, :], KAd[h, m * 64:(m + 1) * 64, :])
                P.dma("sp", KT[m], TB, KT[m].t[64:68, :], kaug_b[h, :, :])
            for c4 in range(4):
                P.dma("sp", VT, None, VT.t[:, c4 * 36:(c4 + 1) * 36, :],
                      VAd[h, c4 * 4608:(c4 + 1) * 4608, :].rearrange("(kt p) e -> p kt e", p=128))
            for qi_, (kind, t, q0, N) in enumerate(QT):
                if _lim2 and qi_ not in (0, 1, 8, 12):
                    continue
                Q = nx_QS()
                qsrc = QAd[h, :, q0:q0 + N].rearrange("(m d) q -> d m q", m=2)
                for v in range(3):
                    P.dma("sp", Q, None, Q.t[0:64, 2 * v:2 * v + 2, 0:N], qsrc)
                for m in range(2):
                    P.dma("sp", Q, TB, Q.t[64:68, m, 0:N], qaugp_b[:, q0:q0 + N])
                    P.dma("sp", Q, TB, Q.t[64:68, 2 + m, 0:N], qaugm_b[:, q0:q0 + N])
                if kind == "s":
                    kts = list(range(128, 144))
                else:
                    kts = list(range(0, 128))
                tiles = []
                for kt in kts:
                    if kind == "p":
                        d0 = 12 + 4 * t
                        if kt < 12 or kt >= 44 or kt < d0:
                            var, dk = 0, None
                        elif kt < d0 + 4:
                            var, dk = 2, kt - d0
                        else:
                            var, dk = 1, None
                    elif kind == "h":
                        var, dk = (1 if 12 <= kt < 44 else 0), None
                    else:
                        d0 = 128 + 4 * t
                        if kt < d0:
                            var, dk = 0, None
                        elif kt < d0 + 4:
                            var, dk = 2, kt - d0
                        else:
                            var, dk = 1, None
                    tiles.append((kt, var, dk))
                nt = len(tiles)
                DEPTH = 2
                pts = [None] * nt
                for i in range(nt + DEPTH):
                    if i < nt:
                        kt, var, dk = tiles[i]
                        SCa, SCb, PDp = nx_scp()
                        SCm = (SCa, SCb)
                        for m in range(2):
                            P.op("pe", [KT[m], Q], [SCm[m]], lambda e, m=m, kt=kt, var=var, Q=Q, SCm=SCm: e.matmul(
                                SCm[m].t[:, 0:N], KT[m].t[0:68, kt * 128:(kt + 1) * 128], Q.t[0:68, 2 * var + m, 0:N], start=True, stop=True))
                        if dk is not None:
                            S2 = nx_SB()
                            off = 384 - 128 * dk
                            for m in range(2):
                                P.op("dve", [SCm[m], AT], [S2], lambda e, S2=S2, m=m, SCm=SCm, off=off: e.scalar_tensor_tensor(
                                    S2.t[:, m, 0:N], absat[:, off:off + N], -SLOPE_A[h], SCm[m].t[:, 0:N], ALU.mult, ALU.add))
                            srcb = [S2]
                            src_ap = S2.t[:, :, 0:N]
                        else:
                            srcb = [SCa, SCb]
                            src_ap = PDp[:, :].rearrange("p (m n) -> p m n", m=2)[:, :, 0:N]
                        Pt = nx_PT()
                        P.op("act", srcb, [Pt], lambda e, Pt=Pt, src_ap=src_ap: e.activation(Pt.t[:, :, 0:N], src_ap, AF.Exp))
                        pts[i] = Pt
                    j = i - DEPTH
                    if j >= 0:
                        kt, var, dk = tiles[j]
                        Pt = pts[j]
                        st = (j == 0)
                        sp_ = (j == nt - 1)
                        for m in range(2):
                            P.op("pe", [VT, Pt], [OB_[m]], lambda e, m=m, kt=kt, Pt=Pt, st=st, sp_=sp_: e.matmul(
                                OB_[m].t[:, 0:N], VT.t[:, kt, :], Pt.t[:, m, 0:N], start=st, stop=sp_))
                            P.op("pe", [CB, Pt], [LB_[m]], lambda e, m=m, Pt=Pt, st=st, sp_=sp_: e.matmul(
                                LB_[m].t[:, 0:N], ones[:], Pt.t[:, m, 0:N], start=st, stop=sp_))
                for m in range(2):
                    P.op("dve", [LB_[m]], [lr[m]], lambda e, m=m: e.reciprocal(lr[m].t[:, 0:N], LB_[m].t[:, 0:N]))
                    P.op("dve", [OB_[m], lr[m]], [o12[m]], lambda e, m=m: e.tensor_tensor(
                        o12[m].t[:, 0:N], OB_[m].t[:, 0:N], lr[m].t[:, 0:N], ALU.mult))
                P.op("dve", [o12[0], o12[1], CB], [ob], lambda e: e.scalar_tensor_tensor(
                    ob.t[:, 0:N], o12[1].t[:, 0:N], NEGLAM, o12[0].t[:, 0:N], ALU.mult, ALU.add))
                OA = nx_oast()
                norm_tail(N, None, nx_sc(), sq2, ln2, ones[:], 1.0 / 128, EPS5, cst[:, 7:8], OA, z_sbuf=ob)
                P.dma("sp", None, OA, OAd[h, :, q0:q0 + N], OA.t[:, 0:N])
        P.barrier()
        P.release(KT + [VT] + QSb + oast + [AT])

    if dbg == 2:
        return nc, es

    QTB = [(t * 512, 512, OWN0 + t * 512, 0, 56) for t in range(8)]
    QTB += [(QL, 1, OWN0 - 1, 0, 56), (QR, 1, OWN1, 0, 56)]
    QTB += [(QS0 + t * 512, 512, t * 512, 56, 72) for t in range(4)]
    with ExitStack() as ph:
        def psb(name, shape, dt):
            return ph.enter_context(nc.sbuf_tensor(name, list(shape), dt))
        KBT = Buf(psb("KBT", [128, 3, NB], BF16))
        VBT = Buf(psb("VBT", [128, 3, 72, 128], BF16))
        QBT = [Buf(psb("QBT%d" % i, [128, 3, 512], BF16)) for i in range(2)]
        dbs = psb("dbs", [128, DB_TOT], BF16)
        vbs = psb("vbs", [128, 72], F32)
        PT = [Buf(psb("PTb%d" % i, [128, 512], BF16)) for i in range(6)]
        SBf = [Buf(psb("SBb%d" % i, [128, 512], F32)) for i in range(4)]
        lrb = Buf(psb("lrb", [128, 512], F32))
        obst = [Buf(psb("obst%d" % i, [128, 512], BF16)) for i in range(2)]
        nx_QB, nx_PT, nx_SB, nx_obst = _rot(QBT), _rot(PT), _rot(SBf), _rot(obst)
        nx_sc = _rot(PS[0:4])
        OBk, LBk = PS[4], PS[6]
        DT = Buf(None)
        P.dma("pool", DT, None, dbs[:], dbt[:, :])
        P.dma("sp", DT, None, vbs[:], valb[:, :])
        for h in range(4):
            for g in range(3):
                P.dma("sp", KBT, None, KBT.t[:, g, :], KBd[g * 4 + h, :, :])
                for c2 in range(2):
                    P.dma("sp", VBT, None, VBT.t[:, g, c2 * 36:(c2 + 1) * 36, :],
                          VBd[g * 4 + h, c2 * 4608:(c2 + 1) * 4608, :].rearrange("(kt p) e -> p kt e", p=128))
            for (q0, N, qpos, klo, khi) in QTB:
                Q = nx_QB()
                for g in range(3):
                    P.dma("sp", Q, None, Q.t[:, g, 0:N], QBd[g * 4 + h, :, q0:q0 + N])
                tiles = []
                for g in range(3):
                    wl = 64 * DIL[g]
                    for kt in range(klo, khi):
                        kpos = (kt - klo) * 128
                        dbase = kpos - qpos
                        if dbase - (N - 1) <= wl and dbase + 127 >= -wl:
                            tiles.append((g, kt, DB_OFF[g] + DB_C0[g] - dbase))
                nt = len(tiles)
                DEPTH = 3
                pts = [None] * nt
                for i in range(nt + DEPTH):
                    if i < nt:
                        g, kt, off = tiles[i]
                        SC = nx_sc()
                        P.op("pe", [KBT, Q], [SC], lambda e, SC=SC, g=g, kt=kt, Q=Q: e.matmul(
                            SC.t[:, 0:N], KBT.t[:, g, kt * 128:(kt + 1) * 128], Q.t[:, g, 0:N], start=True, stop=True))
                        S2 = nx_SB()
                        sl = -SLOPE_B[g * 4 + h]
                        P.op("dve", [SC, DT], [S2], lambda e, S2=S2, SC=SC, off=off, sl=sl: e.scalar_tensor_tensor(
                            S2.t[:, 0:N], dbs[:, off:off + N], sl, SC.t[:, 0:N], ALU.mult, ALU.add))
                        Pt = nx_PT()
                        P.op("act", [S2, DT], [Pt], lambda e, Pt=Pt, S2=S2, kt=kt: e.activation(
                            Pt.t[:, 0:N], S2.t[:, 0:N], AF.Exp, bias=vbs[:, kt:kt + 1]))
                        pts[i] = Pt
                    j = i - DEPTH
                    if j >= 0:
                        g, kt, off = tiles[j]
                        Pt = pts[j]
                        P.op("pe", [VBT, Pt], [OBk], lambda e, g=g, kt=kt, Pt=Pt, j=j: e.matmul(
                            OBk.t[:, 0:N], VBT.t[:, g, kt, :], Pt.t[:, 0:N], start=(j == 0), stop=(j == nt - 1)))
                        P.op("pe", [CB, Pt], [LBk], lambda e, Pt=Pt, j=j: e.matmul(
                            LBk.t[:, 0:N], ones[:], Pt.t[:, 0:N], start=(j == 0), stop=(j == nt - 1)))
                P.op("dve", [LBk], [lrb], lambda e: e.reciprocal(lrb.t[:, 0:N], LBk.t[:, 0:N]))
                OO = nx_obst()
                P.op("dve", [OBk, lrb], [OO], lambda e, OO=OO: e.tensor_tensor(OO.t[:, 0:N], OBk.t[:, 0:N], lrb.t[:, 0:N], ALU.mult))
                P.dma("sp", None, OO, OBd[h, :, q0:q0 + N], OO.t[:, 0:N])
        P.barrier()
        P.release([KBT, VBT, DT] + QBT + obst)

    if dbg == 3:
        return nc, es

    with ExitStack() as ph:
        def psb(name, shape, dt):
            return ph.enter_context(nc.sbuf_tensor(name, list(shape), dt))
        wo = psb("wo", [128, 16, D], BF16)
        gft = psb("gft", [128, D], F32)
        W3 = Buf(None)
        P.dma("sp", W3, WB["o"], wo[:], wb_o[:, :, :])
        P.dma("sp", W3, None, gft[:], gffnb[:, :])
        wpc = [Buf(psb("wpc%d" % i, [128, 12, 512], BF16)) for i in range(2)]
        oin = [Buf(psb("oin%d" % i, [128, 12, 512], BF16)) for i in range(1)]
        gin = [Buf(psb("gin%d" % i, [128, 2, 512], BF16)) for i in range(2)]
        mT = Buf(psb("mT", [128, 16, 512], BF16))
        t1 = [Buf(psb("t1%d" % i, [128, 512], F32)) for i in range(2)]
        t2 = [Buf(psb("t2%d" % i, [128, 512], F32)) for i in range(2)]
        xb = [Buf(psb("xb%d" % i, [128, D], F32)) for i in range(1)]
        x1b = [Buf(psb("x1b%d" % i, [128, D], F32)) for i in range(1)]
        junk3 = psb("junk3", [128, D], BF16)
        s3 = [Buf(psb("s3%d" % i, [128, 4], F32)) for i in range(2)]
        h2s = [Buf(psb("h2s%d" % i, [128, D], BF16)) for i in range(1)]
        h2t = [Buf(psb("h2t%d" % i, [128, 16, 128], BF16)) for i in range(2)]
        zt = Buf(psb("zt", [128, 16, 2], BF16))
        nx_wpc, nx_oin, nx_gin, nx_t1, nx_t2 = _rot(wpc), _rot(oin), _rot(gin), _rot(t1), _rot(t2)
        nx_xb, nx_x1b, nx_s3, nx_h2s, nx_h2t = _rot(xb), _rot(x1b), _rot(s3), _rot(h2s), _rot(h2t)
        nx_pa, nx_pbk = _rot(PS[0:2]), _rot(PS[2:4])
        nx_xo = _rot(PS[4:6])
        ptb = [(PS[i], PSt[i].bitcast(BF16)) for i in (6, 7)]
        nx_pt = _rot(ptb)
        P.op("dve", [], [zt], lambda e: e.memset(zt.t[:], 0.0))
        P.dma("sp", None, zt, H2d[:, :, CVS0:CVS0 + 1], zt.t[:, :, 0:1])
        P.dma("sp", None, zt, H2d[:, :, NCV - 1:NCV], zt.t[:, :, 1:2])
        blk_i = [0]

        QT3 = [(t * 512, 512, "p", t) for t in range(8)] + [(QL, 2, "h", 0)] + [(QS0 + t * 512, 512, "s", t) for t in range(4)]
        for qi_, (q0, N, kind, t) in enumerate(QT3):
            if _lim2 and qi_ not in (0, 1, 8, 12):
                continue
            OI = nx_oin()
            P.dma("sp", OI, None, OI.t[:, 0:8, 0:N], OAd[:, :, q0:q0 + N].rearrange("h p q -> p h q"))
            P.dma("sp", OI, None, OI.t[:, 8:12, 0:N], OBd[:, :, q0:q0 + N].rearrange("h p q -> p h q"))
            for fo in range(16):
                if fo % 4 == 0:
                    WP = nx_wpc()
                    cg_ = fo // 4
                    P.dma("sp", WP, WB["pa"], WP.t[:, 0:8, :], wb_pa[:, :, cg_ * 512:(cg_ + 1) * 512])
                    P.dma("sp", WP, WB["pb"], WP.t[:, 8:12, :], wb_pb[:, :, cg_ * 512:(cg_ + 1) * 512])
                fl = fo % 4
                GI = nx_gin()
                P.dma("sp", GI, None, GI.t[:, 0, 0:N], GTd[fo, :, q0:q0 + N])
                P.dma("sp", GI, None, GI.t[:, 1, 0:N], GTd[16 + fo, :, q0:q0 + N])
                A = nx_pa()
                Bk = nx_pbk()

                def mma(e, A=A, fl=fl, OI=OI, WP=WP):
                    for kt in range(8):
                        ins = e.matmul(A.t[:, 0:N], WP.t[:, kt, fl * 128:(fl + 1) * 128], OI.t[:, kt, 0:N], start=(kt == 0), stop=(kt == 7))
                    return ins

                def mmb(e, Bk=Bk, fl=fl, OI=OI, WP=WP):
                    for kt in range(4):
                        ins = e.matmul(Bk.t[:, 0:N], WP.t[:, 8 + kt, fl * 128:(fl + 1) * 128], OI.t[:, 8 + kt, 0:N], start=(kt == 0), stop=(kt == 3))
                    return ins
                P.op("pe", [WP, OI], [A], mma)
                P.op("pe", [WP, OI], [Bk], mmb)
                T1, T2 = nx_t1(), nx_t2()
                P.op("dve", [A, GI], [T1], lambda e, T1=T1, A=A, GI=GI: e.tensor_tensor(T1.t[:, 0:N], A.t[:, 0:N], GI.t[:, 0, 0:N], ALU.mult))
                P.op("dve", [Bk, GI], [T2], lambda e, T2=T2, Bk=Bk, GI=GI: e.tensor_tensor(T2.t[:, 0:N], Bk.t[:, 0:N], GI.t[:, 1, 0:N], ALU.mult))
                P.op("pool", [T1, T2], [mT], lambda e, T1=T1, T2=T2, fo=fo: e.tensor_tensor(mT.t[:, fo, 0:N], T1.t[:, 0:N], T2.t[:, 0:N], ALU.add))
            nblk = (N + 127) // 128
            for b in range(nblk):
                nb_ = min(128, N - b * 128)
                XB = nx_xb()
                if kind == "p":
                    r0 = OWN0 + t * 512 + b * 128
                    P.dma("sp", XB, None, XB.t[0:nb_, :], xp[r0:r0 + nb_, :])
                    cvs = [(1 + t * 512 + b * 128, 0, nb_)]
                elif kind == "s":
                    r0 = t * 512 + b * 128
                    P.dma("sp", XB, None, XB.t[0:nb_, :], xs[r0:r0 + nb_, :])
                    cvs = [(CVS0 + 1 + t * 512 + b * 128, 0, nb_)]
                else:
                    P.dma("sp", XB, None, XB.t[0:1, :], xp[OWN0 - 1:OWN0, :])
                    P.dma("sp", XB, None, XB.t[1:2, :], xp[OWN1:OWN1 + 1, :])
                    cvs = [(0, 0, 1), (4097, 1, 1)]
                X1 = nx_x1b()
                for cc in range(4):
                    XO = nx_xo()

                    def mmo(e, XO=XO, cc=cc, b=b, nb_=nb_):
                        for kt in range(16):
                            ins = e.matmul(XO.t[0:nb_, :], mT.t[:, kt, b * 128:b * 128 + nb_], wo[:, kt, cc * 512:(cc + 1) * 512], start=(kt == 0), stop=(kt == 15))
                        return ins
                    P.op("pe", [mT, W3], [XO], mmo)
                    P.op("dve", [XO, XB], [X1], lambda e, X1=X1, XO=XO, XB=XB, cc=cc, nb_=nb_: e.tensor_tensor(
                        X1.t[0:nb_, cc * 512:(cc + 1) * 512], XO.t[0:nb_, :], XB.t[0:nb_, cc * 512:(cc + 1) * 512], ALU.add))
                S3 = nx_s3()
                P.op("act", [X1], [S3], lambda e, X1=X1, S3=S3, nb_=nb_: e.activation(junk3[0:nb_, :], X1.t[0:nb_, :], AF.Square, accum_out=S3.t[0:nb_, 0:1]))
                P.op("act", [S3, CB], [S3], lambda e, S3=S3, nb_=nb_: e.activation(S3.t[0:nb_, 1:2], S3.t[0:nb_, 0:1], AF.Ln, bias=cst[0:nb_, 0:1], scale=1.0 / D))
                P.op("act", [S3], [S3], lambda e, S3=S3, nb_=nb_: e.activation(S3.t[0:nb_, 1:2], S3.t[0:nb_, 1:2], AF.Exp, scale=-0.5))
                if kind != "h":
                    P.dma("sp", None, X1, X1d[cvs[0][0]:cvs[0][0] + nb_, :], X1.t[0:nb_, :])
                HS = nx_h2s()
                P.op("dve", [X1, S3, W3], [HS], lambda e, HS=HS, X1=X1, S3=S3, nb_=nb_: e.scalar_tensor_tensor(
                    HS.t[0:nb_, :], X1.t[0:nb_, :], S3.t[0:nb_, 1:2], gft[0:nb_, :], ALU.mult, ALU.mult))
                HT = nx_h2t()
                blk_i[0] += 1
                for kh in range(2):
                    pb, pvw = nx_pt()

                    def tr(e, HS=HS, pvw=pvw, kh=kh, nb_=nb_):
                        for k in range(8):
                            kt = kh * 8 + k
                            ins = e.transpose(pvw[:, k * 128:k * 128 + nb_], HS.t[0:nb_, kt * 128:(kt + 1) * 128], idt[0:nb_, 0:nb_])
                        return ins
                    P.op("pe", [HS, CB], [pb], tr)
                    src = pvw[:, :].rearrange("p (k t) -> p k t", k=8)[:, :, 0:nb_]
                    dst = HT.t[:, kh * 8:kh * 8 + 8, 0:nb_]
                    if kind == "h":
                        for k in range(8):
                            P.op("dve", [pb, CB], [HT], lambda e, k=k, kh=kh, pvw=pvw, HT=HT: e.tensor_tensor(
                                HT.t[:, kh * 8 + k, 0:2], pvw[:, k * 128:k * 128 + 2], pvt[:, PV_HM:PV_HM + 2], ALU.mult))
                    elif blk_i[0] % 2:
                        P.op("dve", [pb], [HT], lambda e, dst=dst, src=src: e.tensor_copy(dst, src))
                    else:
                        P.op("act", [pb], [HT], lambda e, dst=dst, src=src: e.activation(dst, src, AF.Copy))
                for (cv0, c0, n) in cvs:
                    P.dma("sp", None, HT, H2d[:, :, cv0:cv0 + n], HT.t[:, :, c0:c0 + n])
        P.barrier()
        P.release([W3, zt] + wpc + oin + gin + xb + x1b + h2t)

    if dbg == 4:
        return nc, es

    WIN = [(510 * k, min(512, 4098 - 510 * k), "p") for k in range(9)]
    WIN += [(CVS0 + 510 * k, min(512, 2050 - 510 * k), "s") for k in range(5)]
    with ExitStack() as ph:
        def psb(name, shape, dt):
            return ph.enter_context(nc.sbuf_tensor(name, list(shape), dt))
        hw = [Buf(psb("hw%d" % i, [128, 16, 512], BF16)) for i in range(2)]
        wu = [Buf(psb("wu%d" % i, [128, 2, 16, 128], BF16)) for i in range(3)]
        gT = Buf(psb("gT", [128, 44, 512], BF16))
        wd = [Buf(psb("wd%d" % i, [128, 44, 256], BF16)) for i in range(2)]
        cg = [Buf(psb("cg%d" % i, [128, 512], F32)) for i in range(2)]
        cv_ = [Buf(psb("cv%d" % i, [128, 512], F32)) for i in range(2)]
        ge = [Buf(psb("ge%d" % i, [128, 512], F32)) for i in range(2)]
        x1r = [Buf(psb("x1r%d" % i, [128, 256], F32)) for i in range(3)]
        yb = [Buf(psb("yb%d" % i, [128, 256], F32)) for i in range(3)]
        nx_hw, nx_wu, nx_wd, nx_cg, nx_cv, nx_ge = _rot(hw), _rot(wu), _rot(wd), _rot(cg), _rot(cv_), _rot(ge)
        nx_x1r, nx_yb = _rot(x1r), _rot(yb)
        nx_ug, nx_uv, nx_dn = _rot(PS[0:2]), _rot(PS[2:4]), _rot(PS[4:8])

        def cw(j, ft):
            c = PV_CW + j * 88 + ft
            return pvt[:, c:c + 1]

        def cb(ft):
            c = PV_CB + ft
            return pvt[:, c:c + 1]

        for wi_, (c0, n, kind) in enumerate(WIN):
            if _lim2 and wi_ not in (0, 1, 13):
                continue
            no = n - 2
            HW = nx_hw()
            P.dma("sp", HW, None, HW.t[:, :, 0:n], H2d[:, :, c0:c0 + n])
            pend = []

            def load_wu(j):
                s = nx_wu()
                P.dma("sp", s, WB["up"], s.t[:, 0, :, :], wb_up[:, :, j * 128:(j + 1) * 128])
                P.dma("sp", s, WB["up"], s.t[:, 1, :, :], wb_up[:, :, DFF + j * 128:DFF + (j + 1) * 128])
                return s
            for j in range(44):
                while len(pend) < 2 and j + len(pend) < 44:
                    pend.append(load_wu(j + len(pend)))
                WU = pend.pop(0)
                UG, UV = nx_ug(), nx_uv()
                for half, U in ((0, UG), (1, UV)):
                    def mmu(e, U=U, half=half, WU=WU):
                        for kt in range(16):
                            ins = e.matmul(U.t[:, 0:n], WU.t[:, half, kt, :], HW.t[:, kt, 0:n], start=(kt == 0), stop=(kt == 15))
                        return ins
                    P.op("pe", [WU, HW], [U], mmu)
                CG, CV = nx_cg(), nx_cv()
                for U, C, ft in ((UG, CG, j), (UV, CV, 44 + j)):
                    P.op("dve", [U, CB], [C], lambda e, U=U, C=C, ft=ft: e.tensor_scalar(
                        C.t[:, 0:no], U.t[:, 0:no], cw(0, ft), cb(ft), ALU.mult, ALU.add))
                    P.op("dve", [U, C, CB], [C], lambda e, U=U, C=C, ft=ft: e.scalar_tensor_tensor(
                        C.t[:, 0:no], U.t[:, 1:no + 1], cw(1, ft), C.t[:, 0:no], ALU.mult, ALU.add))
                    P.op("dve", [U, C, CB], [C], lambda e, U=U, C=C, ft=ft: e.scalar_tensor_tensor(
                        C.t[:, 0:no], U.t[:, 2:no + 2], cw(2, ft), C.t[:, 0:no], ALU.mult, ALU.add))
                GE = nx_ge()
                P.op("act", [CG], [GE], lambda e, GE=GE, CG=CG: e.activation(GE.t[:, 0:no], CG.t[:, 0:no], AF.Gelu))
                P.op("pool", [GE, CV], [gT], lambda e, GE=GE, CV=CV, j=j: e.tensor_tensor(gT.t[:, j, 0:no], GE.t[:, 0:no], CV.t[:, 0:no], ALU.mult))
            nblk = (no + 127) // 128
            for c8 in range(8):
                WD = nx_wd()
                P.dma("sp", WD, WB["dn"], WD.t[:], wb_dn[:, :, c8 * 256:(c8 + 1) * 256])
                for b in range(nblk):
                    nb_ = min(128, no - b * 128)
                    DN = nx_dn()

                    def mmd(e, DN=DN, WD=WD, b=b, nb_=nb_):
                        for j in range(44):
                            ins = e.matmul(DN.t[0:nb_, 0:256], gT.t[:, j, b * 128:b * 128 + nb_], WD.t[:, j, :], start=(j == 0), stop=(j == 43))
                        return ins
                    P.op("pe", [gT, WD], [DN], mmd)
                    cvr = c0 + 1 + b * 128
                    XR = nx_x1r()
                    P.dma("sp", XR, None, XR.t[0:nb_, 0:256], X1d[cvr:cvr + nb_, c8 * 256:(c8 + 1) * 256])
                    YB = nx_yb()
                    P.op("dve", [DN, XR], [YB], lambda e, YB=YB, DN=DN, XR=XR, nb_=nb_: e.tensor_tensor(
                        YB.t[0:nb_, 0:256], DN.t[0:nb_, 0:256], XR.t[0:nb_, 0:256], ALU.add))
                    if kind == "p":
                        dst = yp[cvr - 1:cvr - 1 + nb_, c8 * 256:(c8 + 1) * 256]
                    else:
                        dst = ys[cvr - CVS0 - 1:cvr - CVS0 - 1 + nb_, c8 * 256:(c8 + 1) * 256]
                    P.dma("sp", None, YB, dst, YB.t[0:nb_, 0:256])
        P.barrier()
    return nc, es


def _tables(c):
    r = c % 4
    own_lo, own_hi = 4096 * r, 4096 * (r + 1)
    u = np.arange(SEQ)
    pk = (own_lo - OWN0 + u) % SEQ
    is_own = (u >= OWN0) & (u < OWN1)
    sig = np.where(is_own, 1.0, np.where(pk < own_lo, 1.0, -1.0))
    pk_all = np.concatenate([pk, np.arange(SSEQ)]).astype(np.float64)
    sig_all = np.concatenate([sig, np.ones(SSEQ)])
    jlo = pk_all % 128
    jhi = pk_all - jlo
    kaug = np.zeros((8, 4, NCTX), np.float32)
    for h in range(8):
        s = SLOPE_A[h]
        kaug[h, 0] = -sig_all * s
        kaug[h, 1] = -sig_all * s
        kaug[h, 2] = sig_all * s * jlo
        kaug[h, 3] = sig_all * s * jhi
    qpos = np.zeros(NQ, np.float64)
    qpos[0:4096] = own_lo + np.arange(4096)
    qpos[QL] = own_lo - 1
    qpos[QR] = own_hi
    qpos[QS0:] = np.arange(SSEQ)
    ilo = qpos % 128
    ihi = qpos - ilo
    qaugp = np.stack([ilo, ihi, np.ones(NQ), np.ones(NQ)]).astype(np.float32)
    qaugm = -qaugp
    qaugm[:, QR] = qaugp[:, QR]
    valb = np.zeros((128, 72), np.float32)
    ub = np.arange(BWIN)
    pb = own_lo - OWN0 + ub
    bad = (pb < 0) | (pb >= SEQ)
    valb[:, :56] = np.where(bad, -30000.0, 0.0).reshape(56, 128).T
    hm = np.array([1.0 if r > 0 else 0.0, 1.0 if r < 3 else 0.0], np.float32)
    return kaug, qaugp, qaugm, valb, hm


def _shared_tables():
    k = np.arange(128)[:, None]
    absa = np.abs(k - np.arange(896)[None, :] + 384).astype(np.float32)
    dbt = np.zeros((128, DB_TOT), np.float32)
    for g in range(3):
        cc = np.arange(DB_W[g])[None, :]
        dl = k - cc + DB_C0[g]
        ok = (np.abs(dl) <= 64 * DIL[g]) & (dl % DIL[g] == 0)
        dbt[:, DB_OFF[g]:DB_OFF[g] + DB_W[g]] = np.where(ok, np.abs(dl), BIGD)
    return absa, dbt, np.eye(128, dtype=np.float32)


def make_in_maps(inp):
    f = lambda a: np.ascontiguousarray(a, dtype=np.float32)
    absa, dbt, ident = _shared_tables()
    w_in, w_pa, w_pb, w_o = f(inp["w_in"][0]), f(inp["w_pa"][0]), f(inp["w_pb"][0]), f(inp["w_o"][0])
    w_up, w_dn = f(inp["w_up"][0]), f(inp["w_down"][0])
    pvb = np.zeros((128, NPV), np.float32)
    gmixb = np.ascontiguousarray(np.broadcast_to(inp["g_mix_norm"][0][None, :], (128, D)), dtype=np.float32)
    gffnb = np.ascontiguousarray(np.broadcast_to(inp["g_ffn_norm"][0][None, :], (128, D)), dtype=np.float32)
    pvb[:, PV_GQA] = np.tile(inp["g_qa"][0], 2)
    pvb[:, PV_GKA] = np.tile(inp["g_ka"][0], 2)
    pvb[:, PV_GQB] = inp["g_qb"][0]
    pvb[:, PV_GKB] = inp["g_kb"][0]
    pvb[:, PV_GSUB] = inp["g_subln"][0]
    cwv = inp["conv_w"][0]
    for j in range(3):
        pvb[:, PV_CW + j * 88:PV_CW + (j + 1) * 88] = cwv[j].reshape(88, 128).T
    pvb[:, PV_CB:PV_CB + 88] = inp["conv_b"][0].reshape(88, 128).T
    for i, kname in enumerate(("lam_q1", "lam_k1", "lam_q2", "lam_k2")):
        pvb[:, PV_LAM + 64 * i:PV_LAM + 64 * (i + 1)] = inp[kname][0][None, :]
    maps = []
    for c in range(8):
        bp, r, bs = c // 4, c % 4, c // 2
        kaug, qaugp, qaugm, valb, hm = _tables(c)
        pvc = pvb.copy()
        pvc[:, PV_HM:PV_HM + 2] = hm[None, :]
        xpl = np.roll(inp["x_prompt"][bp], -(4096 * r - OWN0), axis=0)
        maps.append({
            "xp": f(xpl), "xs": f(inp["x_sample"][bs]),
            "w_in": w_in, "w_pa": w_pa, "w_pb": w_pb, "w_o": w_o, "w_up": w_up, "w_dn": w_dn,
            "pv": pvc, "gmixb": gmixb, "gffnb": gffnb, "kaug": kaug, "qaugp": qaugp, "qaugm": qaugm,
            "absa": absa, "dbt": dbt, "valb": valb, "ident": ident,
        })
    return maps


def kernel(**inp):
    nc, es = build()
    maps = make_in_maps(inp)
    res = run_bass_kernel_spmd(nc, maps, core_ids=list(range(8)))
    es.close()
    yp = np.zeros((2, SEQ, D), np.float32)
    ys = np.zeros((4, SSEQ, D), np.float32)
    for c in range(8):
        bp, r, bs, hs = c // 4, c % 4, c // 2, c % 2
        yp[bp, 4096 * r:4096 * (r + 1)] = res.results[c]["yp"]
        ys[bs, 1024 * hs:1024 * (hs + 1)] = res.results[c]["ys"][1024 * hs:1024 * (hs + 1)]
    return yp, ys
```

```python
import numpy as np
from contextlib import ExitStack
import concourse.bass as bass
import concourse.mybir as mybir
from concourse.bass_utils import run_bass_kernel_spmd

F32 = mybir.dt.float32
BF16 = mybir.dt.bfloat16
AF = mybir.ActivationFunctionType
ALU = mybir.AluOpType
AX = mybir.AxisListType

D = 2048
SEQ = 16384
SSEQ = 2048
N_IN = 11776
DFF = 5632
NCTX = SEQ + SSEQ
OWN0 = 1536
OWN1 = OWN0 + 4096
BWIN = 7168
NB = BWIN + SSEQ
NQ = 4096 + 2 + SSEQ
QL, QR, QS0 = 4096, 4097, 4098
NCV = 4098 + 2050
CVS0 = 4098
SLOPE_A = [2.0 ** (-(h + 1)) for h in range(8)]
SLOPE_B = [2.0 ** (-8.0 * (i + 1) / 12.0) for i in range(12)]
DIL = [1, 4, 16]
BIGD = 1.0e7
DB_C0 = [64 * d + 512 for d in DIL]
DB_W = [DB_C0[g] + 64 * DIL[g] + 640 for g in range(3)]
DB_OFF = [0, DB_W[0], DB_W[0] + DB_W[1]]
DB_TOT = sum(DB_W)

PV_GQA = 0
PV_GKA = 1
PV_GQB = 2
PV_GKB = 3
PV_GSUB = 4
PV_CW = 5
PV_CB = PV_CW + 264
PV_LAM = PV_CB + 88
PV_HM = PV_LAM + 256
NPV = PV_HM + 2

ENGS = ("pe", "act", "dve", "pool", "sp")


class Buf:
    __slots__ = ("t", "w", "r", "ds", "name")

    def __init__(self, t, name=""):
        self.t = t
        self.w = None
        self.r = {}
        self.ds = None
        self.name = name


class Prog:
    def __init__(self, nc, es):
        self.nc = nc
        self.es = es
        self.E = {"pe": nc.tensor, "act": nc.scalar, "dve": nc.vector,
                  "pool": nc.gpsimd, "sp": nc.sync}
        self.sem = {}
        self.cnt = {}
        self.waited = {e: {} for e in ENGS}
        for e in ("pe", "act", "dve", "pool"):
            self._mk("E_" + e)
        self.free_ds = {"sw": [], "hw": []}
        self.nds = 0

    def _mk(self, name):
        self.sem[name] = self.es.enter_context(self.nc.semaphore(name))
        self.cnt[name] = 0

    def get_ds(self, kind):
        if self.free_ds[kind]:
            return self.free_ds[kind].pop()
        name = "D%s%d" % (kind, self.nds)
        self.nds += 1
        self._mk(name)
        return name

    def release(self, bufs):
        for b in bufs:
            if b.ds is not None:
                for kind, nm in b.ds.items():
                    self.free_ds[kind].append(nm)
                b.ds = None

    def wait(self, eng, tok):
        if tok is None:
            return
        s, v = tok
        if eng == "pe" and s == "E_pe":
            return
        if self.waited[eng].get(s, 0) >= v:
            return
        self.waited[eng][s] = v
        self.E[eng].wait_ge(self.sem[s], v)

    def _deps(self, eng, reads, writes):
        for b in reads:
            self.wait(eng, b.w)
        for b in writes:
            self.wait(eng, b.w)
            for t in b.r.values():
                self.wait(eng, t)

    def op(self, eng, reads, writes, fn):
        self._deps(eng, reads, writes)
        ins = fn(self.E[eng])
        s = "E_" + eng
        self.cnt[s] += 1
        ins.then_inc(self.sem[s], 1)
        tok = (s, self.cnt[s])
        for b in reads:
            b.r[eng] = tok
        for b in writes:
            b.w = tok
            b.r = {}
        return tok

    def dma(self, q, out_b, in_b, out_ap, in_ap, track=None):
        reads = [in_b] if in_b is not None else []
        writes = [out_b] if out_b is not None else []
        self._deps(q, reads, writes)
        tb = track if track is not None else (out_b if out_b is not None else in_b)
        if tb.ds is None:
            tb.ds = {}
        kind = "sw" if q == "pool" else "hw"
        if kind not in tb.ds:
            tb.ds[kind] = self.get_ds(kind)
        s = tb.ds[kind]
        self.cnt[s] += 16
        try:
            ins = self.E[q].dma_start(out=out_ap, in_=in_ap)
        except ValueError:
            ins = self.E[q].dma_start(out=out_ap, in_=in_ap, allow_slow_non_contiguous=True)
        ins.then_inc(self.sem[s], 16)
        tok = (s, self.cnt[s])
        if in_b is not None:
            in_b.r["dma_" + s] = tok
        if out_b is not None:
            out_b.w = tok
            out_b.r = {}
        return tok

    def barrier(self):
        for e in ENGS:
            for s, c in self.cnt.items():
                if c > 0:
                    self.wait(e, (s, c))


def _rot(lst):
    i = [0]

    def nxt():
        b = lst[i[0] % len(lst)]
        i[0] += 1
        return b
    return nxt


def build(dbg=False):
    nc = bass.Bass("TRN2", target_bir_lowering=False)
    es = ExitStack()
    P = Prog(nc, es)
    es.enter_context(nc.allow_low_precision(reason="bf16 matmul operands by design, fp32 accumulation"))

    def din(name, shape):
        return nc.dram_tensor(name, list(shape), F32, kind="ExternalInput").ap()

    def dscr(name, shape, dt):
        if dbg:
            return nc.dram_tensor(name, list(shape), dt, kind="ExternalOutput").ap()
        return nc.dram_tensor(name, list(shape), dt).ap()

    xp = din("xp", [SEQ, D])
    xs = din("xs", [SSEQ, D])
    w_in = din("w_in", [D, N_IN])
    w_pa = din("w_pa", [1024, D])
    w_pb = din("w_pb", [512, D])
    w_o = din("w_o", [D, D])
    w_up = din("w_up", [D, 2 * DFF])
    w_dn = din("w_dn", [DFF, D])
    pv = din("pv", [128, NPV])
    gmixb = din("gmixb", [128, D])
    gffnb = din("gffnb", [128, D])
    kaug = din("kaug", [8, 4, NCTX])
    qaugp = din("qaugp", [4, NQ])
    qaugm = din("qaugm", [4, NQ])
    absa = din("absa", [128, 896])
    dbt = din("dbt", [128, DB_TOT])
    valb = din("valb", [128, 72])
    ident = din("ident", [128, 128])
    yp = nc.dram_tensor("yp", [4096, D], F32, kind="ExternalOutput").ap()
    ys = nc.dram_tensor("ys", [SSEQ, D], F32, kind="ExternalOutput").ap()

    wb_in = nc.dram_tensor("wb_in", [128, 16, N_IN], BF16).ap()
    wb_pa = nc.dram_tensor("wb_pa", [128, 8, D], BF16).ap()
    wb_pb = nc.dram_tensor("wb_pb", [128, 4, D], BF16).ap()
    wb_o = nc.dram_tensor("wb_o", [128, 16, D], BF16).ap()
    wb_up = nc.dram_tensor("wb_up", [128, 16, 2 * DFF], BF16).ap()
    wb_dn = nc.dram_tensor("wb_dn", [128, 44, D], BF16).ap()
    kaug_b = nc.dram_tensor("kaug_b", [8, 4, NCTX], BF16).ap()
    qaugp_b = nc.dram_tensor("qaugp_b", [4, NQ], BF16).ap()
    qaugm_b = nc.dram_tensor("qaugm_b", [4, NQ], BF16).ap()
    QAd = dscr("QAd", [8, 128, NQ], BF16)
    KAd = dscr("KAd", [8, 128, NCTX], BF16)
    VAd = dscr("VAd", [8, NCTX, 128], BF16)
    QBd = dscr("QBd", [12, 128, NQ], BF16)
    KBd = dscr("KBd", [12, 128, NB], BF16)
    VBd = dscr("VBd", [12, NB, 128], BF16)
    GTd = dscr("GTd", [32, 128, NQ], BF16)
    OAd = dscr("OAd", [8, 128, NQ], BF16)
    OBd = dscr("OBd", [4, 128, NQ], BF16)
    X1d = dscr("X1d", [NCV, D], F32)
    H2d = dscr("H2d", [128, 16, NCV], BF16)

    def sb(name, shape, dt):
        return es.enter_context(nc.sbuf_tensor(name, list(shape), dt))

    PDt = [es.enter_context(nc.psum_tensor("pd%d" % i, [128, 1024], F32)) for i in range(4)]
    PSt = [PDt[i // 2][:, (i % 2) * 512:(i % 2 + 1) * 512] for i in range(8)]
    PS = [Buf(t, "ps%d" % i) for i, t in enumerate(PSt)]

    pvt = sb("pvt", [128, NPV], F32)
    cst = sb("cst", [128, 16], F32)
    idt = sb("idt", [128, 128], BF16)
    idf = sb("idf", [128, 128], F32)
    ones = sb("ones", [128, 128], BF16)
    blk = sb("blk", [128, 128], BF16)
    lamt = sb("lamt", [128, 64], F32)
    lams = sb("lams", [128, 4], F32)
    CB = Buf(None, "consts")

    WB = {}

    def cast(name, dst, src, ktn):
        b = Buf(None, name)
        for kt in range(ktn):
            P.dma("pool", b, None, dst[:, kt, :], src[kt * 128:(kt + 1) * 128, :])
        WB[name] = b

    TB = Buf(None, "tables")
    P.dma("pool", CB, None, idt[:], ident[:, :])
    P.dma("pool", TB, None, kaug_b[:, :, :], kaug[:, :, :])
    P.dma("pool", TB, None, qaugp_b[:, :], qaugp[:, :])
    P.dma("pool", TB, None, qaugm_b[:, :], qaugm[:, :])
    bkv = Buf(None, "in_kv")
    for kt in range(16):
        P.dma("pool", bkv, None, wb_in[:, kt, 1024:3072], w_in[kt * 128:(kt + 1) * 128, 1024:3072])
    WB["in_kv"] = bkv
    brest = Buf(None, "in")
    for kt in range(16):
        P.dma("pool", brest, None, wb_in[:, kt, 0:1024], w_in[kt * 128:(kt + 1) * 128, 0:1024])
        P.dma("pool", brest, None, wb_in[:, kt, 3072:N_IN], w_in[kt * 128:(kt + 1) * 128, 3072:N_IN])
    WB["in"] = brest
    cast("pa", wb_pa, w_pa, 8)
    cast("pb", wb_pb, w_pb, 4)
    cast("o", wb_o, w_o, 16)
    cast("up", wb_up, w_up, 16)
    cast("dn", wb_dn, w_dn, 44)

    P.dma("sp", CB, None, pvt[:], pv[:, :])
    P.dma("sp", CB, None, idf[:], ident[:, :])

    def c_op(eng, fn):
        P.op(eng, [CB], [CB], fn)

    c_op("dve", lambda e: e.memset(cst[:, 0:1], 1e-6))
    c_op("dve", lambda e: e.memset(cst[:, 1:2], 1e-5))
    c_op("dve", lambda e: e.memset(cst[:, 8:9], 0.0))
    c_op("dve", lambda e: e.memset(ones[:], 1.0))
    c_op("dve", lambda e: e.memset(blk[:], 0.0))
    c_op("dve", lambda e: e.memset(blk[0:64, 0:64], 1.0))
    c_op("dve", lambda e: e.memset(blk[64:128, 64:128], 1.0))
    c_op("dve", lambda e: e.tensor_scalar(cst[:, 3:4], pvt[:, PV_GQA:PV_GQA + 1], 0.125, None, ALU.mult))
    c_op("dve", lambda e: e.tensor_copy(cst[:, 4:5], pvt[:, PV_GKA:PV_GKA + 1]))
    c_op("dve", lambda e: e.tensor_scalar(cst[:, 5:6], pvt[:, PV_GQB:PV_GQB + 1], 128.0 ** -0.5, None, ALU.mult))
    c_op("dve", lambda e: e.tensor_copy(cst[:, 6:7], pvt[:, PV_GKB:PV_GKB + 1]))
    c_op("dve", lambda e: e.tensor_scalar(cst[:, 7:8], pvt[:, PV_GSUB:PV_GSUB + 1], 0.8, None, ALU.mult))
    for j in range(2):
        a0 = PV_LAM + 128 * j
        c_op("dve", lambda e, a0=a0: e.tensor_tensor(lamt[:], pvt[:, a0:a0 + 64], pvt[:, a0 + 64:a0 + 128], ALU.mult))
        c_op("dve", lambda e, j=j: e.tensor_reduce(lams[:, j:j + 1], lamt[:], AX.X, ALU.add))
        c_op("act", lambda e, j=j: e.activation(lams[:, 2 + j:3 + j], lams[:, j:j + 1], AF.Exp))
    c_op("dve", lambda e: e.tensor_tensor(lams[:, 0:1], lams[:, 3:4], lams[:, 2:3], ALU.subtract))
    c_op("dve", lambda e: e.tensor_scalar(cst[:, 2:3], lams[:, 0:1], -0.2, None, ALU.add))
    EPS6 = cst[:, 0:1]
    EPS5 = cst[:, 1:2]
    NEGLAM = cst[:, 2:3]

    def norm_tail(N, ps_z, ps_bs, sqb, lnb, onesmat, inv_n, eps_ap, g_ap, out_b, z_sbuf=None):
        zb = z_sbuf if z_sbuf is not None else ps_z
        P.op("act", [zb], [sqb], lambda e: e.activation(sqb.t[:, 0:N], zb.t[:, 0:N], AF.Square))
        P.op("pe", [sqb, CB], [ps_bs], lambda e: e.matmul(ps_bs.t[:, 0:N], onesmat, sqb.t[:, 0:N], start=True, stop=True))
        P.op("act", [ps_bs, CB], [lnb], lambda e: e.activation(lnb.t[:, 0:N], ps_bs.t[:, 0:N], AF.Ln, bias=eps_ap, scale=inv_n))
        P.op("act", [lnb], [lnb], lambda e: e.activation(lnb.t[:, 0:N], lnb.t[:, 0:N], AF.Exp, scale=-0.5))
        P.op("dve", [zb, lnb, CB], [out_b], lambda e: e.scalar_tensor_tensor(
            out_b.t[:, 0:N], zb.t[:, 0:N], g_ap, lnb.t[:, 0:N], ALU.mult, ALU.mult))

    with ExitStack() as ph:
        def psb(name, shape, dt):
            return ph.enter_context(nc.sbuf_tensor(name, list(shape), dt))
        xt = [Buf(psb("xt%d" % i, [128, D], F32)) for i in range(3)]
        junk = psb("junk", [128, D], BF16)
        ssb = [Buf(psb("ssb%d" % i, [128, 8], F32)) for i in range(2)]
        xsb = [Buf(psb("xsb%d" % i, [128, D], BF16)) for i in range(2)]
        hTt = [psb("hT%d" % i, [128, 16, 512], BF16) for i in range(2)]
        hT = [[Buf(t) for _ in range(2)] for t in hTt]
        gmt = psb("gmt", [128, D], F32)
        GM = Buf(None)
        P.dma("sp", GM, None, gmt[:], gmixb[:, :])
        wch = [Buf(psb("wch%d" % i, [128, 16, 512], BF16)) for i in range(4)]
        zst = [Buf(psb("zst%d" % i, [128, 512], BF16)) for i in range(4)]
        vst = [Buf(psb("vst%d" % i, [128, 512], BF16)) for i in range(3)]
        sqb = [Buf(psb("sqb%d" % i, [128, 512], BF16)) for i in range(2)]
        lnb = [Buf(psb("lnb%d" % i, [128, 512], F32)) for i in range(2)]
        gex = [Buf(psb("gex%d" % i, [128, 512], F32)) for i in range(2)]
        nx_xt, nx_ss, nx_xs, nx_hT = _rot(xt), _rot(ssb), _rot(xsb), _rot(hT)
        nx_zst, nx_vst, nx_sq, nx_ln, nx_gex = _rot(zst), _rot(vst), _rot(sqb), _rot(lnb), _rot(gex)
        ptb = [(PS[i], PSt[i].bitcast(BF16)) for i in range(2)]
        nx_pt = _rot(ptb)
        nx_z = _rot(PS[2:6])
        nx_bs = _rot(PS[6:8])
        evac_i = [0]
        wslot = _rot(wch)

        def load_chunk(ci, slot):
            P.dma("sp", slot, WB["in_kv" if ci in (2, 3, 4, 5) else "in"], slot.t[:], wb_in[:, :, ci * 512:(ci + 1) * 512])

        import os
        _lim = os.environ.get("KDBG", "")

        def prologue(xsrc, r0):
            Hs = nx_hT()
            Ht = Hs[0].t
            ss = nx_ss()
            for b in range(4):
                X = nx_xt()
                P.dma("sp", X, None, X.t[:], xsrc[r0 + b * 128:r0 + (b + 1) * 128, :])
                P.op("act", [X], [ss], lambda e, X=X, b=b: e.activation(
                    junk[:], X.t[:], AF.Square, accum_out=ss.t[:, b:b + 1]))
                P.op("act", [ss, CB], [ss], lambda e, b=b: e.activation(ss.t[:, 4 + b:5 + b], ss.t[:, b:b + 1], AF.Ln, bias=EPS6, scale=1.0 / D))
                P.op("act", [ss], [ss], lambda e, b=b: e.activation(ss.t[:, 4 + b:5 + b], ss.t[:, 4 + b:5 + b], AF.Exp, scale=-0.5))
                S = nx_xs()
                P.op("dve", [X, ss, GM], [S], lambda e, X=X, S=S, b=b: e.scalar_tensor_tensor(
                    S.t[:], X.t[:], ss.t[:, 4 + b:5 + b], gmt[:], ALU.mult, ALU.mult))
                for kh in range(2):
                    pb, pvw = nx_pt()

                    def tr(e, S=S, pvw=pvw, kh=kh):
                        for k in range(8):
                            kt = kh * 8 + k
                            ins = e.transpose(pvw[:, k * 128:(k + 1) * 128], S.t[:, kt * 128:(kt + 1) * 128], idt[:])
                        return ins
                    P.op("pe", [S, CB], [pb], tr)
                    src = pvw[:, :].rearrange("p (k t) -> p k t", k=8)
                    dst = Ht[:, kh * 8:kh * 8 + 8, b * 128:(b + 1) * 128]
                    if kh % 2:
                        P.op("dve", [pb], [Hs[kh]], lambda e, dst=dst, src=src: e.tensor_copy(dst, src))
                    else:
                        P.op("act", [pb], [Hs[kh]], lambda e, dst=dst, src=src: e.activation(dst, src, AF.Copy))
            return Hs

        stream = []
        loaded = {}
        spos = [0]

        last_res = [-1]

        def ensure(k):
            if k < len(stream) and k not in loaded:
                cj, resident = stream[k]
                if cj in resident:
                    loaded[k] = resident[cj]
                else:
                    if spos[0] <= last_res[0] + 1:
                        return
                    sl = wslot()
                    load_chunk(cj, sl)
                    loaded[k] = sl

        def project(Hs, ctx0, b0, q0, qc, chunks, hoist=None, hoist_at=0):
            Ht = Hs[0].t
            for j, ci in enumerate(chunks):
                k = spos[0]
                spos[0] += 1
                ensure(k)
                ensure(k + 1)
                ensure(k + 2)
                W = loaded.pop(k)
                if hoist is not None and j == hoist_at:
                    hoist()
                if ci in (4, 5, 12, 13, 14):
                    for b in range(4):
                        Z = nx_z()

                        def mm(e, W=W, Z=Z, b=b):
                            for kt in range(16):
                                ins = e.matmul(Z.t[:, :], Ht[:, kt, b * 128:(b + 1) * 128], W.t[:, kt, :], start=(kt == 0), stop=(kt == 15))
                            return ins
                        P.op("pe", Hs + [W], [Z], mm)
                        V = nx_vst()
                        evac_i[0] += 1
                        if evac_i[0] % 2:
                            P.op("dve", [Z], [V], lambda e, V=V, Z=Z: e.tensor_copy(V.t[:], Z.t[:]))
                        else:
                            P.op("act", [Z], [V], lambda e, V=V, Z=Z: e.activation(V.t[:], Z.t[:], AF.Copy))
                        if ci in (4, 5):
                            h0 = 4 * (ci - 4)
                            dst = VAd[h0:h0 + 4, ctx0 + b * 128:ctx0 + (b + 1) * 128, :].rearrange("h t e -> t h e")
                        else:
                            h0 = 4 * (ci - 12)
                            dst = VBd[h0:h0 + 4, b0 + b * 128:b0 + (b + 1) * 128, :].rearrange("h t e -> t h e")
                        P.dma("sp", None, V, dst, V.t[:].rearrange("p (h e) -> p h e", h=4))
                    continue
                for f in range(4):
                    ft = ci * 4 + f
                    isq = ft < 8 or 24 <= ft < 36 or ft >= 60
                    c0, c1 = qc if isq else (0, 512)
                    N = c1 - c0
                    Z = nx_z()

                    def mm(e, W=W, Z=Z, f=f, c0=c0, c1=c1, N=N):
                        for kt in range(16):
                            ins = e.matmul(Z.t[:, 0:N], W.t[:, kt, f * 128:(f + 1) * 128], Ht[:, kt, c0:c1], start=(kt == 0), stop=(kt == 15))
                        return ins
                    P.op("pe", Hs + [W], [Z], mm)
                    O = nx_zst()
                    if ft >= 60:
                        G = nx_gex()
                        P.op("act", [Z], [G], lambda e, G=G, Z=Z, N=N: e.activation(G.t[:, 0:N], Z.t[:, 0:N], AF.Exp, scale=-1.0))
                        P.op("dve", [G], [G], lambda e, G=G, N=N: e.tensor_scalar(G.t[:, 0:N], G.t[:, 0:N], 1.0, None, ALU.add))
                        P.op("dve", [G], [O], lambda e, G=G, O=O, N=N: e.reciprocal(O.t[:, 0:N], G.t[:, 0:N]))
                        dst = GTd[ft - 60, :, q0:q0 + N]
                    else:
                        if ft < 8:
                            om, inv, g, dst = blk[:], 1.0 / 64, cst[:, 3:4], QAd[ft, :, q0:q0 + N]
                        elif ft < 16:
                            om, inv, g, dst = blk[:], 1.0 / 64, cst[:, 4:5], KAd[ft - 8, :, ctx0:ctx0 + 512]
                        elif ft < 36:
                            om, inv, g, dst = ones[:], 1.0 / 128, cst[:, 5:6], QBd[ft - 24, :, q0:q0 + N]
                        else:
                            om, inv, g, dst = ones[:], 1.0 / 128, cst[:, 6:7], KBd[ft - 36, :, b0:b0 + 512]
                        norm_tail(N, Z, nx_bs(), nx_sq(), nx_ln(), om, inv, EPS6, g, O)
                    P.dma("sp", None, O, dst, O.t[:, 0:N])

        KV = [2, 3, 4, 5]
        KVB = [2, 3, 4, 5, 9, 10, 11, 12, 13, 14]
        ALLC = list(range(23))
        res = {}
        for ci in KV:
            sl = wslot()
            load_chunk(ci, sl)
            res[ci] = sl
        TD = []
        for T in range(14, 32):
            TD.append((xp, T * 512, T * 512, None, None, None, KV, res))
        for T in range(0, 14):
            if 3 <= T <= 10:
                TD.append((xp, T * 512, T * 512, T * 512, (T - 3) * 512, (0, 512), ALLC, {}))
            elif T == 2:
                TD.append((xp, T * 512, T * 512, T * 512, QL, (511, 512), ALLC, {}))
            elif T == 11:
                TD.append((xp, T * 512, T * 512, T * 512, QR, (0, 1), ALLC, {}))
            else:
                TD.append((xp, T * 512, T * 512, T * 512, None, None, KVB, {}))
        for T in range(4):
            TD.append((xs, T * 512, SEQ + T * 512, BWIN + T * 512, QS0 + T * 512, (0, 512), ALLC, {}))
        for d in TD:
            for ci in d[6]:
                if ci in d[7]:
                    last_res[0] = len(stream)
                stream.append((ci, d[7]))
        nextH = [prologue(TD[0][0], TD[0][1])]
        for i, d in enumerate(TD):
            Hcur = nextH[0]

            def hoist(i=i):
                if i + 1 < len(TD):
                    nextH[0] = prologue(TD[i + 1][0], TD[i + 1][1])
            project(Hcur, d[2], d[3], d[4], d[5], d[6], hoist=hoist, hoist_at=min(1, len(d[6]) - 1))
        P.barrier()
        P.release(xt + wch + zst + vst + [GM])

    if dbg == 1:
        return nc, es

    import os
    _lim2 = os.environ.get("KDBG2", "")
    QT = [("p", t, t * 512, 512) for t in range(8)] + [("h", 0, QL, 2)] + [("s", t, QS0 + t * 512, 512) for t in range(4)]
    with ExitStack() as ph:
        def psb(name, shape, dt):
            return ph.enter_context(nc.sbuf_tensor(name, list(shape), dt))
        KT = [Buf(psb("KT%d" % m, [68, NCTX], BF16)) for m in range(2)]
        VT = Buf(psb("VT", [128, 144, 128], BF16))
        QSb = [Buf(psb("QS%d" % i, [68, 6, 512], BF16)) for i in range(2)]
        PT = [Buf(psb("PT%d" % i, [128, 2, 512], BF16)) for i in range(6)]
        PSm = [Buf(psb("PSm%d" % i, [128, 2, 512], BF16)) for i in range(3)]
        nx_PSm = _rot(PSm)
        SBf = [Buf(psb("SBf%d" % i, [128, 2, 512], F32)) for i in range(2)]
        absat = psb("absat", [128, 896], F32)
        lr = [Buf(psb("lr%d" % i, [128, 512], F32)) for i in range(2)]
        o12 = [Buf(psb("o12%d" % i, [128, 512], F32)) for i in range(2)]
        ob = Buf(psb("ob", [128, 512], F32))
        sq2 = Buf(psb("sq2", [128, 512], BF16))
        ln2 = Buf(psb("ln2", [128, 512], F32))
        oast = [Buf(psb("oast%d" % i, [128, 512], BF16)) for i in range(2)]
        nx_QS, nx_PT, nx_SB, nx_oast = _rot(QSb), _rot(PT), _rot(SBf), _rot(oast)
        nx_sc = _rot(PS[0:4])
        nx_scp = _rot([(PS[0], PS[1], PDt[0]), (PS[2], PS[3], PDt[1])])
        OB_, LB_ = PS[4:6], PS[6:8]
        AT = Buf(None)
        P.dma("sp", AT, None, absat[:], absa[:, :])
        for q in QSb:
            P.op("dve", [], [q], lambda e, q=q: e.memset(q.t[64:68, 4:6, :], 0.0))

        QTL = [q for i, q in enumerate(QT) if not (_lim2 and i not in (0, 1, 8, 12))]
        KTc = [[Buf(KT[m].t) for _ in range(4)] for m in range(2)]
        VTc = [Buf(VT.t) for _ in range(4)]

        def load_kv(hh, cs):
            for c4 in cs:
                k0, k1 = c4 * 4608, (c4 + 1) * 4608
                for m in range(2):
                    P.dma("sp", KTc[m][c4], None, KT[m].t[0:64, k0:k1], KAd[hh, m * 64:(m + 1) * 64, k0:k1])
                    P.dma("sp", KTc[m][c4], TB, KT[m].t[64:68, k0:k1], kaug_b[hh, :, k0:k1])
                P.dma("sp", VTc[c4], None, VT.t[:, c4 * 36:(c4 + 1) * 36, :],
                      VAd[hh, k0:k1, :].rearrange("(kt p) e -> p kt e", p=128))
        qitems = [(hh, qq) for hh in range(8) for qq in QTL]
        qld = {}

        def ens_q(k):
            if k < len(qitems) and k not in qld:
                hh, (kind_, t_, q0_, N_) = qitems[k]
                Qb = nx_QS()
                qsrc = QAd[hh, :, q0_:q0_ + N_].rearrange("(m d) q -> d m q", m=2)
                for v in range(3):
                    P.dma("sp", Qb, None, Qb.t[0:64, 2 * v:2 * v + 2, 0:N_], qsrc)
                for m in range(2):
                    P.dma("sp", Qb, TB, Qb.t[64:68, m, 0:N_], qaugp_b[:, q0_:q0_ + N_])
                    P.dma("sp", Qb, TB, Qb.t[64:68, 2 + m, 0:N_], qaugm_b[:, q0_:q0_ + N_])
                qld[k] = Qb
        load_kv(0, [0, 1, 2, 3])
        for h in range(8):
            for qi_, (kind, t, q0, N) in enumerate(QTL):
                kq = h * len(QTL) + qi_
                ens_q(kq)
                Q = qld.pop(kq)
                ens_q(kq + 1)
                if kind == "s" and t == (3 if _lim2 else 0) and h + 1 < 8:
                    load_kv(h + 1, [0, 1, 2])
                if kind == "s":
                    kts = list(range(128, 144))
                else:
                    kts = list(range(0, 128))
                tiles = []
                for kt in kts:
                    if kind == "p":
                        d0 = 12 + 4 * t
                        if kt < 12 or kt >= 44 or kt < d0:
                            var, dk = 0, None
                        elif kt < d0 + 4:
                            var, dk = 2, kt - d0
                        else:
                            var, dk = 1, None
                    elif kind == "h":
                        var, dk = (1 if 12 <= kt < 44 else 0), None
                    else:
                        d0 = 128 + 4 * t
                        if kt < d0:
                            var, dk = 0, None
                        elif kt < d0 + 4:
                            var, dk = 2, kt - d0
                        else:
                            var, dk = 1, None
                    tiles.append((kt, var, dk))
                nt = len(tiles)
                DEPTH = 2
                pts = [None] * nt
                sums = [None] * nt
                for i in range(nt + DEPTH):
                    if i < nt:
                        kt, var, dk = tiles[i]
                        SCa, SCb, PDp = nx_scp()
                        SCm = (SCa, SCb)
                        for m in range(2):
                            P.op("pe", [KTc[m][kt // 36], Q], [SCm[m]], lambda e, m=m, kt=kt, var=var, Q=Q, SCm=SCm: e.matmul(
                                SCm[m].t[:, 0:N], KT[m].t[0:68, kt * 128:(kt + 1) * 128], Q.t[0:68, 2 * var + m, 0:N], start=True, stop=True))
                        if dk is not None:
                            S2 = nx_SB()
                            off = 384 - 128 * dk
                            for m in range(2):
                                P.op("dve", [SCm[m], AT], [S2], lambda e, S2=S2, m=m, SCm=SCm, off=off: e.scalar_tensor_tensor(
                                    S2.t[:, m, 0:N], absat[:, off:off + N], -SLOPE_A[h], SCm[m].t[:, 0:N], ALU.mult, ALU.add))
                            srcb = [S2]
                            src_ap = S2.t[:, :, 0:N]
                        else:
                            srcb = [SCa, SCb]
                            src_ap = PDp[:, :].rearrange("p (m n) -> p m n", m=2)[:, :, 0:N]
                        Pt = nx_PT()
                        P.op("act", srcb, [Pt], lambda e, Pt=Pt, src_ap=src_ap: e.activation(Pt.t[:, :, 0:N], src_ap, AF.Exp))
                        pts[i] = Pt
                        if i % 2 == 1:
                            Sm = nx_PSm()
                            eng = "dve" if (i // 2) % 2 == 0 else "pool"
                            Pa = pts[i - 1]
                            P.op(eng, [Pa, Pt], [Sm], lambda e, Sm=Sm, Pa=Pa, Pt=Pt: e.tensor_tensor(
                                Sm.t[:, :, 0:N], Pa.t[:, :, 0:N], Pt.t[:, :, 0:N], ALU.add))
                            sums[i] = Sm
                    j = i - DEPTH
                    if j >= 0:
                        kt, var, dk = tiles[j]
                        Pt = pts[j]
                        st = (j == 0)
                        sp_ = (j == nt - 1)
                        for m in range(2):
                            P.op("pe", [VTc[kt // 36], Pt], [OB_[m]], lambda e, m=m, kt=kt, Pt=Pt, st=st, sp_=sp_: e.matmul(
                                OB_[m].t[:, 0:N], VT.t[:, kt, :], Pt.t[:, m, 0:N], start=st, stop=sp_))
                        if j % 2 == 1 or j == nt - 1:
                            Ls = sums[j] if j % 2 == 1 else Pt
                            lst = (j <= 1)
                            for m in range(2):
                                P.op("pe", [CB, Ls], [LB_[m]], lambda e, m=m, Ls=Ls, lst=lst, sp_=sp_: e.matmul(
                                    LB_[m].t[:, 0:N], ones[:], Ls.t[:, m, 0:N], start=lst, stop=sp_))
                for m in range(2):
                    P.op("dve", [LB_[m]], [lr[m]], lambda e, m=m: e.reciprocal(lr[m].t[:, 0:N], LB_[m].t[:, 0:N]))
                    P.op("dve", [OB_[m], lr[m]], [o12[m]], lambda e, m=m: e.tensor_tensor(
                        o12[m].t[:, 0:N], OB_[m].t[:, 0:N], lr[m].t[:, 0:N], ALU.mult))
                P.op("dve", [o12[0], o12[1], CB], [ob], lambda e: e.scalar_tensor_tensor(
                    ob.t[:, 0:N], o12[1].t[:, 0:N], NEGLAM, o12[0].t[:, 0:N], ALU.mult, ALU.add))
                OA = nx_oast()
                norm_tail(N, None, nx_sc(), sq2, ln2, ones[:], 1.0 / 128, EPS5, cst[:, 7:8], OA, z_sbuf=ob)
                P.dma("sp", None, OA, OAd[h, :, q0:q0 + N], OA.t[:, 0:N])
            if h + 1 < 8:
                load_kv(h + 1, [3])
        P.barrier()
        P.release(KTc[0] + KTc[1] + VTc + QSb + oast + [AT])

    if dbg == 2:
        return nc, es

    QTB = [(t * 512, 512, OWN0 + t * 512, 0, 56) for t in range(8)]
    QTB += [(QL, 1, OWN0 - 1, 0, 56), (QR, 1, OWN1, 0, 56)]
    QTB += [(QS0 + t * 512, 512, t * 512, 56, 72) for t in range(4)]
    with ExitStack() as ph:
        def psb(name, shape, dt):
            return ph.enter_context(nc.sbuf_tensor(name, list(shape), dt))
        KBT = Buf(psb("KBT", [128, 3, NB], BF16))
        VBT = Buf(psb("VBT", [128, 3, 72, 128], BF16))
        QBT = [Buf(psb("QBT%d" % i, [128, 3, 512], BF16)) for i in range(2)]
        dbs = psb("dbs", [128, DB_TOT], BF16)
        vbs = psb("vbs", [128, 72], F32)
        PT = [Buf(psb("PTb%d" % i, [128, 512], BF16)) for i in range(6)]
        SBf = [Buf(psb("SBb%d" % i, [128, 512], F32)) for i in range(4)]
        lrb = Buf(psb("lrb", [128, 512], F32))
        obst = [Buf(psb("obst%d" % i, [128, 512], BF16)) for i in range(2)]
        nx_QB, nx_PT, nx_SB, nx_obst = _rot(QBT), _rot(PT), _rot(SBf), _rot(obst)
        nx_sc = _rot(PS[0:4])
        OBk, LBk = PS[4], PS[6]
        DT = Buf(None)
        P.dma("pool", DT, None, dbs[:], dbt[:, :])
        P.dma("sp", DT, None, vbs[:], valb[:, :])
        KBg = [Buf(KBT.t) for _ in range(3)]
        VBg = [Buf(VBT.t) for _ in range(3)]
        qbitems = [(hh, qq) for hh in range(4) for qq in QTB]
        qbl = {}

        def ens_qb(k):
            if k < len(qbitems) and k not in qbl:
                hh, (q0_, N_, _a, _b, _c) = qbitems[k]
                Qb = nx_QB()
                for g in range(3):
                    P.dma("sp", Qb, None, Qb.t[:, g, 0:N_], QBd[g * 4 + hh, :, q0_:q0_ + N_])
                qbl[k] = Qb
        for h in range(4):
            for g in range(3):
                P.dma("sp", KBg[g], None, KBT.t[:, g, :], KBd[g * 4 + h, :, :])
                for c2 in range(2):
                    P.dma("sp", VBg[g], None, VBT.t[:, g, c2 * 36:(c2 + 1) * 36, :],
                          VBd[g * 4 + h, c2 * 4608:(c2 + 1) * 4608, :].rearrange("(kt p) e -> p kt e", p=128))
            for qi_, (q0, N, qpos, klo, khi) in enumerate(QTB):
                kq = h * len(QTB) + qi_
                ens_qb(kq)
                Q = qbl.pop(kq)
                ens_qb(kq + 1)
                tiles = []
                for g in range(3):
                    wl = 64 * DIL[g]
                    for kt in range(klo, khi):
                        kpos = (kt - klo) * 128
                        dbase = kpos - qpos
                        if dbase - (N - 1) <= wl and dbase + 127 >= -wl:
                            tiles.append((g, kt, DB_OFF[g] + DB_C0[g] - dbase))
                nt = len(tiles)
                DEPTH = 3
                pts = [None] * nt
                for i in range(nt + DEPTH):
                    if i < nt:
                        g, kt, off = tiles[i]
                        SC = nx_sc()
                        P.op("pe", [KBg[g], Q], [SC], lambda e, SC=SC, g=g, kt=kt, Q=Q: e.matmul(
                            SC.t[:, 0:N], KBT.t[:, g, kt * 128:(kt + 1) * 128], Q.t[:, g, 0:N], start=True, stop=True))
                        S2 = nx_SB()
                        sl = -SLOPE_B[g * 4 + h]
                        P.op("dve", [SC, DT], [S2], lambda e, S2=S2, SC=SC, off=off, sl=sl: e.scalar_tensor_tensor(
                            S2.t[:, 0:N], dbs[:, off:off + N], sl, SC.t[:, 0:N], ALU.mult, ALU.add))
                        Pt = nx_PT()
                        P.op("act", [S2, DT], [Pt], lambda e, Pt=Pt, S2=S2, kt=kt: e.activation(
                            Pt.t[:, 0:N], S2.t[:, 0:N], AF.Exp, bias=vbs[:, kt:kt + 1]))
                        pts[i] = Pt
                    j = i - DEPTH
                    if j >= 0:
                        g, kt, off = tiles[j]
                        Pt = pts[j]
                        P.op("pe", [VBg[g], Pt], [OBk], lambda e, g=g, kt=kt, Pt=Pt, j=j: e.matmul(
                            OBk.t[:, 0:N], VBT.t[:, g, kt, :], Pt.t[:, 0:N], start=(j == 0), stop=(j == nt - 1)))
                        P.op("pe", [CB, Pt], [LBk], lambda e, Pt=Pt, j=j: e.matmul(
                            LBk.t[:, 0:N], ones[:], Pt.t[:, 0:N], start=(j == 0), stop=(j == nt - 1)))
                P.op("dve", [LBk], [lrb], lambda e: e.reciprocal(lrb.t[:, 0:N], LBk.t[:, 0:N]))
                OO = nx_obst()
                P.op("dve", [OBk, lrb], [OO], lambda e, OO=OO: e.tensor_tensor(OO.t[:, 0:N], OBk.t[:, 0:N], lrb.t[:, 0:N], ALU.mult))
                P.dma("sp", None, OO, OBd[h, :, q0:q0 + N], OO.t[:, 0:N])
        P.barrier()
        P.release(KBg + VBg + [DT] + QBT + obst)

    if dbg == 3:
        return nc, es

    with ExitStack() as ph:
        def psb(name, shape, dt):
            return ph.enter_context(nc.sbuf_tensor(name, list(shape), dt))
        wo = psb("wo", [128, 16, D], BF16)
        gft = psb("gft", [128, D], F32)
        W3 = Buf(None)
        P.dma("sp", W3, WB["o"], wo[:], wb_o[:, :, :])
        P.dma("sp", W3, None, gft[:], gffnb[:, :])
        wpc = [Buf(psb("wpc%d" % i, [128, 12, 256], BF16)) for i in range(2)]
        oin = [Buf(psb("oin%d" % i, [128, 12, 512], BF16)) for i in range(2)]
        gin = [Buf(psb("gin%d" % i, [128, 2, 512], BF16)) for i in range(2)]
        mT = Buf(psb("mT", [128, 16, 512], BF16))
        t1 = [Buf(psb("t1%d" % i, [128, 512], F32)) for i in range(2)]
        t2 = [Buf(psb("t2%d" % i, [128, 512], F32)) for i in range(2)]
        xb = [Buf(psb("xb%d" % i, [128, D], F32)) for i in range(2)]
        x1b = [Buf(psb("x1b%d" % i, [128, D], F32)) for i in range(2)]
        s3 = [Buf(psb("s3%d" % i, [128, 4], F32)) for i in range(2)]
        h2s = [Buf(psb("h2s%d" % i, [128, D], BF16)) for i in range(2)]
        h2t = [Buf(psb("h2t%d" % i, [128, 16, 128], BF16)) for i in range(2)]
        zt = Buf(psb("zt", [128, 16, 2], BF16))
        nx_wpc, nx_oin, nx_gin, nx_t1, nx_t2 = _rot(wpc), _rot(oin), _rot(gin), _rot(t1), _rot(t2)
        nx_xb, nx_x1b, nx_s3, nx_h2s, nx_h2t = _rot(xb), _rot(x1b), _rot(s3), _rot(h2s), _rot(h2t)
        nx_pa, nx_pbk = _rot(PS[0:2]), _rot(PS[2:4])
        nx_xo = _rot(PS[4:6])
        ptb = [(PS[i], PSt[i].bitcast(BF16)) for i in (6, 7)]
        nx_pt = _rot(ptb)
        P.op("dve", [], [zt], lambda e: e.memset(zt.t[:], 0.0))
        P.dma("sp", None, zt, H2d[:, :, CVS0:CVS0 + 1], zt.t[:, :, 0:1])
        P.dma("sp", None, zt, H2d[:, :, NCV - 1:NCV], zt.t[:, :, 1:2])
        blk_i = [0]

        QT3 = [(t * 512, 512, "p", t) for t in range(8)] + [(QL, 2, "h", 0)] + [(QS0 + t * 512, 512, "s", t) for t in range(4)]
        if _lim2:
            QT3 = [QT3[i] for i in (0, 1, 8, 12)]
        oil = {}

        def ens_oi(k):
            if k < len(QT3) and k not in oil:
                q0_, N_, _k, _t = QT3[k]
                OIb = nx_oin()
                P.dma("sp", OIb, None, OIb.t[:, 0:8, 0:N_], OAd[:, :, q0_:q0_ + N_].rearrange("h p q -> p h q"))
                P.dma("sp", OIb, None, OIb.t[:, 8:12, 0:N_], OBd[:, :, q0_:q0_ + N_].rearrange("h p q -> p h q"))
                oil[k] = OIb

        for qi_, (q0, N, kind, t) in enumerate(QT3):
            ens_oi(qi_)
            OI = oil.pop(qi_)
            ens_oi(qi_ + 1)
            for fo in range(16):
                if fo % 2 == 0:
                    WP = nx_wpc()
                    cg_ = fo // 2
                    P.dma("sp", WP, WB["pa"], WP.t[:, 0:8, :], wb_pa[:, :, cg_ * 256:(cg_ + 1) * 256])
                    P.dma("sp", WP, WB["pb"], WP.t[:, 8:12, :], wb_pb[:, :, cg_ * 256:(cg_ + 1) * 256])
                fl = fo % 2
                GI = nx_gin()
                P.dma("sp", GI, None, GI.t[:, 0, 0:N], GTd[fo, :, q0:q0 + N])
                P.dma("sp", GI, None, GI.t[:, 1, 0:N], GTd[16 + fo, :, q0:q0 + N])
                A = nx_pa()
                Bk = nx_pbk()

                def mma(e, A=A, fl=fl, OI=OI, WP=WP):
                    for kt in range(8):
                        ins = e.matmul(A.t[:, 0:N], WP.t[:, kt, fl * 128:(fl + 1) * 128], OI.t[:, kt, 0:N], start=(kt == 0), stop=(kt == 7))
                    return ins

                def mmb(e, Bk=Bk, fl=fl, OI=OI, WP=WP):
                    for kt in range(4):
                        ins = e.matmul(Bk.t[:, 0:N], WP.t[:, 8 + kt, fl * 128:(fl + 1) * 128], OI.t[:, 8 + kt, 0:N], start=(kt == 0), stop=(kt == 3))
                    return ins
                P.op("pe", [WP, OI], [A], mma)
                P.op("pe", [WP, OI], [Bk], mmb)
                T1, T2 = nx_t1(), nx_t2()
                P.op("dve", [A, GI], [T1], lambda e, T1=T1, A=A, GI=GI: e.tensor_tensor(T1.t[:, 0:N], A.t[:, 0:N], GI.t[:, 0, 0:N], ALU.mult))
                P.op("dve", [Bk, GI], [T2], lambda e, T2=T2, Bk=Bk, GI=GI: e.tensor_tensor(T2.t[:, 0:N], Bk.t[:, 0:N], GI.t[:, 1, 0:N], ALU.mult))
                P.op("pool", [T1, T2], [mT], lambda e, T1=T1, T2=T2, fo=fo: e.tensor_tensor(mT.t[:, fo, 0:N], T1.t[:, 0:N], T2.t[:, 0:N], ALU.add))
            nblk = (N + 127) // 128

            def x1_part(b):
                nb_ = min(128, N - b * 128)
                XB = nx_xb()
                if kind == "p":
                    r0 = OWN0 + t * 512 + b * 128
                    P.dma("sp", XB, None, XB.t[0:nb_, :], xp[r0:r0 + nb_, :])
                    cvs = [(1 + t * 512 + b * 128, 0, nb_)]
                elif kind == "s":
                    r0 = t * 512 + b * 128
                    P.dma("sp", XB, None, XB.t[0:nb_, :], xs[r0:r0 + nb_, :])
                    cvs = [(CVS0 + 1 + t * 512 + b * 128, 0, nb_)]
                else:
                    P.dma("sp", XB, None, XB.t[0:1, :], xp[OWN0 - 1:OWN0, :])
                    P.dma("sp", XB, None, XB.t[1:2, :], xp[OWN1:OWN1 + 1, :])
                    cvs = [(0, 0, 1), (4097, 1, 1)]
                X1 = nx_x1b()
                for cc in range(4):
                    XO = nx_xo()

                    def mmo(e, XO=XO, cc=cc, b=b, nb_=nb_):
                        for kt in range(16):
                            ins = e.matmul(XO.t[0:nb_, :], mT.t[:, kt, b * 128:b * 128 + nb_], wo[:, kt, cc * 512:(cc + 1) * 512], start=(kt == 0), stop=(kt == 15))
                        return ins
                    P.op("pe", [mT, W3], [XO], mmo)
                    P.op("dve", [XO, XB], [X1], lambda e, X1=X1, XO=XO, XB=XB, cc=cc, nb_=nb_: e.tensor_tensor(
                        X1.t[0:nb_, cc * 512:(cc + 1) * 512], XO.t[0:nb_, :], XB.t[0:nb_, cc * 512:(cc + 1) * 512], ALU.add))
                S3 = nx_s3()
                HS = nx_h2s()
                P.op("act", [X1], [S3, HS], lambda e, X1=X1, S3=S3, HS=HS, nb_=nb_: e.activation(HS.t[0:nb_, :], X1.t[0:nb_, :], AF.Square, accum_out=S3.t[0:nb_, 0:1]))
                P.op("act", [S3, CB], [S3], lambda e, S3=S3, nb_=nb_: e.activation(S3.t[0:nb_, 1:2], S3.t[0:nb_, 0:1], AF.Ln, bias=cst[0:nb_, 0:1], scale=1.0 / D))
                P.op("act", [S3], [S3], lambda e, S3=S3, nb_=nb_: e.activation(S3.t[0:nb_, 1:2], S3.t[0:nb_, 1:2], AF.Exp, scale=-0.5))
                if kind != "h":
                    P.dma("sp", None, X1, X1d[cvs[0][0]:cvs[0][0] + nb_, :], X1.t[0:nb_, :])
                P.op("dve", [X1, S3, W3], [HS], lambda e, HS=HS, X1=X1, S3=S3, nb_=nb_: e.scalar_tensor_tensor(
                    HS.t[0:nb_, :], X1.t[0:nb_, :], S3.t[0:nb_, 1:2], gft[0:nb_, :], ALU.mult, ALU.mult))
                return (HS, nb_, cvs)

            def tr_part(st_):
                HS, nb_, cvs = st_
                HT = nx_h2t()
                blk_i[0] += 1
                for kh in range(2):
                    pb, pvw = nx_pt()

                    def tr(e, HS=HS, pvw=pvw, kh=kh, nb_=nb_):
                        for k in range(8):
                            kt = kh * 8 + k
                            ins = e.transpose(pvw[:, k * 128:k * 128 + nb_], HS.t[0:nb_, kt * 128:(kt + 1) * 128], idt[0:nb_, 0:nb_])
                        return ins
                    P.op("pe", [HS, CB], [pb], tr)
                    src = pvw[:, :].rearrange("p (k t) -> p k t", k=8)[:, :, 0:nb_]
                    dst = HT.t[:, kh * 8:kh * 8 + 8, 0:nb_]
                    if kind == "h":
                        for k in range(8):
                            P.op("dve", [pb, CB], [HT], lambda e, k=k, kh=kh, pvw=pvw, HT=HT: e.tensor_tensor(
                                HT.t[:, kh * 8 + k, 0:2], pvw[:, k * 128:k * 128 + 2], pvt[:, PV_HM:PV_HM + 2], ALU.mult))
                    elif blk_i[0] % 2:
                        P.op("dve", [pb], [HT], lambda e, dst=dst, src=src: e.tensor_copy(dst, src))
                    else:
                        P.op("act", [pb], [HT], lambda e, dst=dst, src=src: e.activation(dst, src, AF.Copy))
                for (cv0, c0, n) in cvs:
                    P.dma("sp", None, HT, H2d[:, :, cv0:cv0 + n], HT.t[:, :, c0:c0 + n])

            prev = None
            for b in range(nblk):
                cur = x1_part(b)
                if prev is not None:
                    tr_part(prev)
                prev = cur
            tr_part(prev)
        P.barrier()
        P.release([W3, zt] + wpc + oin + gin + xb + x1b + h2t)

    if dbg == 4:
        return nc, es

    WIN = [(510 * k, min(512, 4098 - 510 * k), "p") for k in range(9)]
    WIN += [(CVS0 + 510 * k, min(512, 2050 - 510 * k), "s") for k in range(5)]
    if _lim2:
        WIN = [WIN[i] for i in (0, 1, 13)]
    with ExitStack() as ph:
        def psb(name, shape, dt):
            return ph.enter_context(nc.sbuf_tensor(name, list(shape), dt))
        hw = [Buf(psb("hw%d" % i, [128, 16, 512], BF16)) for i in range(2)]
        wu = [Buf(psb("wu%d" % i, [128, 2, 16, 128], BF16)) for i in range(3)]
        gT = Buf(psb("gT", [128, 44, 512], BF16))
        wd = [Buf(psb("wd%d" % i, [128, 44, 256], BF16)) for i in range(2)]
        cg = [Buf(psb("cg%d" % i, [128, 512], F32)) for i in range(2)]
        cv_ = [Buf(psb("cv%d" % i, [128, 512], F32)) for i in range(2)]
        ge = [Buf(psb("ge%d" % i, [128, 512], F32)) for i in range(2)]
        x1r = [Buf(psb("x1r%d" % i, [128, 4, 256], F32)) for i in range(2)]
        yb = [Buf(psb("yb%d" % i, [128, 256], F32)) for i in range(3)]
        nx_hw, nx_wu, nx_wd, nx_cg, nx_cv, nx_ge = _rot(hw), _rot(wu), _rot(wd), _rot(cg), _rot(cv_), _rot(ge)
        nx_x1r, nx_yb = _rot(x1r), _rot(yb)
        nx_ug, nx_uv, nx_dn = _rot(PS[0:2]), _rot(PS[2:4]), _rot(PS[4:8])

        def cw(j, ft):
            c = PV_CW + j * 88 + ft
            return pvt[:, c:c + 1]

        def cb(ft):
            c = PV_CB + ft
            return pvt[:, c:c + 1]

        hwl = {}

        def ens_hw(w):
            if w < len(WIN) and w not in hwl:
                c0_, n_, _k = WIN[w]
                b_ = nx_hw()
                P.dma("sp", b_, None, b_.t[:, :, 0:n_], H2d[:, :, c0_:c0_ + n_])
                hwl[w] = b_
        wul = {}

        def ens_wu(k):
            if k < 44 * len(WIN) and k not in wul:
                j_ = k % 44
                s_ = nx_wu()
                P.dma("sp", s_, WB["up"], s_.t[:, 0, :, :], wb_up[:, :, j_ * 128:(j_ + 1) * 128])
                P.dma("sp", s_, WB["up"], s_.t[:, 1, :, :], wb_up[:, :, DFF + j_ * 128:DFF + (j_ + 1) * 128])
                wul[k] = s_
        wdl = {}

        def ens_wd(k):
            if k < 8 * len(WIN) and k not in wdl:
                w_, c8_ = k // 8, k % 8
                c0_, n_, _k = WIN[w_]
                no_ = n_ - 2
                s_ = nx_wd()
                P.dma("sp", s_, WB["dn"], s_.t[:], wb_dn[:, :, c8_ * 256:(c8_ + 1) * 256])
                xr_ = nx_x1r()
                for b_ in range((no_ + 127) // 128):
                    nb2 = min(128, no_ - b_ * 128)
                    cvr_ = c0_ + 1 + b_ * 128
                    P.dma("sp", xr_, None, xr_.t[0:nb2, b_, :], X1d[cvr_:cvr_ + nb2, c8_ * 256:(c8_ + 1) * 256])
                wdl[k] = (s_, xr_)

        for wi, (c0, n, kind) in enumerate(WIN):
            no = n - 2
            ens_hw(wi)
            HW = hwl.pop(wi)
            ens_hw(wi + 1)
            for j in range(44):
                k = wi * 44 + j
                ens_wu(k)
                ens_wu(k + 1)
                ens_wu(k + 2)
                WU = wul.pop(k)
                UG, UV = nx_ug(), nx_uv()
                for half, U in ((0, UG), (1, UV)):
                    def mmu(e, U=U, half=half, WU=WU):
                        for kt in range(16):
                            ins = e.matmul(U.t[:, 0:n], WU.t[:, half, kt, :], HW.t[:, kt, 0:n], start=(kt == 0), stop=(kt == 15))
                        return ins
                    P.op("pe", [WU, HW], [U], mmu)
                CG, CV = nx_cg(), nx_cv()
                for U, C, ft in ((UG, CG, j), (UV, CV, 44 + j)):
                    P.op("dve", [U, CB], [C], lambda e, U=U, C=C, ft=ft: e.tensor_scalar(
                        C.t[:, 0:no], U.t[:, 0:no], cw(0, ft), cb(ft), ALU.mult, ALU.add))
                    P.op("dve", [U, C, CB], [C], lambda e, U=U, C=C, ft=ft: e.scalar_tensor_tensor(
                        C.t[:, 0:no], U.t[:, 1:no + 1], cw(1, ft), C.t[:, 0:no], ALU.mult, ALU.add))
                    P.op("dve", [U, C, CB], [C], lambda e, U=U, C=C, ft=ft: e.scalar_tensor_tensor(
                        C.t[:, 0:no], U.t[:, 2:no + 2], cw(2, ft), C.t[:, 0:no], ALU.mult, ALU.add))
                GE = nx_ge()
                P.op("act", [CG], [GE], lambda e, GE=GE, CG=CG: e.activation(GE.t[:, 0:no], CG.t[:, 0:no], AF.Gelu))
                P.op("pool", [GE, CV], [gT], lambda e, GE=GE, CV=CV, j=j: e.tensor_tensor(gT.t[:, j, 0:no], GE.t[:, 0:no], CV.t[:, 0:no], ALU.mult))
                if j == 40:
                    ens_wd(wi * 8)
            nblk = (no + 127) // 128
            for c8 in range(8):
                k = wi * 8 + c8
                ens_wd(k)
                WD, XR = wdl.pop(k)
                if c8 < 7:
                    ens_wd(k + 1)
                for b in range(nblk):
                    nb_ = min(128, no - b * 128)
                    DN = nx_dn()

                    def mmd(e, DN=DN, WD=WD, b=b, nb_=nb_):
                        for j in range(44):
                            ins = e.matmul(DN.t[0:nb_, 0:256], gT.t[:, j, b * 128:b * 128 + nb_], WD.t[:, j, :], start=(j == 0), stop=(j == 43))
                        return ins
                    P.op("pe", [gT, WD], [DN], mmd)
                    cvr = c0 + 1 + b * 128
                    YB = nx_yb()
                    P.op("dve", [DN, XR], [YB], lambda e, YB=YB, DN=DN, XR=XR, nb_=nb_, b=b: e.tensor_tensor(
                        YB.t[0:nb_, 0:256], DN.t[0:nb_, 0:256], XR.t[0:nb_, b, :], ALU.add))
                    if kind == "p":
                        dst = yp[cvr - 1:cvr - 1 + nb_, c8 * 256:(c8 + 1) * 256]
                    else:
                        dst = ys[cvr - CVS0 - 1:cvr - CVS0 - 1 + nb_, c8 * 256:(c8 + 1) * 256]
                    P.dma("sp", None, YB, dst, YB.t[0:nb_, 0:256])
        P.barrier()
    return nc, es


def _tables(c):
    r = c % 4
    own_lo, own_hi = 4096 * r, 4096 * (r + 1)
    u = np.arange(SEQ)
    pk = (own_lo - OWN0 + u) % SEQ
    is_own = (u >= OWN0) & (u < OWN1)
    sig = np.where(is_own, 1.0, np.where(pk < own_lo, 1.0, -1.0))
    pk_all = np.concatenate([pk, np.arange(SSEQ)]).astype(np.float64)
    sig_all = np.concatenate([sig, np.ones(SSEQ)])
    jlo = pk_all % 128
    jhi = pk_all - jlo
    kaug = np.zeros((8, 4, NCTX), np.float32)
    for h in range(8):
        s = SLOPE_A[h]
        kaug[h, 0] = -sig_all * s
        kaug[h, 1] = -sig_all * s
        kaug[h, 2] = sig_all * s * jlo
        kaug[h, 3] = sig_all * s * jhi
    qpos = np.zeros(NQ, np.float64)
    qpos[0:4096] = own_lo + np.arange(4096)
    qpos[QL] = own_lo - 1
    qpos[QR] = own_hi
    qpos[QS0:] = np.arange(SSEQ)
    ilo = qpos % 128
    ihi = qpos - ilo
    qaugp = np.stack([ilo, ihi, np.ones(NQ), np.ones(NQ)]).astype(np.float32)
    qaugm = -qaugp
    qaugm[:, QR] = qaugp[:, QR]
    valb = np.zeros((128, 72), np.float32)
    ub = np.arange(BWIN)
    pb = own_lo - OWN0 + ub
    bad = (pb < 0) | (pb >= SEQ)
    valb[:, :56] = np.where(bad, -30000.0, 0.0).reshape(56, 128).T
    hm = np.array([1.0 if r > 0 else 0.0, 1.0 if r < 3 else 0.0], np.float32)
    return kaug, qaugp, qaugm, valb, hm


def _shared_tables():
    k = np.arange(128)[:, None]
    absa = np.abs(k - np.arange(896)[None, :] + 384).astype(np.float32)
    dbt = np.zeros((128, DB_TOT), np.float32)
    for g in range(3):
        cc = np.arange(DB_W[g])[None, :]
        dl = k - cc + DB_C0[g]
        ok = (np.abs(dl) <= 64 * DIL[g]) & (dl % DIL[g] == 0)
        dbt[:, DB_OFF[g]:DB_OFF[g] + DB_W[g]] = np.where(ok, np.abs(dl), BIGD)
    return absa, dbt, np.eye(128, dtype=np.float32)


def make_in_maps(inp):
    f = lambda a: np.ascontiguousarray(a, dtype=np.float32)
    absa, dbt, ident = _shared_tables()
    w_in, w_pa, w_pb, w_o = f(inp["w_in"][0]), f(inp["w_pa"][0]), f(inp["w_pb"][0]), f(inp["w_o"][0])
    w_up, w_dn = f(inp["w_up"][0]), f(inp["w_down"][0])
    pvb = np.zeros((128, NPV), np.float32)
    gmixb = np.ascontiguousarray(np.broadcast_to(inp["g_mix_norm"][0][None, :], (128, D)), dtype=np.float32)
    gffnb = np.ascontiguousarray(np.broadcast_to(inp["g_ffn_norm"][0][None, :], (128, D)), dtype=np.float32)
    pvb[:, PV_GQA] = np.tile(inp["g_qa"][0], 2)
    pvb[:, PV_GKA] = np.tile(inp["g_ka"][0], 2)
    pvb[:, PV_GQB] = inp["g_qb"][0]
    pvb[:, PV_GKB] = inp["g_kb"][0]
    pvb[:, PV_GSUB] = inp["g_subln"][0]
    cwv = inp["conv_w"][0]
    for j in range(3):
        pvb[:, PV_CW + j * 88:PV_CW + (j + 1) * 88] = cwv[j].reshape(88, 128).T
    pvb[:, PV_CB:PV_CB + 88] = inp["conv_b"][0].reshape(88, 128).T
    for i, kname in enumerate(("lam_q1", "lam_k1", "lam_q2", "lam_k2")):
        pvb[:, PV_LAM + 64 * i:PV_LAM + 64 * (i + 1)] = inp[kname][0][None, :]
    maps = []
    for c in range(8):
        bp, r, bs = c // 4, c % 4, c // 2
        kaug, qaugp, qaugm, valb, hm = _tables(c)
        pvc = pvb.copy()
        pvc[:, PV_HM:PV_HM + 2] = hm[None, :]
        xpl = np.roll(inp["x_prompt"][bp], -(4096 * r - OWN0), axis=0)
        maps.append({
            "xp": f(xpl), "xs": f(inp["x_sample"][bs]),
            "w_in": w_in, "w_pa": w_pa, "w_pb": w_pb, "w_o": w_o, "w_up": w_up, "w_dn": w_dn,
            "pv": pvc, "gmixb": gmixb, "gffnb": gffnb, "kaug": kaug, "qaugp": qaugp, "qaugm": qaugm,
            "absa": absa, "dbt": dbt, "valb": valb, "ident": ident,
        })
    return maps


def kernel(**inp):
    nc, es = build()
    maps = make_in_maps(inp)
    res = run_bass_kernel_spmd(nc, maps, core_ids=list(range(8)))
    es.close()
    yp = np.zeros((2, SEQ, D), np.float32)
    ys = np.zeros((4, SSEQ, D), np.float32)
    for c in range(8):
        bp, r, bs, hs = c // 4, c % 4, c // 2, c % 2
        yp[bp, 4096 * r:4096 * (r + 1)] = res.results[c]["yp"]
        ys[bs, 1024 * hs:1024 * (hs + 1)] = res.results[c]["ys"][1024 * hs:1024 * (hs + 1)]
    return yp, ys
```

```python
import numpy as np
from contextlib import ExitStack
import concourse.bass as bass
import concourse.mybir as mybir
from concourse.bass_utils import run_bass_kernel_spmd

F32 = mybir.dt.float32
BF16 = mybir.dt.bfloat16
AF = mybir.ActivationFunctionType
ALU = mybir.AluOpType
AX = mybir.AxisListType

D = 2048
SEQ = 16384
SSEQ = 2048
N_IN = 11776
DFF = 5632
NCTX = SEQ + SSEQ
OWN0 = 1536
OWN1 = OWN0 + 4096
BWIN = 7168
NB = BWIN + SSEQ
NQ = 4096 + 2 + SSEQ
QL, QR, QS0 = 4096, 4097, 4098
NCV = 4098 + 2050
CVS0 = 4098
SLOPE_A = [2.0 ** (-(h + 1)) for h in range(8)]
SLOPE_B = [2.0 ** (-8.0 * (i + 1) / 12.0) for i in range(12)]
DIL = [1, 4, 16]
BIGD = 1.0e7
DB_C0 = [64 * d + 512 for d in DIL]
DB_W = [DB_C0[g] + 64 * DIL[g] + 640 for g in range(3)]
DB_OFF = [0, DB_W[0], DB_W[0] + DB_W[1]]
DB_TOT = sum(DB_W)

PV_GQA = 0
PV_GKA = 1
PV_GQB = 2
PV_GKB = 3
PV_GSUB = 4
PV_CW = 5
PV_CB = PV_CW + 264
PV_LAM = PV_CB + 88
PV_HM = PV_LAM + 256
NPV = PV_HM + 2

ENGS = ("pe", "act", "dve", "pool", "sp")


class Buf:
    __slots__ = ("t", "w", "r", "ds", "name")

    def __init__(self, t, name=""):
        self.t = t
        self.w = None
        self.r = {}
        self.ds = None
        self.name = name


class Prog:
    def __init__(self, nc, es):
        self.nc = nc
        self.es = es
        self.E = {"pe": nc.tensor, "act": nc.scalar, "dve": nc.vector,
                  "pool": nc.gpsimd, "sp": nc.sync}
        self.sem = {}
        self.cnt = {}
        self.waited = {e: {} for e in ENGS}
        for e in ("pe", "act", "dve", "pool"):
            self._mk("E_" + e)
        self.free_ds = {"sw": [], "hw": []}
        self.nds = 0

    def _mk(self, name):
        self.sem[name] = self.es.enter_context(self.nc.semaphore(name))
        self.cnt[name] = 0

    def get_ds(self, kind):
        if self.free_ds[kind]:
            return self.free_ds[kind].pop()
        name = "D%s%d" % (kind, self.nds)
        self.nds += 1
        self._mk(name)
        return name

    def release(self, bufs):
        for b in bufs:
            if b.ds is not None:
                for kind, nm in b.ds.items():
                    self.free_ds[kind].append(nm)
                b.ds = None

    def wait(self, eng, tok):
        if tok is None:
            return
        s, v = tok
        if eng == "pe" and s == "E_pe":
            return
        if self.waited[eng].get(s, 0) >= v:
            return
        self.waited[eng][s] = v
        self.E[eng].wait_ge(self.sem[s], v)

    def _deps(self, eng, reads, writes):
        for b in reads:
            self.wait(eng, b.w)
        for b in writes:
            self.wait(eng, b.w)
            for t in b.r.values():
                self.wait(eng, t)

    def op(self, eng, reads, writes, fn):
        self._deps(eng, reads, writes)
        ins = fn(self.E[eng])
        s = "E_" + eng
        self.cnt[s] += 1
        ins.then_inc(self.sem[s], 1)
        tok = (s, self.cnt[s])
        for b in reads:
            b.r[eng] = tok
        for b in writes:
            b.w = tok
            b.r = {}
        return tok

    def dma(self, q, out_b, in_b, out_ap, in_ap, track=None):
        reads = [in_b] if in_b is not None else []
        writes = [out_b] if out_b is not None else []
        self._deps(q, reads, writes)
        tb = track if track is not None else (out_b if out_b is not None else in_b)
        if tb.ds is None:
            tb.ds = {}
        kind = "sw" if q == "pool" else "hw"
        if kind not in tb.ds:
            tb.ds[kind] = self.get_ds(kind)
        s = tb.ds[kind]
        self.cnt[s] += 16
        try:
            ins = self.E[q].dma_start(out=out_ap, in_=in_ap)
        except ValueError:
            ins = self.E[q].dma_start(out=out_ap, in_=in_ap, allow_slow_non_contiguous=True)
        ins.then_inc(self.sem[s], 16)
        tok = (s, self.cnt[s])
        if in_b is not None:
            in_b.r["dma_" + s] = tok
        if out_b is not None:
            out_b.w = tok
            out_b.r = {}
        return tok

    def barrier(self):
        for e in ENGS:
            for s, c in self.cnt.items():
                if c > 0:
                    self.wait(e, (s, c))


def _rot(lst):
    i = [0]

    def nxt():
        b = lst[i[0] % len(lst)]
        i[0] += 1
        return b
    return nxt


def build(dbg=False):
    nc = bass.Bass("TRN2", target_bir_lowering=False)
    es = ExitStack()
    P = Prog(nc, es)
    es.enter_context(nc.allow_low_precision(reason="bf16 matmul operands by design, fp32 accumulation"))

    def din(name, shape):
        return nc.dram_tensor(name, list(shape), F32, kind="ExternalInput").ap()

    def dscr(name, shape, dt):
        if dbg:
            return nc.dram_tensor(name, list(shape), dt, kind="ExternalOutput").ap()
        return nc.dram_tensor(name, list(shape), dt).ap()

    xp = din("xp", [SEQ, D])
    xs = din("xs", [SSEQ, D])
    w_in = din("w_in", [D, N_IN])
    w_pa = din("w_pa", [1024, D])
    w_pb = din("w_pb", [512, D])
    w_o = din("w_o", [D, D])
    w_up = din("w_up", [D, 2 * DFF])
    w_dn = din("w_dn", [DFF, D])
    pv = din("pv", [128, NPV])
    gmixb = din("gmixb", [128, D])
    gffnb = din("gffnb", [128, D])
    kaug = din("kaug", [8, 4, NCTX])
    qaugp = din("qaugp", [4, NQ])
    qaugm = din("qaugm", [4, NQ])
    absa = din("absa", [128, 896])
    dbt = din("dbt", [128, DB_TOT])
    valb = din("valb", [128, 72])
    ident = din("ident", [128, 128])
    yp = nc.dram_tensor("yp", [4096, D], F32, kind="ExternalOutput").ap()
    ys = nc.dram_tensor("ys", [SSEQ, D], F32, kind="ExternalOutput").ap()

    wb_in = nc.dram_tensor("wb_in", [128, 16, N_IN], BF16).ap()
    wb_pa = nc.dram_tensor("wb_pa", [128, 8, D], BF16).ap()
    wb_pb = nc.dram_tensor("wb_pb", [128, 4, D], BF16).ap()
    wb_o = nc.dram_tensor("wb_o", [128, 16, D], BF16).ap()
    wb_up = nc.dram_tensor("wb_up", [128, 16, 2 * DFF], BF16).ap()
    wb_dn = nc.dram_tensor("wb_dn", [128, 44, D], BF16).ap()
    kaug_b = nc.dram_tensor("kaug_b", [8, 4, NCTX], BF16).ap()
    qaugp_b = nc.dram_tensor("qaugp_b", [4, NQ], BF16).ap()
    qaugm_b = nc.dram_tensor("qaugm_b", [4, NQ], BF16).ap()
    QAd = dscr("QAd", [8, 128, NQ], BF16)
    KAd = dscr("KAd", [8, 128, NCTX], BF16)
    VAd = dscr("VAd", [8, NCTX, 128], BF16)
    QBd = dscr("QBd", [12, 128, NQ], BF16)
    KBd = dscr("KBd", [12, 128, NB], BF16)
    VBd = dscr("VBd", [12, NB, 128], BF16)
    GTd = dscr("GTd", [32, 128, NQ], BF16)
    OAd = dscr("OAd", [8, 128, NQ], BF16)
    OBd = dscr("OBd", [4, 128, NQ], BF16)
    X1d = dscr("X1d", [NCV, D], F32)
    H2d = dscr("H2d", [128, 16, NCV], BF16)

    def sb(name, shape, dt):
        return es.enter_context(nc.sbuf_tensor(name, list(shape), dt))

    PDt = [es.enter_context(nc.psum_tensor("pd%d" % i, [128, 1024], F32)) for i in range(4)]
    PSt = [PDt[i // 2][:, (i % 2) * 512:(i % 2 + 1) * 512] for i in range(8)]
    PS = [Buf(t, "ps%d" % i) for i, t in enumerate(PSt)]

    pvt = sb("pvt", [128, NPV], F32)
    cst = sb("cst", [128, 16], F32)
    idt = sb("idt", [128, 128], BF16)
    idf = sb("idf", [128, 128], F32)
    ones = sb("ones", [128, 128], BF16)
    blk = sb("blk", [128, 128], BF16)
    lamt = sb("lamt", [128, 64], F32)
    lams = sb("lams", [128, 4], F32)
    CB = Buf(None, "consts")

    WB = {}

    def cast(name, dst, src, ktn):
        b = Buf(None, name)
        for kt in range(ktn):
            P.dma("pool", b, None, dst[:, kt, :], src[kt * 128:(kt + 1) * 128, :])
        WB[name] = b

    TB = Buf(None, "tables")
    P.dma("pool", CB, None, idt[:], ident[:, :])
    P.dma("pool", TB, None, kaug_b[:, :, :], kaug[:, :, :])
    P.dma("pool", TB, None, qaugp_b[:, :], qaugp[:, :])
    P.dma("pool", TB, None, qaugm_b[:, :], qaugm[:, :])
    bkv = Buf(None, "in_kv")
    for kt in range(16):
        P.dma("pool", bkv, None, wb_in[:, kt, 1024:3072], w_in[kt * 128:(kt + 1) * 128, 1024:3072])
    WB["in_kv"] = bkv
    brest = Buf(None, "in")
    for kt in range(16):
        P.dma("pool", brest, None, wb_in[:, kt, 0:1024], w_in[kt * 128:(kt + 1) * 128, 0:1024])
        P.dma("pool", brest, None, wb_in[:, kt, 3072:N_IN], w_in[kt * 128:(kt + 1) * 128, 3072:N_IN])
    WB["in"] = brest
    cast("pa", wb_pa, w_pa, 8)
    cast("pb", wb_pb, w_pb, 4)
    cast("o", wb_o, w_o, 16)
    cast("up", wb_up, w_up, 16)
    cast("dn", wb_dn, w_dn, 44)

    P.dma("sp", CB, None, pvt[:], pv[:, :])
    P.dma("sp", CB, None, idf[:], ident[:, :])

    def c_op(eng, fn):
        P.op(eng, [CB], [CB], fn)

    c_op("dve", lambda e: e.memset(cst[:, 0:1], 1e-6))
    c_op("dve", lambda e: e.memset(cst[:, 1:2], 1e-5))
    c_op("dve", lambda e: e.memset(cst[:, 8:9], 0.0))
    c_op("dve", lambda e: e.memset(ones[:], 1.0))
    c_op("dve", lambda e: e.memset(blk[:], 0.0))
    c_op("dve", lambda e: e.memset(blk[0:64, 0:64], 1.0))
    c_op("dve", lambda e: e.memset(blk[64:128, 64:128], 1.0))
    c_op("dve", lambda e: e.tensor_scalar(cst[:, 3:4], pvt[:, PV_GQA:PV_GQA + 1], 0.125, None, ALU.mult))
    c_op("dve", lambda e: e.tensor_copy(cst[:, 4:5], pvt[:, PV_GKA:PV_GKA + 1]))
    c_op("dve", lambda e: e.tensor_scalar(cst[:, 5:6], pvt[:, PV_GQB:PV_GQB + 1], 128.0 ** -0.5, None, ALU.mult))
    c_op("dve", lambda e: e.tensor_copy(cst[:, 6:7], pvt[:, PV_GKB:PV_GKB + 1]))
    c_op("dve", lambda e: e.tensor_scalar(cst[:, 7:8], pvt[:, PV_GSUB:PV_GSUB + 1], 0.8, None, ALU.mult))
    for j in range(2):
        a0 = PV_LAM + 128 * j
        c_op("dve", lambda e, a0=a0: e.tensor_tensor(lamt[:], pvt[:, a0:a0 + 64], pvt[:, a0 + 64:a0 + 128], ALU.mult))
        c_op("dve", lambda e, j=j: e.tensor_reduce(lams[:, j:j + 1], lamt[:], AX.X, ALU.add))
        c_op("act", lambda e, j=j: e.activation(lams[:, 2 + j:3 + j], lams[:, j:j + 1], AF.Exp))
    c_op("dve", lambda e: e.tensor_tensor(lams[:, 0:1], lams[:, 3:4], lams[:, 2:3], ALU.subtract))
    c_op("dve", lambda e: e.tensor_scalar(cst[:, 2:3], lams[:, 0:1], -0.2, None, ALU.add))
    EPS6 = cst[:, 0:1]
    EPS5 = cst[:, 1:2]
    NEGLAM = cst[:, 2:3]

    def norm_tail(N, ps_z, ps_bs, sqb, lnb, onesmat, inv_n, eps_ap, g_ap, out_b, z_sbuf=None):
        zb = z_sbuf if z_sbuf is not None else ps_z
        P.op("act", [zb], [sqb], lambda e: e.activation(sqb.t[:, 0:N], zb.t[:, 0:N], AF.Square))
        P.op("pe", [sqb, CB], [ps_bs], lambda e: e.matmul(ps_bs.t[:, 0:N], onesmat, sqb.t[:, 0:N], start=True, stop=True))
        P.op("act", [ps_bs, CB], [lnb], lambda e: e.activation(lnb.t[:, 0:N], ps_bs.t[:, 0:N], AF.Ln, bias=eps_ap, scale=inv_n))
        P.op("act", [lnb], [lnb], lambda e: e.activation(lnb.t[:, 0:N], lnb.t[:, 0:N], AF.Exp, scale=-0.5))
        P.op("dve", [zb, lnb, CB], [out_b], lambda e: e.scalar_tensor_tensor(
            out_b.t[:, 0:N], zb.t[:, 0:N], g_ap, lnb.t[:, 0:N], ALU.mult, ALU.mult))

    with ExitStack() as ph:
        def psb(name, shape, dt):
            return ph.enter_context(nc.sbuf_tensor(name, list(shape), dt))
        xt = [Buf(psb("xt%d" % i, [128, D], F32)) for i in range(3)]
        junk = psb("junk", [128, D], BF16)
        ssb = [Buf(psb("ssb%d" % i, [128, 8], F32)) for i in range(2)]
        xsb = [Buf(psb("xsb%d" % i, [128, D], BF16)) for i in range(2)]
        hTt = [psb("hT%d" % i, [128, 16, 512], BF16) for i in range(2)]
        hT = [[Buf(t) for _ in range(2)] for t in hTt]
        gmt = psb("gmt", [128, D], F32)
        GM = Buf(None)
        P.dma("sp", GM, None, gmt[:], gmixb[:, :])
        wch = [Buf(psb("wch%d" % i, [128, 16, 512], BF16)) for i in range(4)]
        zst = [Buf(psb("zst%d" % i, [128, 512], BF16)) for i in range(4)]
        vst = [Buf(psb("vst%d" % i, [128, 512], BF16)) for i in range(3)]
        sqb = [Buf(psb("sqb%d" % i, [128, 512], BF16)) for i in range(2)]
        lnb = [Buf(psb("lnb%d" % i, [128, 512], F32)) for i in range(2)]
        gex = [Buf(psb("gex%d" % i, [128, 512], F32)) for i in range(2)]
        nx_xt, nx_ss, nx_xs, nx_hT = _rot(xt), _rot(ssb), _rot(xsb), _rot(hT)
        nx_zst, nx_vst, nx_sq, nx_ln, nx_gex = _rot(zst), _rot(vst), _rot(sqb), _rot(lnb), _rot(gex)
        ptb = [(PS[i], PSt[i].bitcast(BF16)) for i in range(2)]
        nx_pt = _rot(ptb)
        nx_z = _rot(PS[2:6])
        nx_bs = _rot(PS[6:8])
        evac_i = [0]
        wslot = _rot(wch)

        def load_chunk(ci, slot):
            P.dma("sp", slot, WB["in_kv" if ci in (2, 3, 4, 5) else "in"], slot.t[:], wb_in[:, :, ci * 512:(ci + 1) * 512])

        import os
        _lim = os.environ.get("KDBG", "")

        def prologue(xsrc, r0):
            Hs = nx_hT()
            Ht = Hs[0].t
            ss = nx_ss()
            for b in range(4):
                X = nx_xt()
                P.dma("sp", X, None, X.t[:], xsrc[r0 + b * 128:r0 + (b + 1) * 128, :])
                P.op("act", [X], [ss], lambda e, X=X, b=b: e.activation(
                    junk[:], X.t[:], AF.Square, accum_out=ss.t[:, b:b + 1]))
                P.op("act", [ss, CB], [ss], lambda e, b=b: e.activation(ss.t[:, 4 + b:5 + b], ss.t[:, b:b + 1], AF.Ln, bias=EPS6, scale=1.0 / D))
                P.op("act", [ss], [ss], lambda e, b=b: e.activation(ss.t[:, 4 + b:5 + b], ss.t[:, 4 + b:5 + b], AF.Exp, scale=-0.5))
                S = nx_xs()
                P.op("dve", [X, ss, GM], [S], lambda e, X=X, S=S, b=b: e.scalar_tensor_tensor(
                    S.t[:], X.t[:], ss.t[:, 4 + b:5 + b], gmt[:], ALU.mult, ALU.mult))
                for kh in range(2):
                    pb, pvw = nx_pt()

                    def tr(e, S=S, pvw=pvw, kh=kh):
                        for k in range(8):
                            kt = kh * 8 + k
                            ins = e.transpose(pvw[:, k * 128:(k + 1) * 128], S.t[:, kt * 128:(kt + 1) * 128], idt[:])
                        return ins
                    P.op("pe", [S, CB], [pb], tr)
                    src = pvw[:, :].rearrange("p (k t) -> p k t", k=8)
                    dst = Ht[:, kh * 8:kh * 8 + 8, b * 128:(b + 1) * 128]
                    if kh % 2:
                        P.op("dve", [pb], [Hs[kh]], lambda e, dst=dst, src=src: e.tensor_copy(dst, src))
                    else:
                        P.op("act", [pb], [Hs[kh]], lambda e, dst=dst, src=src: e.activation(dst, src, AF.Copy))
            return Hs

        stream = []
        loaded = {}
        spos = [0]

        last_res = [-1]

        def ensure(k):
            if k < len(stream) and k not in loaded:
                cj, resident = stream[k]
                if cj in resident:
                    loaded[k] = resident[cj]
                else:
                    if spos[0] <= last_res[0] + 1:
                        return
                    sl = wslot()
                    load_chunk(cj, sl)
                    loaded[k] = sl

        def project(Hs, ctx0, b0, q0, qc, chunks, hoist=None, hoist_at=0):
            Ht = Hs[0].t
            for j, ci in enumerate(chunks):
                k = spos[0]
                spos[0] += 1
                ensure(k)
                ensure(k + 1)
                ensure(k + 2)
                W = loaded.pop(k)
                if hoist is not None and j == hoist_at:
                    hoist()
                if ci in (4, 5, 12, 13, 14):
                    for b in range(4):
                        Z = nx_z()

                        def mm(e, W=W, Z=Z, b=b):
                            for kt in range(16):
                                ins = e.matmul(Z.t[:, :], Ht[:, kt, b * 128:(b + 1) * 128], W.t[:, kt, :], start=(kt == 0), stop=(kt == 15))
                            return ins
                        P.op("pe", Hs + [W], [Z], mm)
                        V = nx_vst()
                        evac_i[0] += 1
                        if evac_i[0] % 2:
                            P.op("dve", [Z], [V], lambda e, V=V, Z=Z: e.tensor_copy(V.t[:], Z.t[:]))
                        else:
                            P.op("act", [Z], [V], lambda e, V=V, Z=Z: e.activation(V.t[:], Z.t[:], AF.Copy))
                        if ci in (4, 5):
                            h0 = 4 * (ci - 4)
                            dst = VAd[h0:h0 + 4, ctx0 + b * 128:ctx0 + (b + 1) * 128, :].rearrange("h t e -> t h e")
                        else:
                            h0 = 4 * (ci - 12)
                            dst = VBd[h0:h0 + 4, b0 + b * 128:b0 + (b + 1) * 128, :].rearrange("h t e -> t h e")
                        P.dma("sp", None, V, dst, V.t[:].rearrange("p (h e) -> p h e", h=4))
                    continue
                for f in range(4):
                    ft = ci * 4 + f
                    isq = ft < 8 or 24 <= ft < 36 or ft >= 60
                    c0, c1 = qc if isq else (0, 512)
                    N = c1 - c0
                    Z = nx_z()

                    def mm(e, W=W, Z=Z, f=f, c0=c0, c1=c1, N=N):
                        for kt in range(16):
                            ins = e.matmul(Z.t[:, 0:N], W.t[:, kt, f * 128:(f + 1) * 128], Ht[:, kt, c0:c1], start=(kt == 0), stop=(kt == 15))
                        return ins
                    P.op("pe", Hs + [W], [Z], mm)
                    O = nx_zst()
                    if ft >= 60:
                        G = nx_gex()
                        P.op("act", [Z], [G], lambda e, G=G, Z=Z, N=N: e.activation(G.t[:, 0:N], Z.t[:, 0:N], AF.Exp, scale=-1.0))
                        P.op("dve", [G], [G], lambda e, G=G, N=N: e.tensor_scalar(G.t[:, 0:N], G.t[:, 0:N], 1.0, None, ALU.add))
                        P.op("dve", [G], [O], lambda e, G=G, O=O, N=N: e.reciprocal(O.t[:, 0:N], G.t[:, 0:N]))
                        dst = GTd[ft - 60, :, q0:q0 + N]
                    else:
                        if ft < 8:
                            om, inv, g, dst = blk[:], 1.0 / 64, cst[:, 3:4], QAd[ft, :, q0:q0 + N]
                        elif ft < 16:
                            om, inv, g, dst = blk[:], 1.0 / 64, cst[:, 4:5], KAd[ft - 8, :, ctx0:ctx0 + 512]
                        elif ft < 36:
                            om, inv, g, dst = ones[:], 1.0 / 128, cst[:, 5:6], QBd[ft - 24, :, q0:q0 + N]
                        else:
                            om, inv, g, dst = ones[:], 1.0 / 128, cst[:, 6:7], KBd[ft - 36, :, b0:b0 + 512]
                        norm_tail(N, Z, nx_bs(), nx_sq(), nx_ln(), om, inv, EPS6, g, O)
                    P.dma("sp", None, O, dst, O.t[:, 0:N])

        KV = [2, 3, 4, 5]
        KVB = [2, 3, 4, 5, 9, 10, 11, 12, 13, 14]
        ALLC = list(range(23))
        res = {}
        for ci in KV:
            sl = wslot()
            load_chunk(ci, sl)
            res[ci] = sl
        TD = []
        for T in range(14, 32):
            TD.append((xp, T * 512, T * 512, None, None, None, KV, res))
        for T in range(0, 14):
            if 3 <= T <= 10:
                TD.append((xp, T * 512, T * 512, T * 512, (T - 3) * 512, (0, 512), ALLC, {}))
            elif T == 2:
                TD.append((xp, T * 512, T * 512, T * 512, QL, (511, 512), ALLC, {}))
            elif T == 11:
                TD.append((xp, T * 512, T * 512, T * 512, QR, (0, 1), ALLC, {}))
            else:
                TD.append((xp, T * 512, T * 512, T * 512, None, None, KVB, {}))
        for T in range(4):
            TD.append((xs, T * 512, SEQ + T * 512, BWIN + T * 512, QS0 + T * 512, (0, 512), ALLC, {}))
        for d in TD:
            for ci in d[6]:
                if ci in d[7]:
                    last_res[0] = len(stream)
                stream.append((ci, d[7]))
        nextH = [prologue(TD[0][0], TD[0][1])]
        for i, d in enumerate(TD):
            Hcur = nextH[0]

            def hoist(i=i):
                if i + 1 < len(TD):
                    nextH[0] = prologue(TD[i + 1][0], TD[i + 1][1])
            project(Hcur, d[2], d[3], d[4], d[5], d[6], hoist=hoist, hoist_at=min(1, len(d[6]) - 1))
        P.barrier()
        P.release(xt + wch + zst + vst + [GM])

    if dbg == 1:
        return nc, es

    import os
    _lim2 = os.environ.get("KDBG2", "")
    QT = [("p", t, t * 512, 512) for t in range(8)] + [("h", 0, QL, 2)] + [("s", t, QS0 + t * 512, 512) for t in range(4)]
    with ExitStack() as ph:
        def psb(name, shape, dt):
            return ph.enter_context(nc.sbuf_tensor(name, list(shape), dt))
        KT = [Buf(psb("KT%d" % m, [68, NCTX], BF16)) for m in range(2)]
        VT = Buf(psb("VT", [128, 144, 128], BF16))
        QSb = [Buf(psb("QS%d" % i, [68, 6, 512], BF16)) for i in range(2)]
        PT = [Buf(psb("PT%d" % i, [128, 2, 512], BF16)) for i in range(6)]
        PSm = [Buf(psb("PSm%d" % i, [128, 2, 512], BF16)) for i in range(3)]
        nx_PSm = _rot(PSm)
        SBf = [Buf(psb("SBf%d" % i, [128, 2, 512], F32)) for i in range(2)]
        absat = psb("absat", [128, 896], F32)
        lr = [Buf(psb("lr%d" % i, [128, 512], F32)) for i in range(2)]
        o12 = [Buf(psb("o12%d" % i, [128, 512], F32)) for i in range(2)]
        ob = Buf(psb("ob", [128, 512], F32))
        sq2 = Buf(psb("sq2", [128, 512], BF16))
        ln2 = Buf(psb("ln2", [128, 512], F32))
        oast = [Buf(psb("oast%d" % i, [128, 512], BF16)) for i in range(2)]
        nx_QS, nx_PT, nx_SB, nx_oast = _rot(QSb), _rot(PT), _rot(SBf), _rot(oast)
        nx_sc = _rot(PS[0:4])
        nx_scp = _rot([(PS[0], PS[1], PDt[0]), (PS[2], PS[3], PDt[1])])
        OB_, LB_ = PS[4:6], PS[6:8]
        AT = Buf(None)
        P.dma("sp", AT, None, absat[:], absa[:, :])
        for q in QSb:
            P.op("dve", [], [q], lambda e, q=q: e.memset(q.t[64:68, 4:6, :], 0.0))

        QTL = [q for i, q in enumerate(QT) if not (_lim2 and i not in (0, 1, 8, 12))]
        KTc = [[Buf(KT[m].t) for _ in range(4)] for m in range(2)]
        VTc = [Buf(VT.t) for _ in range(4)]

        def load_kv(hh, cs):
            for c4 in cs:
                k0, k1 = c4 * 4608, (c4 + 1) * 4608
                for m in range(2):
                    P.dma("sp", KTc[m][c4], None, KT[m].t[0:64, k0:k1], KAd[hh, m * 64:(m + 1) * 64, k0:k1])
                    P.dma("sp", KTc[m][c4], TB, KT[m].t[64:68, k0:k1], kaug_b[hh, :, k0:k1])
                P.dma("sp", VTc[c4], None, VT.t[:, c4 * 36:(c4 + 1) * 36, :],
                      VAd[hh, k0:k1, :].rearrange("(kt p) e -> p kt e", p=128))
        qitems = [(hh, qq) for hh in range(8) for qq in QTL]
        qld = {}

        def ens_q(k):
            if k < len(qitems) and k not in qld:
                hh, (kind_, t_, q0_, N_) = qitems[k]
                Qb = nx_QS()
                qsrc = QAd[hh, :, q0_:q0_ + N_].rearrange("(m d) q -> d m q", m=2)
                for v in range(3):
                    P.dma("sp", Qb, None, Qb.t[0:64, 2 * v:2 * v + 2, 0:N_], qsrc)
                for m in range(2):
                    P.dma("sp", Qb, TB, Qb.t[64:68, m, 0:N_], qaugp_b[:, q0_:q0_ + N_])
                    P.dma("sp", Qb, TB, Qb.t[64:68, 2 + m, 0:N_], qaugm_b[:, q0_:q0_ + N_])
                qld[k] = Qb
        load_kv(0, [0, 1, 2, 3])
        for h in range(8):
            for qi_, (kind, t, q0, N) in enumerate(QTL):
                kq = h * len(QTL) + qi_
                ens_q(kq)
                Q = qld.pop(kq)
                ens_q(kq + 1)
                if kind == "s" and t == (3 if _lim2 else 0) and h + 1 < 8:
                    load_kv(h + 1, [0, 1, 2])
                if kind == "s":
                    kts = list(range(128, 144))
                else:
                    kts = list(range(0, 128))
                tiles = []
                for kt in kts:
                    if kind == "p":
                        d0 = 12 + 4 * t
                        if kt < 12 or kt >= 44 or kt < d0:
                            var, dk = 0, None
                        elif kt < d0 + 4:
                            var, dk = 2, kt - d0
                        else:
                            var, dk = 1, None
                    elif kind == "h":
                        var, dk = (1 if 12 <= kt < 44 else 0), None
                    else:
                        d0 = 128 + 4 * t
                        if kt < d0:
                            var, dk = 0, None
                        elif kt < d0 + 4:
                            var, dk = 2, kt - d0
                        else:
                            var, dk = 1, None
                    tiles.append((kt, var, dk))
                nt = len(tiles)
                DEPTH = 2
                pts = [None] * nt
                sums = [None] * nt
                for i in range(nt + DEPTH):
                    if i < nt:
                        kt, var, dk = tiles[i]
                        SCa, SCb, PDp = nx_scp()
                        SCm = (SCa, SCb)
                        for m in range(2):
                            P.op("pe", [KTc[m][kt // 36], Q], [SCm[m]], lambda e, m=m, kt=kt, var=var, Q=Q, SCm=SCm: e.matmul(
                                SCm[m].t[:, 0:N], KT[m].t[0:68, kt * 128:(kt + 1) * 128], Q.t[0:68, 2 * var + m, 0:N], start=True, stop=True))
                        if dk is not None:
                            S2 = nx_SB()
                            off = 384 - 128 * dk
                            for m in range(2):
                                P.op("dve", [SCm[m], AT], [S2], lambda e, S2=S2, m=m, SCm=SCm, off=off: e.scalar_tensor_tensor(
                                    S2.t[:, m, 0:N], absat[:, off:off + N], -SLOPE_A[h], SCm[m].t[:, 0:N], ALU.mult, ALU.add))
                            srcb = [S2]
                            src_ap = S2.t[:, :, 0:N]
                        else:
                            srcb = [SCa, SCb]
                            src_ap = PDp[:, :].rearrange("p (m n) -> p m n", m=2)[:, :, 0:N]
                        Pt = nx_PT()
                        P.op("act", srcb, [Pt], lambda e, Pt=Pt, src_ap=src_ap: e.activation(Pt.t[:, :, 0:N], src_ap, AF.Exp))
                        pts[i] = Pt
                        if i % 2 == 1:
                            Sm = nx_PSm()
                            eng = "dve"
                            Pa = pts[i - 1]
                            P.op(eng, [Pa, Pt], [Sm], lambda e, Sm=Sm, Pa=Pa, Pt=Pt: e.tensor_tensor(
                                Sm.t[:, :, 0:N], Pa.t[:, :, 0:N], Pt.t[:, :, 0:N], ALU.add))
                            sums[i] = Sm
                    j = i - DEPTH
                    if j >= 0:
                        kt, var, dk = tiles[j]
                        Pt = pts[j]
                        st = (j == 0)
                        sp_ = (j == nt - 1)
                        for m in range(2):
                            P.op("pe", [VTc[kt // 36], Pt], [OB_[m]], lambda e, m=m, kt=kt, Pt=Pt, st=st, sp_=sp_: e.matmul(
                                OB_[m].t[:, 0:N], VT.t[:, kt, :], Pt.t[:, m, 0:N], start=st, stop=sp_))
                        if j % 2 == 1 or j == nt - 1:
                            Ls = sums[j] if j % 2 == 1 else Pt
                            lst = (j <= 1)
                            for m in range(2):
                                P.op("pe", [CB, Ls], [LB_[m]], lambda e, m=m, Ls=Ls, lst=lst, sp_=sp_: e.matmul(
                                    LB_[m].t[:, 0:N], ones[:], Ls.t[:, m, 0:N], start=lst, stop=sp_))
                for m in range(2):
                    P.op("act", [LB_[m]], [lr[m]], lambda e, m=m: e.activation(lr[m].t[:, 0:N], LB_[m].t[:, 0:N], AF.Copy))
                    P.op("dve", [OB_[m]], [o12[m]], lambda e, m=m: e.tensor_copy(o12[m].t[:, 0:N], OB_[m].t[:, 0:N]))
                for m in range(2):
                    P.op("dve", [lr[m]], [lr[m]], lambda e, m=m: e.reciprocal(lr[m].t[:, 0:N], lr[m].t[:, 0:N]))
                    P.op("dve", [o12[m], lr[m]], [o12[m]], lambda e, m=m: e.tensor_tensor(
                        o12[m].t[:, 0:N], o12[m].t[:, 0:N], lr[m].t[:, 0:N], ALU.mult))
                P.op("dve", [o12[0], o12[1], CB], [ob], lambda e: e.scalar_tensor_tensor(
                    ob.t[:, 0:N], o12[1].t[:, 0:N], NEGLAM, o12[0].t[:, 0:N], ALU.mult, ALU.add))
                OA = nx_oast()
                norm_tail(N, None, nx_sc(), sq2, ln2, ones[:], 1.0 / 128, EPS5, cst[:, 7:8], OA, z_sbuf=ob)
                P.dma("sp", None, OA, OAd[h, :, q0:q0 + N], OA.t[:, 0:N])
            if h + 1 < 8:
                load_kv(h + 1, [3])
        P.barrier()
        P.release(KTc[0] + KTc[1] + VTc + QSb + oast + [AT])

    if dbg == 2:
        return nc, es

    QTB = [(t * 512, 512, OWN0 + t * 512, 0, 56) for t in range(8)]
    QTB += [(QL, 1, OWN0 - 1, 0, 56), (QR, 1, OWN1, 0, 56)]
    QTB += [(QS0 + t * 512, 512, t * 512, 56, 72) for t in range(4)]
    with ExitStack() as ph:
        def psb(name, shape, dt):
            return ph.enter_context(nc.sbuf_tensor(name, list(shape), dt))
        KBT = Buf(psb("KBT", [128, 3, NB], BF16))
        VBT = Buf(psb("VBT", [128, 3, 72, 128], BF16))
        QBT = [Buf(psb("QBT%d" % i, [128, 3, 512], BF16)) for i in range(2)]
        dbs = psb("dbs", [128, DB_TOT], BF16)
        vbs = psb("vbs", [128, 72], F32)
        PT = [Buf(psb("PTb%d" % i, [128, 512], BF16)) for i in range(6)]
        SBf = [Buf(psb("SBb%d" % i, [128, 512], F32)) for i in range(4)]
        lrb = Buf(psb("lrb", [128, 512], F32))
        obf = Buf(psb("obf", [128, 512], F32))
        obst = [Buf(psb("obst%d" % i, [128, 512], BF16)) for i in range(2)]
        nx_QB, nx_PT, nx_SB, nx_obst = _rot(QBT), _rot(PT), _rot(SBf), _rot(obst)
        nx_sc = _rot(PS[0:4])
        OBk, LBk = PS[4], PS[6]
        DT = Buf(None)
        P.dma("pool", DT, None, dbs[:], dbt[:, :])
        P.dma("sp", DT, None, vbs[:], valb[:, :])
        KBg = [Buf(KBT.t) for _ in range(3)]
        VBg = [Buf(VBT.t) for _ in range(3)]
        qbitems = [(hh, qq) for hh in range(4) for qq in QTB]
        qbl = {}

        def ens_qb(k):
            if k < len(qbitems) and k not in qbl:
                hh, (q0_, N_, _a, _b, _c) = qbitems[k]
                Qb = nx_QB()
                for g in range(3):
                    P.dma("sp", Qb, None, Qb.t[:, g, 0:N_], QBd[g * 4 + hh, :, q0_:q0_ + N_])
                qbl[k] = Qb
        for h in range(4):
            for g in range(3):
                P.dma("sp", KBg[g], None, KBT.t[:, g, :], KBd[g * 4 + h, :, :])
                for c2 in range(2):
                    P.dma("sp", VBg[g], None, VBT.t[:, g, c2 * 36:(c2 + 1) * 36, :],
                          VBd[g * 4 + h, c2 * 4608:(c2 + 1) * 4608, :].rearrange("(kt p) e -> p kt e", p=128))
            for qi_, (q0, N, qpos, klo, khi) in enumerate(QTB):
                kq = h * len(QTB) + qi_
                ens_qb(kq)
                Q = qbl.pop(kq)
                ens_qb(kq + 1)
                tiles = []
                for g in range(3):
                    wl = 64 * DIL[g]
                    for kt in range(klo, khi):
                        kpos = (kt - klo) * 128
                        dbase = kpos - qpos
                        if dbase - (N - 1) <= wl and dbase + 127 >= -wl:
                            tiles.append((g, kt, DB_OFF[g] + DB_C0[g] - dbase))
                nt = len(tiles)
                DEPTH = 3
                pts = [None] * nt
                for i in range(nt + DEPTH):
                    if i < nt:
                        g, kt, off = tiles[i]
                        SC = nx_sc()
                        P.op("pe", [KBg[g], Q], [SC], lambda e, SC=SC, g=g, kt=kt, Q=Q: e.matmul(
                            SC.t[:, 0:N], KBT.t[:, g, kt * 128:(kt + 1) * 128], Q.t[:, g, 0:N], start=True, stop=True))
                        S2 = nx_SB()
                        sl = -SLOPE_B[g * 4 + h]
                        P.op("dve", [SC, DT], [S2], lambda e, S2=S2, SC=SC, off=off, sl=sl: e.scalar_tensor_tensor(
                            S2.t[:, 0:N], dbs[:, off:off + N], sl, SC.t[:, 0:N], ALU.mult, ALU.add))
                        Pt = nx_PT()
                        P.op("act", [S2, DT], [Pt], lambda e, Pt=Pt, S2=S2, kt=kt: e.activation(
                            Pt.t[:, 0:N], S2.t[:, 0:N], AF.Exp, bias=vbs[:, kt:kt + 1]))
                        pts[i] = Pt
                    j = i - DEPTH
                    if j >= 0:
                        g, kt, off = tiles[j]
                        Pt = pts[j]
                        P.op("pe", [VBg[g], Pt], [OBk], lambda e, g=g, kt=kt, Pt=Pt, j=j: e.matmul(
                            OBk.t[:, 0:N], VBT.t[:, g, kt, :], Pt.t[:, 0:N], start=(j == 0), stop=(j == nt - 1)))
                        P.op("pe", [CB, Pt], [LBk], lambda e, Pt=Pt, j=j: e.matmul(
                            LBk.t[:, 0:N], ones[:], Pt.t[:, 0:N], start=(j == 0), stop=(j == nt - 1)))
                P.op("act", [LBk], [lrb], lambda e: e.activation(lrb.t[:, 0:N], LBk.t[:, 0:N], AF.Copy))
                P.op("dve", [OBk], [obf], lambda e: e.tensor_copy(obf.t[:, 0:N], OBk.t[:, 0:N]))
                P.op("dve", [lrb], [lrb], lambda e: e.reciprocal(lrb.t[:, 0:N], lrb.t[:, 0:N]))
                OO = nx_obst()
                P.op("dve", [obf, lrb], [OO], lambda e, OO=OO: e.tensor_tensor(OO.t[:, 0:N], obf.t[:, 0:N], lrb.t[:, 0:N], ALU.mult))
                P.dma("sp", None, OO, OBd[h, :, q0:q0 + N], OO.t[:, 0:N])
        P.barrier()
        P.release(KBg + VBg + [DT] + QBT + obst)

    if dbg == 3:
        return nc, es

    with ExitStack() as ph:
        def psb(name, shape, dt):
            return ph.enter_context(nc.sbuf_tensor(name, list(shape), dt))
        wo = psb("wo", [128, 16, D], BF16)
        gft = psb("gft", [128, D], F32)
        W3 = Buf(None)
        P.dma("sp", W3, WB["o"], wo[:], wb_o[:, :, :])
        P.dma("sp", W3, None, gft[:], gffnb[:, :])
        wpc = [Buf(psb("wpc%d" % i, [128, 12, 256], BF16)) for i in range(2)]
        oin = [Buf(psb("oin%d" % i, [128, 12, 512], BF16)) for i in range(2)]
        gin = [Buf(psb("gin%d" % i, [128, 2, 512], BF16)) for i in range(2)]
        mT = Buf(psb("mT", [128, 16, 512], BF16))
        t1 = [Buf(psb("t1%d" % i, [128, 512], F32)) for i in range(2)]
        t2 = [Buf(psb("t2%d" % i, [128, 512], F32)) for i in range(2)]
        xb = [Buf(psb("xb%d" % i, [128, D], F32)) for i in range(2)]
        x1b = [Buf(psb("x1b%d" % i, [128, D], F32)) for i in range(2)]
        s3 = [Buf(psb("s3%d" % i, [128, 4], F32)) for i in range(2)]
        h2s = [Buf(psb("h2s%d" % i, [128, D], BF16)) for i in range(2)]
        h2t = [Buf(psb("h2t%d" % i, [128, 16, 128], BF16)) for i in range(2)]
        zt = Buf(psb("zt", [128, 16, 2], BF16))
        nx_wpc, nx_oin, nx_gin, nx_t1, nx_t2 = _rot(wpc), _rot(oin), _rot(gin), _rot(t1), _rot(t2)
        nx_xb, nx_x1b, nx_s3, nx_h2s, nx_h2t = _rot(xb), _rot(x1b), _rot(s3), _rot(h2s), _rot(h2t)
        nx_pa, nx_pbk = _rot(PS[0:2]), _rot(PS[2:4])
        nx_xo = _rot(PS[4:6])
        ptb = [(PS[i], PSt[i].bitcast(BF16)) for i in (6, 7)]
        nx_pt = _rot(ptb)
        P.op("dve", [], [zt], lambda e: e.memset(zt.t[:], 0.0))
        P.dma("sp", None, zt, H2d[:, :, CVS0:CVS0 + 1], zt.t[:, :, 0:1])
        P.dma("sp", None, zt, H2d[:, :, NCV - 1:NCV], zt.t[:, :, 1:2])
        blk_i = [0]

        QT3 = [(t * 512, 512, "p", t) for t in range(8)] + [(QL, 2, "h", 0)] + [(QS0 + t * 512, 512, "s", t) for t in range(4)]
        if _lim2:
            QT3 = [QT3[i] for i in (0, 1, 8, 12)]
        oil = {}

        def ens_oi(k):
            if k < len(QT3) and k not in oil:
                q0_, N_, _k, _t = QT3[k]
                OIb = nx_oin()
                P.dma("sp", OIb, None, OIb.t[:, 0:8, 0:N_], OAd[:, :, q0_:q0_ + N_].rearrange("h p q -> p h q"))
                P.dma("sp", OIb, None, OIb.t[:, 8:12, 0:N_], OBd[:, :, q0_:q0_ + N_].rearrange("h p q -> p h q"))
                oil[k] = OIb

        for qi_, (q0, N, kind, t) in enumerate(QT3):
            ens_oi(qi_)
            OI = oil.pop(qi_)
            ens_oi(qi_ + 1)
            for fo in range(16):
                if fo % 2 == 0:
                    WP = nx_wpc()
                    cg_ = fo // 2
                    P.dma("sp", WP, WB["pa"], WP.t[:, 0:8, :], wb_pa[:, :, cg_ * 256:(cg_ + 1) * 256])
                    P.dma("sp", WP, WB["pb"], WP.t[:, 8:12, :], wb_pb[:, :, cg_ * 256:(cg_ + 1) * 256])
                fl = fo % 2
                GI = nx_gin()
                P.dma("sp", GI, None, GI.t[:, 0, 0:N], GTd[fo, :, q0:q0 + N])
                P.dma("sp", GI, None, GI.t[:, 1, 0:N], GTd[16 + fo, :, q0:q0 + N])
                A = nx_pa()
                Bk = nx_pbk()

                def mma(e, A=A, fl=fl, OI=OI, WP=WP):
                    for kt in range(8):
                        ins = e.matmul(A.t[:, 0:N], WP.t[:, kt, fl * 128:(fl + 1) * 128], OI.t[:, kt, 0:N], start=(kt == 0), stop=(kt == 7))
                    return ins

                def mmb(e, Bk=Bk, fl=fl, OI=OI, WP=WP):
                    for kt in range(4):
                        ins = e.matmul(Bk.t[:, 0:N], WP.t[:, 8 + kt, fl * 128:(fl + 1) * 128], OI.t[:, 8 + kt, 0:N], start=(kt == 0), stop=(kt == 3))
                    return ins
                P.op("pe", [WP, OI], [A], mma)
                P.op("pe", [WP, OI], [Bk], mmb)
                T1, T2 = nx_t1(), nx_t2()
                P.op("dve", [A, GI], [T1], lambda e, T1=T1, A=A, GI=GI: e.tensor_tensor(T1.t[:, 0:N], A.t[:, 0:N], GI.t[:, 0, 0:N], ALU.mult))
                P.op("dve", [Bk, GI], [T2], lambda e, T2=T2, Bk=Bk, GI=GI: e.tensor_tensor(T2.t[:, 0:N], Bk.t[:, 0:N], GI.t[:, 1, 0:N], ALU.mult))
                P.op("pool", [T1, T2], [mT], lambda e, T1=T1, T2=T2, fo=fo: e.tensor_tensor(mT.t[:, fo, 0:N], T1.t[:, 0:N], T2.t[:, 0:N], ALU.add))
            nblk = (N + 127) // 128

            def x1_part(b):
                nb_ = min(128, N - b * 128)
                XB = nx_xb()
                if kind == "p":
                    r0 = OWN0 + t * 512 + b * 128
                    P.dma("sp", XB, None, XB.t[0:nb_, :], xp[r0:r0 + nb_, :])
                    cvs = [(1 + t * 512 + b * 128, 0, nb_)]
                elif kind == "s":
                    r0 = t * 512 + b * 128
                    P.dma("sp", XB, None, XB.t[0:nb_, :], xs[r0:r0 + nb_, :])
                    cvs = [(CVS0 + 1 + t * 512 + b * 128, 0, nb_)]
                else:
                    P.dma("sp", XB, None, XB.t[0:1, :], xp[OWN0 - 1:OWN0, :])
                    P.dma("sp", XB, None, XB.t[1:2, :], xp[OWN1:OWN1 + 1, :])
                    cvs = [(0, 0, 1), (4097, 1, 1)]
                X1 = nx_x1b()
                for cc in range(4):
                    XO = nx_xo()

                    def mmo(e, XO=XO, cc=cc, b=b, nb_=nb_):
                        for kt in range(16):
                            ins = e.matmul(XO.t[0:nb_, :], mT.t[:, kt, b * 128:b * 128 + nb_], wo[:, kt, cc * 512:(cc + 1) * 512], start=(kt == 0), stop=(kt == 15))
                        return ins
                    P.op("pe", [mT, W3], [XO], mmo)
                    P.op("dve", [XO, XB], [X1], lambda e, X1=X1, XO=XO, XB=XB, cc=cc, nb_=nb_: e.tensor_tensor(
                        X1.t[0:nb_, cc * 512:(cc + 1) * 512], XO.t[0:nb_, :], XB.t[0:nb_, cc * 512:(cc + 1) * 512], ALU.add))
                S3 = nx_s3()
                HS = nx_h2s()
                P.op("act", [X1], [S3, HS], lambda e, X1=X1, S3=S3, HS=HS, nb_=nb_: e.activation(HS.t[0:nb_, :], X1.t[0:nb_, :], AF.Square, accum_out=S3.t[0:nb_, 0:1]))
                P.op("act", [S3, CB], [S3], lambda e, S3=S3, nb_=nb_: e.activation(S3.t[0:nb_, 1:2], S3.t[0:nb_, 0:1], AF.Ln, bias=cst[0:nb_, 0:1], scale=1.0 / D))
                P.op("act", [S3], [S3], lambda e, S3=S3, nb_=nb_: e.activation(S3.t[0:nb_, 1:2], S3.t[0:nb_, 1:2], AF.Exp, scale=-0.5))
                if kind != "h":
                    P.dma("sp", None, X1, X1d[cvs[0][0]:cvs[0][0] + nb_, :], X1.t[0:nb_, :])
                P.op("dve", [X1, S3, W3], [HS], lambda e, HS=HS, X1=X1, S3=S3, nb_=nb_: e.scalar_tensor_tensor(
                    HS.t[0:nb_, :], X1.t[0:nb_, :], S3.t[0:nb_, 1:2], gft[0:nb_, :], ALU.mult, ALU.mult))
                return (HS, nb_, cvs)

            def tr_part(st_):
                HS, nb_, cvs = st_
                HT = nx_h2t()
                blk_i[0] += 1
                for kh in range(2):
                    pb, pvw = nx_pt()

                    def tr(e, HS=HS, pvw=pvw, kh=kh, nb_=nb_):
                        for k in range(8):
                            kt = kh * 8 + k
                            ins = e.transpose(pvw[:, k * 128:k * 128 + nb_], HS.t[0:nb_, kt * 128:(kt + 1) * 128], idt[0:nb_, 0:nb_])
                        return ins
                    P.op("pe", [HS, CB], [pb], tr)
                    src = pvw[:, :].rearrange("p (k t) -> p k t", k=8)[:, :, 0:nb_]
                    dst = HT.t[:, kh * 8:kh * 8 + 8, 0:nb_]
                    if kind == "h":
                        for k in range(8):
                            P.op("dve", [pb, CB], [HT], lambda e, k=k, kh=kh, pvw=pvw, HT=HT: e.tensor_tensor(
                                HT.t[:, kh * 8 + k, 0:2], pvw[:, k * 128:k * 128 + 2], pvt[:, PV_HM:PV_HM + 2], ALU.mult))
                    elif blk_i[0] % 2:
                        P.op("dve", [pb], [HT], lambda e, dst=dst, src=src: e.tensor_copy(dst, src))
                    else:
                        P.op("act", [pb], [HT], lambda e, dst=dst, src=src: e.activation(dst, src, AF.Copy))
                for (cv0, c0, n) in cvs:
                    P.dma("sp", None, HT, H2d[:, :, cv0:cv0 + n], HT.t[:, :, c0:c0 + n])

            prev = None
            for b in range(nblk):
                cur = x1_part(b)
                if prev is not None:
                    tr_part(prev)
                prev = cur
            tr_part(prev)
        P.barrier()
        P.release([W3, zt] + wpc + oin + gin + xb + x1b + h2t)

    if dbg == 4:
        return nc, es

    WIN = [(510 * k, min(512, 4098 - 510 * k), "p") for k in range(9)]
    WIN += [(CVS0 + 510 * k, min(512, 2050 - 510 * k), "s") for k in range(5)]
    if _lim2:
        WIN = [WIN[i] for i in (0, 1, 13)]
    with ExitStack() as ph:
        def psb(name, shape, dt):
            return ph.enter_context(nc.sbuf_tensor(name, list(shape), dt))
        hw = [Buf(psb("hw%d" % i, [128, 16, 512], BF16)) for i in range(2)]
        wu = [Buf(psb("wu%d" % i, [128, 2, 16, 128], BF16)) for i in range(3)]
        gT = Buf(psb("gT", [128, 44, 512], BF16))
        wd = [Buf(psb("wd%d" % i, [128, 44, 256], BF16)) for i in range(2)]
        cg = [Buf(psb("cg%d" % i, [128, 512], F32)) for i in range(2)]
        cv_ = [Buf(psb("cv%d" % i, [128, 512], F32)) for i in range(2)]
        ge = [Buf(psb("ge%d" % i, [128, 512], F32)) for i in range(2)]
        x1r = [Buf(psb("x1r%d" % i, [128, 4, 256], F32)) for i in range(2)]
        yb = [Buf(psb("yb%d" % i, [128, 256], F32)) for i in range(3)]
        nx_hw, nx_wu, nx_wd, nx_cg, nx_cv, nx_ge = _rot(hw), _rot(wu), _rot(wd), _rot(cg), _rot(cv_), _rot(ge)
        nx_x1r, nx_yb = _rot(x1r), _rot(yb)
        nx_ug, nx_uv, nx_dn = _rot(PS[0:2]), _rot(PS[2:4]), _rot(PS[4:8])

        def cw(j, ft):
            c = PV_CW + j * 88 + ft
            return pvt[:, c:c + 1]

        def cb(ft):
            c = PV_CB + ft
            return pvt[:, c:c + 1]

        hwl = {}

        def ens_hw(w):
            if w < len(WIN) and w not in hwl:
                c0_, n_, _k = WIN[w]
                b_ = nx_hw()
                P.dma("sp", b_, None, b_.t[:, :, 0:n_], H2d[:, :, c0_:c0_ + n_])
                hwl[w] = b_
        wul = {}

        def ens_wu(k):
            if k < 44 * len(WIN) and k not in wul:
                j_ = k % 44
                s_ = nx_wu()
                P.dma("sp", s_, WB["up"], s_.t[:, 0, :, :], wb_up[:, :, j_ * 128:(j_ + 1) * 128])
                P.dma("sp", s_, WB["up"], s_.t[:, 1, :, :], wb_up[:, :, DFF + j_ * 128:DFF + (j_ + 1) * 128])
                wul[k] = s_
        wdl = {}

        def ens_wd(k):
            if k < 8 * len(WIN) and k not in wdl:
                w_, c8_ = k // 8, k % 8
                c0_, n_, _k = WIN[w_]
                no_ = n_ - 2
                s_ = nx_wd()
                P.dma("sp", s_, WB["dn"], s_.t[:], wb_dn[:, :, c8_ * 256:(c8_ + 1) * 256])
                xr_ = nx_x1r()
                for b_ in range((no_ + 127) // 128):
                    nb2 = min(128, no_ - b_ * 128)
                    cvr_ = c0_ + 1 + b_ * 128
                    P.dma("sp", xr_, None, xr_.t[0:nb2, b_, :], X1d[cvr_:cvr_ + nb2, c8_ * 256:(c8_ + 1) * 256])
                wdl[k] = (s_, xr_)

        for wi, (c0, n, kind) in enumerate(WIN):
            no = n - 2
            ens_hw(wi)
            HW = hwl.pop(wi)
            ens_hw(wi + 1)
            for j in range(44):
                k = wi * 44 + j
                ens_wu(k)
                ens_wu(k + 1)
                ens_wu(k + 2)
                WU = wul.pop(k)
                UG, UV = nx_ug(), nx_uv()
                for half, U in ((0, UG), (1, UV)):
                    def mmu(e, U=U, half=half, WU=WU):
                        for kt in range(16):
                            ins = e.matmul(U.t[:, 0:n], WU.t[:, half, kt, :], HW.t[:, kt, 0:n], start=(kt == 0), stop=(kt == 15))
                        return ins
                    P.op("pe", [WU, HW], [U], mmu)
                CG, CV = nx_cg(), nx_cv()
                for U, C, ft in ((UG, CG, j), (UV, CV, 44 + j)):
                    P.op("dve", [U, CB], [C], lambda e, U=U, C=C, ft=ft: e.tensor_scalar(
                        C.t[:, 0:no], U.t[:, 0:no], cw(0, ft), cb(ft), ALU.mult, ALU.add))
                    P.op("dve", [U, C, CB], [C], lambda e, U=U, C=C, ft=ft: e.scalar_tensor_tensor(
                        C.t[:, 0:no], U.t[:, 1:no + 1], cw(1, ft), C.t[:, 0:no], ALU.mult, ALU.add))
                    P.op("dve", [U, C, CB], [C], lambda e, U=U, C=C, ft=ft: e.scalar_tensor_tensor(
                        C.t[:, 0:no], U.t[:, 2:no + 2], cw(2, ft), C.t[:, 0:no], ALU.mult, ALU.add))
                GE = nx_ge()
                P.op("act", [CG], [GE], lambda e, GE=GE, CG=CG: e.activation(GE.t[:, 0:no], CG.t[:, 0:no], AF.Gelu))
                P.op("pool", [GE, CV], [gT], lambda e, GE=GE, CV=CV, j=j: e.tensor_tensor(gT.t[:, j, 0:no], GE.t[:, 0:no], CV.t[:, 0:no], ALU.mult))
                if j == 40:
                    ens_wd(wi * 8)
            nblk = (no + 127) // 128
            for c8 in range(8):
                k = wi * 8 + c8
                ens_wd(k)
                WD, XR = wdl.pop(k)
                if c8 < 7:
                    ens_wd(k + 1)
                for b in range(nblk):
                    nb_ = min(128, no - b * 128)
                    DN = nx_dn()

                    def mmd(e, DN=DN, WD=WD, b=b, nb_=nb_):
                        for j in range(44):
                            ins = e.matmul(DN.t[0:nb_, 0:256], gT.t[:, j, b * 128:b * 128 + nb_], WD.t[:, j, :], start=(j == 0), stop=(j == 43))
                        return ins
                    P.op("pe", [gT, WD], [DN], mmd)
                    cvr = c0 + 1 + b * 128
                    YB = nx_yb()
                    P.op("dve", [DN, XR], [YB], lambda e, YB=YB, DN=DN, XR=XR, nb_=nb_, b=b: e.tensor_tensor(
                        YB.t[0:nb_, 0:256], DN.t[0:nb_, 0:256], XR.t[0:nb_, b, :], ALU.add))
                    if kind == "p":
                        dst = yp[cvr - 1:cvr - 1 + nb_, c8 * 256:(c8 + 1) * 256]
                    else:
                        dst = ys[cvr - CVS0 - 1:cvr - CVS0 - 1 + nb_, c8 * 256:(c8 + 1) * 256]
                    P.dma("sp", None, YB, dst, YB.t[0:nb_, 0:256])
        P.barrier()
    return nc, es


def _tables(c):
    r = c % 4
    own_lo, own_hi = 4096 * r, 4096 * (r + 1)
    u = np.arange(SEQ)
    pk = (own_lo - OWN0 + u) % SEQ
    is_own = (u >= OWN0) & (u < OWN1)
    sig = np.where(is_own, 1.0, np.where(pk < own_lo, 1.0, -1.0))
    pk_all = np.concatenate([pk, np.arange(SSEQ)]).astype(np.float64)
    sig_all = np.concatenate([sig, np.ones(SSEQ)])
    jlo = pk_all % 128
    jhi = pk_all - jlo
    kaug = np.zeros((8, 4, NCTX), np.float32)
    for h in range(8):
        s = SLOPE_A[h]
        kaug[h, 0] = -sig_all * s
        kaug[h, 1] = -sig_all * s
        kaug[h, 2] = sig_all * s * jlo
        kaug[h, 3] = sig_all * s * jhi
    qpos = np.zeros(NQ, np.float64)
    qpos[0:4096] = own_lo + np.arange(4096)
    qpos[QL] = own_lo - 1
    qpos[QR] = own_hi
    qpos[QS0:] = np.arange(SSEQ)
    ilo = qpos % 128
    ihi = qpos - ilo
    qaugp = np.stack([ilo, ihi, np.ones(NQ), np.ones(NQ)]).astype(np.float32)
    qaugm = -qaugp
    qaugm[:, QR] = qaugp[:, QR]
    valb = np.zeros((128, 72), np.float32)
    ub = np.arange(BWIN)
    pb = own_lo - OWN0 + ub
    bad = (pb < 0) | (pb >= SEQ)
    valb[:, :56] = np.where(bad, -30000.0, 0.0).reshape(56, 128).T
    hm = np.array([1.0 if r > 0 else 0.0, 1.0 if r < 3 else 0.0], np.float32)
    return kaug, qaugp, qaugm, valb, hm


def _shared_tables():
    k = np.arange(128)[:, None]
    absa = np.abs(k - np.arange(896)[None, :] + 384).astype(np.float32)
    dbt = np.zeros((128, DB_TOT), np.float32)
    for g in range(3):
        cc = np.arange(DB_W[g])[None, :]
        dl = k - cc + DB_C0[g]
        ok = (np.abs(dl) <= 64 * DIL[g]) & (dl % DIL[g] == 0)
        dbt[:, DB_OFF[g]:DB_OFF[g] + DB_W[g]] = np.where(ok, np.abs(dl), BIGD)
    return absa, dbt, np.eye(128, dtype=np.float32)


def make_in_maps(inp):
    f = lambda a: np.ascontiguousarray(a, dtype=np.float32)
    absa, dbt, ident = _shared_tables()
    w_in, w_pa, w_pb, w_o = f(inp["w_in"][0]), f(inp["w_pa"][0]), f(inp["w_pb"][0]), f(inp["w_o"][0])
    w_up, w_dn = f(inp["w_up"][0]), f(inp["w_down"][0])
    pvb = np.zeros((128, NPV), np.float32)
    gmixb = np.ascontiguousarray(np.broadcast_to(inp["g_mix_norm"][0][None, :], (128, D)), dtype=np.float32)
    gffnb = np.ascontiguousarray(np.broadcast_to(inp["g_ffn_norm"][0][None, :], (128, D)), dtype=np.float32)
    pvb[:, PV_GQA] = np.tile(inp["g_qa"][0], 2)
    pvb[:, PV_GKA] = np.tile(inp["g_ka"][0], 2)
    pvb[:, PV_GQB] = inp["g_qb"][0]
    pvb[:, PV_GKB] = inp["g_kb"][0]
    pvb[:, PV_GSUB] = inp["g_subln"][0]
    cwv = inp["conv_w"][0]
    for j in range(3):
        pvb[:, PV_CW + j * 88:PV_CW + (j + 1) * 88] = cwv[j].reshape(88, 128).T
    pvb[:, PV_CB:PV_CB + 88] = inp["conv_b"][0].reshape(88, 128).T
    for i, kname in enumerate(("lam_q1", "lam_k1", "lam_q2", "lam_k2")):
        pvb[:, PV_LAM + 64 * i:PV_LAM + 64 * (i + 1)] = inp[kname][0][None, :]
    maps = []
    for c in range(8):
        bp, r, bs = c // 4, c % 4, c // 2
        kaug, qaugp, qaugm, valb, hm = _tables(c)
        pvc = pvb.copy()
        pvc[:, PV_HM:PV_HM + 2] = hm[None, :]
        xpl = np.roll(inp["x_prompt"][bp], -(4096 * r - OWN0), axis=0)
        maps.append({
            "xp": f(xpl), "xs": f(inp["x_sample"][bs]),
            "w_in": w_in, "w_pa": w_pa, "w_pb": w_pb, "w_o": w_o, "w_up": w_up, "w_dn": w_dn,
            "pv": pvc, "gmixb": gmixb, "gffnb": gffnb, "kaug": kaug, "qaugp": qaugp, "qaugm": qaugm,
            "absa": absa, "dbt": dbt, "valb": valb, "ident": ident,
        })
    return maps


def kernel(**inp):
    nc, es = build()
    maps = make_in_maps(inp)
    res = run_bass_kernel_spmd(nc, maps, core_ids=list(range(8)))
    es.close()
    yp = np.zeros((2, SEQ, D), np.float32)
    ys = np.zeros((4, SSEQ, D), np.float32)
    for c in range(8):
        bp, r, bs, hs = c // 4, c % 4, c // 2, c % 2
        yp[bp, 4096 * r:4096 * (r + 1)] = res.results[c]["yp"]
        ys[bs, 1024 * hs:1024 * (hs + 1)] = res.results[c]["ys"][1024 * hs:1024 * (hs + 1)]
    return yp, ys
```

```python
import numpy as np
from contextlib import ExitStack
import concourse.bass as bass
import concourse.mybir as mybir
from concourse.bass_utils import run_bass_kernel_spmd

F32 = mybir.dt.float32
BF16 = mybir.dt.bfloat16
AF = mybir.ActivationFunctionType
ALU = mybir.AluOpType
AX = mybir.AxisListType

D = 2048
SEQ = 16384
SSEQ = 2048
N_IN = 11776
DFF = 5632
NCTX = SEQ + SSEQ
OWN0 = 1536
OWN1 = OWN0 + 4096
BWIN = 7168
NB = BWIN + SSEQ
NQ = 4096 + 2 + SSEQ
QL, QR, QS0 = 4096, 4097, 4098
NCV = 4098 + 2050
CVS0 = 4098
SLOPE_A = [2.0 ** (-(h + 1)) for h in range(8)]
SLOPE_B = [2.0 ** (-8.0 * (i + 1) / 12.0) for i in range(12)]
DIL = [1, 4, 16]
BIGD = 1.0e7
ALIBI_CUT = 94.4
DB_C0 = [64 * d + 512 for d in DIL]
DB_W = [DB_C0[g] + 64 * DIL[g] + 640 for g in range(3)]
DB_OFF = [0, DB_W[0], DB_W[0] + DB_W[1]]
DB_TOT = sum(DB_W)

PV_GQA = 0
PV_GKA = 1
PV_GQB = 2
PV_GKB = 3
PV_GSUB = 4
PV_CW = 5
PV_CB = PV_CW + 264
PV_LAM = PV_CB + 88
PV_HM = PV_LAM + 256
NPV = PV_HM + 2

ENGS = ("pe", "act", "dve", "pool", "sp")


class Buf:
    __slots__ = ("t", "w", "r", "ds", "name")

    def __init__(self, t, name=""):
        self.t = t
        self.w = None
        self.r = {}
        self.ds = None
        self.name = name


class Prog:
    def __init__(self, nc, es):
        self.nc = nc
        self.es = es
        self.E = {"pe": nc.tensor, "act": nc.scalar, "dve": nc.vector,
                  "pool": nc.gpsimd, "sp": nc.sync}
        self.sem = {}
        self.cnt = {}
        self.waited = {e: {} for e in ENGS}
        for e in ("pe", "act", "dve", "pool"):
            self._mk("E_" + e)
        self.free_ds = {"sw": [], "hw": []}
        self.nds = 0

    def _mk(self, name):
        self.sem[name] = self.es.enter_context(self.nc.semaphore(name))
        self.cnt[name] = 0

    def get_ds(self, kind):
        if self.free_ds[kind]:
            return self.free_ds[kind].pop()
        name = "D%s%d" % (kind, self.nds)
        self.nds += 1
        self._mk(name)
        return name

    def release(self, bufs):
        for b in bufs:
            if b.ds is not None:
                for kind, nm in b.ds.items():
                    self.free_ds[kind].append(nm)
                b.ds = None

    def wait(self, eng, tok):
        if tok is None:
            return
        s, v = tok
        if eng == "pe" and s == "E_pe":
            return
        if self.waited[eng].get(s, 0) >= v:
            return
        self.waited[eng][s] = v
        self.E[eng].wait_ge(self.sem[s], v)

    def _deps(self, eng, reads, writes):
        for b in reads:
            self.wait(eng, b.w)
        for b in writes:
            self.wait(eng, b.w)
            for t in b.r.values():
                self.wait(eng, t)

    def op(self, eng, reads, writes, fn):
        self._deps(eng, reads, writes)
        ins = fn(self.E[eng])
        s = "E_" + eng
        self.cnt[s] += 1
        ins.then_inc(self.sem[s], 1)
        tok = (s, self.cnt[s])
        for b in reads:
            b.r[eng] = tok
        for b in writes:
            b.w = tok
            b.r = {}
        return tok

    def dma(self, q, out_b, in_b, out_ap, in_ap, track=None):
        reads = [in_b] if in_b is not None else []
        writes = [out_b] if out_b is not None else []
        self._deps(q, reads, writes)
        tb = track if track is not None else (out_b if out_b is not None else in_b)
        if tb.ds is None:
            tb.ds = {}
        kind = "sw" if q == "pool" else "hw"
        if kind not in tb.ds:
            tb.ds[kind] = self.get_ds(kind)
        s = tb.ds[kind]
        self.cnt[s] += 16
        try:
            ins = self.E[q].dma_start(out=out_ap, in_=in_ap)
        except ValueError:
            ins = self.E[q].dma_start(out=out_ap, in_=in_ap, allow_slow_non_contiguous=True)
        ins.then_inc(self.sem[s], 16)
        tok = (s, self.cnt[s])
        if in_b is not None:
            in_b.r["dma_" + s] = tok
        if out_b is not None:
            out_b.w = tok
            out_b.r = {}
        return tok

    def barrier(self):
        for e in ENGS:
            for s, c in self.cnt.items():
                if c > 0:
                    self.wait(e, (s, c))


def _rot(lst):
    i = [0]

    def nxt():
        b = lst[i[0] % len(lst)]
        i[0] += 1
        return b
    return nxt


def build(dbg=False):
    nc = bass.Bass("TRN2", target_bir_lowering=False)
    es = ExitStack()
    P = Prog(nc, es)
    es.enter_context(nc.allow_low_precision(reason="bf16 matmul operands by design, fp32 accumulation"))

    def din(name, shape):
        return nc.dram_tensor(name, list(shape), F32, kind="ExternalInput").ap()

    def dscr(name, shape, dt):
        if dbg:
            return nc.dram_tensor(name, list(shape), dt, kind="ExternalOutput").ap()
        return nc.dram_tensor(name, list(shape), dt).ap()

    xp = din("xp", [SEQ, D])
    xs = din("xs", [SSEQ, D])
    w_in = din("w_in", [D, N_IN])
    w_pa = din("w_pa", [1024, D])
    w_pb = din("w_pb", [512, D])
    w_o = din("w_o", [D, D])
    w_up = din("w_up", [D, 2 * DFF])
    w_dn = din("w_dn", [DFF, D])
    pv = din("pv", [128, NPV])
    gmixb = din("gmixb", [128, D])
    gffnb = din("gffnb", [128, D])
    kaug = din("kaug", [8, 4, NCTX])
    qaugp = din("qaugp", [4, NQ])
    qaugm = din("qaugm", [4, NQ])
    absa = din("absa", [128, 896])
    dbt = din("dbt", [128, DB_TOT])
    valb = din("valb", [128, 72])
    ident = din("ident", [128, 128])
    yp = nc.dram_tensor("yp", [4096, D], F32, kind="ExternalOutput").ap()
    ys = nc.dram_tensor("ys", [SSEQ, D], F32, kind="ExternalOutput").ap()

    wb_in = nc.dram_tensor("wb_in", [128, 16, N_IN], BF16).ap()
    wb_pa = nc.dram_tensor("wb_pa", [128, 8, D], BF16).ap()
    wb_pb = nc.dram_tensor("wb_pb", [128, 4, D], BF16).ap()
    wb_o = nc.dram_tensor("wb_o", [128, 16, D], BF16).ap()
    wb_up = nc.dram_tensor("wb_up", [128, 16, 2 * DFF], BF16).ap()
    wb_dn = nc.dram_tensor("wb_dn", [128, 44, D], BF16).ap()
    kaug_b = nc.dram_tensor("kaug_b", [8, 4, NCTX], BF16).ap()
    qaugp_b = nc.dram_tensor("qaugp_b", [4, NQ], BF16).ap()
    qaugm_b = nc.dram_tensor("qaugm_b", [4, NQ], BF16).ap()
    QAd = dscr("QAd", [8, 128, NQ], BF16)
    KAd = dscr("KAd", [8, 128, NCTX], BF16)
    VAd = dscr("VAd", [8, NCTX, 128], BF16)
    QBd = dscr("QBd", [12, 128, NQ], BF16)
    KBd = dscr("KBd", [12, 128, NB], BF16)
    VBd = dscr("VBd", [12, NB, 128], BF16)
    GTd = dscr("GTd", [32, 128, NQ], BF16)
    OAd = dscr("OAd", [8, 128, NQ], BF16)
    OBd = dscr("OBd", [4, 128, NQ], BF16)
    X1d = dscr("X1d", [NCV, D], F32)
    H2d = dscr("H2d", [128, 16, NCV], BF16)

    def sb(name, shape, dt):
        return es.enter_context(nc.sbuf_tensor(name, list(shape), dt))

    PDt = [es.enter_context(nc.psum_tensor("pd%d" % i, [128, 1024], F32)) for i in range(4)]
    PSt = [PDt[i // 2][:, (i % 2) * 512:(i % 2 + 1) * 512] for i in range(8)]
    PS = [Buf(t, "ps%d" % i) for i, t in enumerate(PSt)]

    pvt = sb("pvt", [128, NPV], F32)
    cst = sb("cst", [128, 16], F32)
    idt = sb("idt", [128, 128], BF16)
    idf = sb("idf", [128, 128], F32)
    ones = sb("ones", [128, 128], BF16)
    blk = sb("blk", [128, 128], BF16)
    lamt = sb("lamt", [128, 64], F32)
    lams = sb("lams", [128, 4], F32)
    CB = Buf(None, "consts")

    WB = {}

    def cast(name, dst, src, ktn):
        b = Buf(None, name)
        for kt in range(ktn):
            P.dma("pool", b, None, dst[:, kt, :], src[kt * 128:(kt + 1) * 128, :])
        WB[name] = b

    TB = Buf(None, "tables")
    P.dma("pool", CB, None, idt[:], ident[:, :])
    P.dma("pool", TB, None, kaug_b[:, :, :], kaug[:, :, :])
    P.dma("pool", TB, None, qaugp_b[:, :], qaugp[:, :])
    P.dma("pool", TB, None, qaugm_b[:, :], qaugm[:, :])
    bkv = Buf(None, "in_kv")
    for kt in range(16):
        P.dma("pool", bkv, None, wb_in[:, kt, 1024:3072], w_in[kt * 128:(kt + 1) * 128, 1024:3072])
    WB["in_kv"] = bkv
    brest = Buf(None, "in")
    for kt in range(16):
        P.dma("pool", brest, None, wb_in[:, kt, 0:1024], w_in[kt * 128:(kt + 1) * 128, 0:1024])
        P.dma("pool", brest, None, wb_in[:, kt, 3072:N_IN], w_in[kt * 128:(kt + 1) * 128, 3072:N_IN])
    WB["in"] = brest
    cast("pa", wb_pa, w_pa, 8)
    cast("pb", wb_pb, w_pb, 4)
    cast("o", wb_o, w_o, 16)
    cast("up", wb_up, w_up, 16)
    cast("dn", wb_dn, w_dn, 44)

    P.dma("sp", CB, None, pvt[:], pv[:, :])
    P.dma("sp", CB, None, idf[:], ident[:, :])

    def c_op(eng, fn):
        P.op(eng, [CB], [CB], fn)

    c_op("dve", lambda e: e.memset(cst[:, 0:1], 1e-6))
    c_op("dve", lambda e: e.memset(cst[:, 1:2], 1e-5))
    c_op("dve", lambda e: e.memset(cst[:, 8:9], 0.0))
    c_op("dve", lambda e: e.memset(ones[:], 1.0))
    c_op("dve", lambda e: e.memset(blk[:], 0.0))
    c_op("dve", lambda e: e.memset(blk[0:64, 0:64], 1.0))
    c_op("dve", lambda e: e.memset(blk[64:128, 64:128], 1.0))
    c_op("dve", lambda e: e.tensor_scalar(cst[:, 3:4], pvt[:, PV_GQA:PV_GQA + 1], 0.125, None, ALU.mult))
    c_op("dve", lambda e: e.tensor_copy(cst[:, 4:5], pvt[:, PV_GKA:PV_GKA + 1]))
    c_op("dve", lambda e: e.tensor_scalar(cst[:, 5:6], pvt[:, PV_GQB:PV_GQB + 1], 128.0 ** -0.5, None, ALU.mult))
    c_op("dve", lambda e: e.tensor_copy(cst[:, 6:7], pvt[:, PV_GKB:PV_GKB + 1]))
    c_op("dve", lambda e: e.tensor_scalar(cst[:, 7:8], pvt[:, PV_GSUB:PV_GSUB + 1], 0.8, None, ALU.mult))
    for j in range(2):
        a0 = PV_LAM + 128 * j
        c_op("dve", lambda e, a0=a0: e.tensor_tensor(lamt[:], pvt[:, a0:a0 + 64], pvt[:, a0 + 64:a0 + 128], ALU.mult))
        c_op("dve", lambda e, j=j: e.tensor_reduce(lams[:, j:j + 1], lamt[:], AX.X, ALU.add))
        c_op("act", lambda e, j=j: e.activation(lams[:, 2 + j:3 + j], lams[:, j:j + 1], AF.Exp))
    c_op("dve", lambda e: e.tensor_tensor(lams[:, 0:1], lams[:, 3:4], lams[:, 2:3], ALU.subtract))
    c_op("dve", lambda e: e.tensor_scalar(cst[:, 2:3], lams[:, 0:1], -0.2, None, ALU.add))
    EPS6 = cst[:, 0:1]
    EPS5 = cst[:, 1:2]
    NEGLAM = cst[:, 2:3]

    def norm_tail(N, ps_z, ps_bs, sqb, lnb, onesmat, inv_n, eps_ap, g_ap, out_b, z_sbuf=None):
        zb = z_sbuf if z_sbuf is not None else ps_z
        P.op("act", [zb], [sqb], lambda e: e.activation(sqb.t[:, 0:N], zb.t[:, 0:N], AF.Square))
        P.op("pe", [sqb, CB], [ps_bs], lambda e: e.matmul(ps_bs.t[:, 0:N], onesmat, sqb.t[:, 0:N], start=True, stop=True))
        P.op("act", [ps_bs, CB], [lnb], lambda e: e.activation(lnb.t[:, 0:N], ps_bs.t[:, 0:N], AF.Ln, bias=eps_ap, scale=inv_n))
        P.op("act", [lnb], [lnb], lambda e: e.activation(lnb.t[:, 0:N], lnb.t[:, 0:N], AF.Exp, scale=-0.5))
        P.op("dve", [zb, lnb, CB], [out_b], lambda e: e.scalar_tensor_tensor(
            out_b.t[:, 0:N], zb.t[:, 0:N], g_ap, lnb.t[:, 0:N], ALU.mult, ALU.mult))

    with ExitStack() as ph:
        def psb(name, shape, dt):
            return ph.enter_context(nc.sbuf_tensor(name, list(shape), dt))
        xt = [Buf(psb("xt%d" % i, [128, D], F32)) for i in range(3)]
        junk = psb("junk", [128, D], BF16)
        ssb = [Buf(psb("ssb%d" % i, [128, 8], F32)) for i in range(2)]
        xsb = [Buf(psb("xsb%d" % i, [128, D], BF16)) for i in range(2)]
        hTt = [psb("hT%d" % i, [128, 16, 512], BF16) for i in range(2)]
        hT = [[Buf(t) for _ in range(2)] for t in hTt]
        gmt = psb("gmt", [128, D], F32)
        GM = Buf(None)
        P.dma("sp", GM, None, gmt[:], gmixb[:, :])
        wch = [Buf(psb("wch%d" % i, [128, 16, 512], BF16)) for i in range(4)]
        zst = [Buf(psb("zst%d" % i, [128, 512], BF16)) for i in range(4)]
        vst = [Buf(psb("vst%d" % i, [128, 512], BF16)) for i in range(3)]
        sqb = [Buf(psb("sqb%d" % i, [128, 512], BF16)) for i in range(2)]
        lnb = [Buf(psb("lnb%d" % i, [128, 512], F32)) for i in range(2)]
        gex = [Buf(psb("gex%d" % i, [128, 512], F32)) for i in range(2)]
        nx_xt, nx_ss, nx_xs, nx_hT = _rot(xt), _rot(ssb), _rot(xsb), _rot(hT)
        nx_zst, nx_vst, nx_sq, nx_ln, nx_gex = _rot(zst), _rot(vst), _rot(sqb), _rot(lnb), _rot(gex)
        ptb = [(PS[i], PSt[i].bitcast(BF16)) for i in range(2)]
        nx_pt = _rot(ptb)
        nx_z = _rot(PS[2:6])
        nx_bs = _rot(PS[6:8])
        evac_i = [0]
        wslot = _rot(wch)

        def load_chunk(ci, slot):
            P.dma("sp", slot, WB["in_kv" if ci in (2, 3, 4, 5) else "in"], slot.t[:], wb_in[:, :, ci * 512:(ci + 1) * 512])

        import os
        _lim = os.environ.get("KDBG", "")

        def prologue(xsrc, r0):
            Hs = nx_hT()
            Ht = Hs[0].t
            ss = nx_ss()
            for b in range(4):
                X = nx_xt()
                P.dma("sp", X, None, X.t[:], xsrc[r0 + b * 128:r0 + (b + 1) * 128, :])
                P.op("act", [X], [ss], lambda e, X=X, b=b: e.activation(
                    junk[:], X.t[:], AF.Square, accum_out=ss.t[:, b:b + 1]))
                P.op("act", [ss, CB], [ss], lambda e, b=b: e.activation(ss.t[:, 4 + b:5 + b], ss.t[:, b:b + 1], AF.Ln, bias=EPS6, scale=1.0 / D))
                P.op("act", [ss], [ss], lambda e, b=b: e.activation(ss.t[:, 4 + b:5 + b], ss.t[:, 4 + b:5 + b], AF.Exp, scale=-0.5))
                S = nx_xs()
                P.op("dve", [X, ss, GM], [S], lambda e, X=X, S=S, b=b: e.scalar_tensor_tensor(
                    S.t[:], X.t[:], ss.t[:, 4 + b:5 + b], gmt[:], ALU.mult, ALU.mult))
                for kh in range(2):
                    pb, pvw = nx_pt()

                    def tr(e, S=S, pvw=pvw, kh=kh):
                        for k in range(8):
                            kt = kh * 8 + k
                            ins = e.transpose(pvw[:, k * 128:(k + 1) * 128], S.t[:, kt * 128:(kt + 1) * 128], idt[:])
                        return ins
                    P.op("pe", [S, CB], [pb], tr)
                    src = pvw[:, :].rearrange("p (k t) -> p k t", k=8)
                    dst = Ht[:, kh * 8:kh * 8 + 8, b * 128:(b + 1) * 128]
                    if kh % 2:
                        P.op("dve", [pb], [Hs[kh]], lambda e, dst=dst, src=src: e.tensor_copy(dst, src))
                    else:
                        P.op("act", [pb], [Hs[kh]], lambda e, dst=dst, src=src: e.activation(dst, src, AF.Copy))
            return Hs

        stream = []
        loaded = {}
        spos = [0]

        last_res = [-1]

        def ensure(k):
            if k < len(stream) and k not in loaded:
                cj, resident = stream[k]
                if cj in resident:
                    loaded[k] = resident[cj]
                else:
                    if spos[0] <= last_res[0] + 1:
                        return
                    sl = wslot()
                    load_chunk(cj, sl)
                    loaded[k] = sl

        def project(Hs, ctx0, b0, q0, qc, chunks, hoist=None, hoist_at=0):
            Ht = Hs[0].t
            for j, ci in enumerate(chunks):
                k = spos[0]
                spos[0] += 1
                ensure(k)
                ensure(k + 1)
                ensure(k + 2)
                W = loaded.pop(k)
                if hoist is not None and j == hoist_at:
                    hoist()
                if ci in (4, 5, 12, 13, 14):
                    for b in range(4):
                        Z = nx_z()

                        def mm(e, W=W, Z=Z, b=b):
                            for kt in range(16):
                                ins = e.matmul(Z.t[:, :], Ht[:, kt, b * 128:(b + 1) * 128], W.t[:, kt, :], start=(kt == 0), stop=(kt == 15))
                            return ins
                        P.op("pe", Hs + [W], [Z], mm)
                        V = nx_vst()
                        evac_i[0] += 1
                        if evac_i[0] % 2:
                            P.op("dve", [Z], [V], lambda e, V=V, Z=Z: e.tensor_copy(V.t[:], Z.t[:]))
                        else:
                            P.op("act", [Z], [V], lambda e, V=V, Z=Z: e.activation(V.t[:], Z.t[:], AF.Copy))
                        if ci in (4, 5):
                            h0 = 4 * (ci - 4)
                            dst = VAd[h0:h0 + 4, ctx0 + b * 128:ctx0 + (b + 1) * 128, :].rearrange("h t e -> t h e")
                        else:
                            h0 = 4 * (ci - 12)
                            dst = VBd[h0:h0 + 4, b0 + b * 128:b0 + (b + 1) * 128, :].rearrange("h t e -> t h e")
                        P.dma("sp", None, V, dst, V.t[:].rearrange("p (h e) -> p h e", h=4))
                    continue
                for f in range(4):
                    ft = ci * 4 + f
                    isq = ft < 8 or 24 <= ft < 36 or ft >= 60
                    c0, c1 = qc if isq else (0, 512)
                    N = c1 - c0
                    Z = nx_z()

                    def mm(e, W=W, Z=Z, f=f, c0=c0, c1=c1, N=N):
                        for kt in range(16):
                            ins = e.matmul(Z.t[:, 0:N], W.t[:, kt, f * 128:(f + 1) * 128], Ht[:, kt, c0:c1], start=(kt == 0), stop=(kt == 15))
                        return ins
                    P.op("pe", Hs + [W], [Z], mm)
                    O = nx_zst()
                    if ft >= 60:
                        G = nx_gex()
                        P.op("act", [Z], [G], lambda e, G=G, Z=Z, N=N: e.activation(G.t[:, 0:N], Z.t[:, 0:N], AF.Exp, scale=-1.0))
                        P.op("dve", [G], [G], lambda e, G=G, N=N: e.tensor_scalar(G.t[:, 0:N], G.t[:, 0:N], 1.0, None, ALU.add))
                        P.op("dve", [G], [O], lambda e, G=G, O=O, N=N: e.reciprocal(O.t[:, 0:N], G.t[:, 0:N]))
                        dst = GTd[ft - 60, :, q0:q0 + N]
                    else:
                        if ft < 8:
                            om, inv, g, dst = blk[:], 1.0 / 64, cst[:, 3:4], QAd[ft, :, q0:q0 + N]
                        elif ft < 16:
                            om, inv, g, dst = blk[:], 1.0 / 64, cst[:, 4:5], KAd[ft - 8, :, ctx0:ctx0 + 512]
                        elif ft < 36:
                            om, inv, g, dst = ones[:], 1.0 / 128, cst[:, 5:6], QBd[ft - 24, :, q0:q0 + N]
                        else:
                            om, inv, g, dst = ones[:], 1.0 / 128, cst[:, 6:7], KBd[ft - 36, :, b0:b0 + 512]
                        norm_tail(N, Z, nx_bs(), nx_sq(), nx_ln(), om, inv, EPS6, g, O)
                    P.dma("sp", None, O, dst, O.t[:, 0:N])

        KV = [2, 3, 4, 5]
        KVB = [2, 3, 4, 5, 9, 10, 11, 12, 13, 14]
        ALLC = list(range(23))
        res = {}
        for ci in KV:
            sl = wslot()
            load_chunk(ci, sl)
            res[ci] = sl
        TD = []
        for T in range(14, 32):
            TD.append((xp, T * 512, T * 512, None, None, None, KV, res))
        for T in range(0, 14):
            if 3 <= T <= 10:
                TD.append((xp, T * 512, T * 512, T * 512, (T - 3) * 512, (0, 512), ALLC, {}))
            elif T == 2:
                TD.append((xp, T * 512, T * 512, T * 512, QL, (511, 512), ALLC, {}))
            elif T == 11:
                TD.append((xp, T * 512, T * 512, T * 512, QR, (0, 1), ALLC, {}))
            else:
                TD.append((xp, T * 512, T * 512, T * 512, None, None, KVB, {}))
        for T in range(4):
            TD.append((xs, T * 512, SEQ + T * 512, BWIN + T * 512, QS0 + T * 512, (0, 512), ALLC, {}))
        for d in TD:
            for ci in d[6]:
                if ci in d[7]:
                    last_res[0] = len(stream)
                stream.append((ci, d[7]))
        nextH = [prologue(TD[0][0], TD[0][1])]
        for i, d in enumerate(TD):
            Hcur = nextH[0]

            def hoist(i=i):
                if i + 1 < len(TD):
                    nextH[0] = prologue(TD[i + 1][0], TD[i + 1][1])
            project(Hcur, d[2], d[3], d[4], d[5], d[6], hoist=hoist, hoist_at=min(1, len(d[6]) - 1))
        P.barrier()
        P.release(xt + wch + zst + vst + [GM])

    if dbg == 1:
        return nc, es

    import os
    _lim2 = os.environ.get("KDBG2", "")
    QT = [("p", t, t * 512, 512) for t in range(8)] + [("h", 0, QL, 2)] + [("s", t, QS0 + t * 512, 512) for t in range(4)]
    with ExitStack() as ph:
        def psb(name, shape, dt):
            return ph.enter_context(nc.sbuf_tensor(name, list(shape), dt))
        KT = [Buf(psb("KT%d" % m, [68, NCTX], BF16)) for m in range(2)]
        VT = Buf(psb("VT", [128, 144, 128], BF16))
        QSb = [Buf(psb("QS%d" % i, [68, 6, 512], BF16)) for i in range(2)]
        PT = [Buf(psb("PT%d" % i, [128, 2, 512], BF16)) for i in range(6)]
        PSm = [Buf(psb("PSm%d" % i, [128, 2, 512], BF16)) for i in range(3)]
        nx_PSm = _rot(PSm)
        SBf = [Buf(psb("SBf%d" % i, [128, 2, 512], F32)) for i in range(2)]
        absat = psb("absat", [128, 896], F32)
        lr = [Buf(psb("lr%d" % i, [128, 512], F32)) for i in range(2)]
        o12 = [Buf(psb("o12%d" % i, [128, 512], F32)) for i in range(2)]
        ob = Buf(psb("ob", [128, 512], F32))
        sq2 = Buf(psb("sq2", [128, 512], BF16))
        ln2 = Buf(psb("ln2", [128, 512], F32))
        oast = [Buf(psb("oast%d" % i, [128, 512], BF16)) for i in range(2)]
        nx_QS, nx_PT, nx_SB, nx_oast = _rot(QSb), _rot(PT), _rot(SBf), _rot(oast)
        nx_sc = _rot(PS[0:4])
        nx_scp = _rot([(PS[0], PS[1], PDt[0]), (PS[2], PS[3], PDt[1])])
        OB_, LB_ = PS[4:6], PS[6:8]
        AT = Buf(None)
        P.dma("sp", AT, None, absat[:], absa[:, :])
        for q in QSb:
            P.op("dve", [], [q], lambda e, q=q: e.memset(q.t[64:68, 4:6, :], 0.0))

        QTL = [q for i, q in enumerate(QT) if not (_lim2 and i not in (0, 1, 8, 12))]
        KTc = [[Buf(KT[m].t) for _ in range(4)] for m in range(2)]
        VTc = [Buf(VT.t) for _ in range(4)]

        def load_kv(hh, cs):
            for c4 in cs:
                k0, k1 = c4 * 4608, (c4 + 1) * 4608
                for m in range(2):
                    P.dma("sp", KTc[m][c4], None, KT[m].t[0:64, k0:k1], KAd[hh, m * 64:(m + 1) * 64, k0:k1])
                    P.dma("sp", KTc[m][c4], TB, KT[m].t[64:68, k0:k1], kaug_b[hh, :, k0:k1])
                P.dma("sp", VTc[c4], None, VT.t[:, c4 * 36:(c4 + 1) * 36, :],
                      VAd[hh, k0:k1, :].rearrange("(kt p) e -> p kt e", p=128))
        qitems = [(hh, qq) for hh in range(8) for qq in QTL]
        qld = {}

        def ens_q(k):
            if k < len(qitems) and k not in qld:
                hh, (kind_, t_, q0_, N_) = qitems[k]
                Qb = nx_QS()
                qsrc = QAd[hh, :, q0_:q0_ + N_].rearrange("(m d) q -> d m q", m=2)
                for v in range(3):
                    P.dma("sp", Qb, None, Qb.t[0:64, 2 * v:2 * v + 2, 0:N_], qsrc)
                for m in range(2):
                    P.dma("sp", Qb, TB, Qb.t[64:68, m, 0:N_], qaugp_b[:, q0_:q0_ + N_])
                    P.dma("sp", Qb, TB, Qb.t[64:68, 2 + m, 0:N_], qaugm_b[:, q0_:q0_ + N_])
                qld[k] = Qb
        load_kv(0, [0, 1, 2, 3])
        for h in range(8):
            for qi_, (kind, t, q0, N) in enumerate(QTL):
                kq = h * len(QTL) + qi_
                ens_q(kq)
                Q = qld.pop(kq)
                ens_q(kq + 1)
                if kind == "s" and t == (3 if _lim2 else 0) and h + 1 < 8:
                    load_kv(h + 1, [0, 1, 2])
                if kind == "s":
                    kts = list(range(128, 144))
                else:
                    kts = list(range(0, 128))
                tiles = []
                for kt in kts:
                    if ALIBI_CUT is not None:
                        if kind == "s":
                            klo, qlo, qhi, per = (kt - 128) * 128, t * 512, t * 512 + 511, None
                            qs_ = [(qlo, qhi)]
                        elif kind == "p":
                            klo, per = kt * 128, SEQ
                            qs_ = [(OWN0 + t * 512, OWN0 + t * 512 + 511)]
                        else:
                            klo, per = kt * 128, SEQ
                            qs_ = [(OWN0 - 1, OWN0 - 1), (OWN1, OWN1)]
                        khi = klo + 127
                        gap = None
                        for (qlo, qhi) in qs_:
                            for sh in ((0,) if per is None else (0, per, -per)):
                                g_ = max(0, klo + sh - qhi, qlo - (khi + sh))
                                gap = g_ if gap is None else min(gap, g_)
                        if SLOPE_A[h] * gap > ALIBI_CUT:
                            continue
                    if kind == "p":
                        d0 = 12 + 4 * t
                        if kt < 12 or kt >= 44 or kt < d0:
                            var, dk = 0, None
                        elif kt < d0 + 4:
                            var, dk = 2, kt - d0
                        else:
                            var, dk = 1, None
                    elif kind == "h":
                        var, dk = (1 if 12 <= kt < 44 else 0), None
                    else:
                        d0 = 128 + 4 * t
                        if kt < d0:
                            var, dk = 0, None
                        elif kt < d0 + 4:
                            var, dk = 2, kt - d0
                        else:
                            var, dk = 1, None
                    tiles.append((kt, var, dk))
                nt = len(tiles)
                DEPTH = 2
                pts = [None] * nt
                sums = [None] * nt
                for i in range(nt + DEPTH):
                    if i < nt:
                        kt, var, dk = tiles[i]
                        SCa, SCb, PDp = nx_scp()
                        SCm = (SCa, SCb)
                        for m in range(2):
                            P.op("pe", [KTc[m][kt // 36], Q], [SCm[m]], lambda e, m=m, kt=kt, var=var, Q=Q, SCm=SCm: e.matmul(
                                SCm[m].t[:, 0:N], KT[m].t[0:68, kt * 128:(kt + 1) * 128], Q.t[0:68, 2 * var + m, 0:N], start=True, stop=True))
                        if dk is not None:
                            S2 = nx_SB()
                            off = 384 - 128 * dk
                            for m in range(2):
                                P.op("dve", [SCm[m], AT], [S2], lambda e, S2=S2, m=m, SCm=SCm, off=off: e.scalar_tensor_tensor(
                                    S2.t[:, m, 0:N], absat[:, off:off + N], -SLOPE_A[h], SCm[m].t[:, 0:N], ALU.mult, ALU.add))
                            srcb = [S2]
                            src_ap = S2.t[:, :, 0:N]
                        else:
                            srcb = [SCa, SCb]
                            src_ap = PDp[:, :].rearrange("p (m n) -> p m n", m=2)[:, :, 0:N]
                        Pt = nx_PT()
                        P.op("act", srcb, [Pt], lambda e, Pt=Pt, src_ap=src_ap: e.activation(Pt.t[:, :, 0:N], src_ap, AF.Exp))
                        pts[i] = Pt
                        if i % 2 == 1:
                            Sm = nx_PSm()
                            eng = "dve"
                            Pa = pts[i - 1]
                            P.op(eng, [Pa, Pt], [Sm], lambda e, Sm=Sm, Pa=Pa, Pt=Pt: e.tensor_tensor(
                                Sm.t[:, :, 0:N], Pa.t[:, :, 0:N], Pt.t[:, :, 0:N], ALU.add))
                            sums[i] = Sm
                    j = i - DEPTH
                    if j >= 0:
                        kt, var, dk = tiles[j]
                        Pt = pts[j]
                        st = (j == 0)
                        sp_ = (j == nt - 1)
                        for m in range(2):
                            P.op("pe", [VTc[kt // 36], Pt], [OB_[m]], lambda e, m=m, kt=kt, Pt=Pt, st=st, sp_=sp_: e.matmul(
                                OB_[m].t[:, 0:N], VT.t[:, kt, :], Pt.t[:, m, 0:N], start=st, stop=sp_))
                        if j % 2 == 1 or j == nt - 1:
                            Ls = sums[j] if j % 2 == 1 else Pt
                            lst = (j <= 1)
                            for m in range(2):
                                P.op("pe", [CB, Ls], [LB_[m]], lambda e, m=m, Ls=Ls, lst=lst, sp_=sp_: e.matmul(
                                    LB_[m].t[:, 0:N], ones[:], Ls.t[:, m, 0:N], start=lst, stop=sp_))
                for m in range(2):
                    P.op("act", [LB_[m]], [lr[m]], lambda e, m=m: e.activation(lr[m].t[:, 0:N], LB_[m].t[:, 0:N], AF.Copy))
                    P.op("dve", [OB_[m]], [o12[m]], lambda e, m=m: e.tensor_copy(o12[m].t[:, 0:N], OB_[m].t[:, 0:N]))
                for m in range(2):
                    P.op("dve", [lr[m]], [lr[m]], lambda e, m=m: e.reciprocal(lr[m].t[:, 0:N], lr[m].t[:, 0:N]))
                    P.op("dve", [o12[m], lr[m]], [o12[m]], lambda e, m=m: e.tensor_tensor(
                        o12[m].t[:, 0:N], o12[m].t[:, 0:N], lr[m].t[:, 0:N], ALU.mult))
                P.op("dve", [o12[0], o12[1], CB], [ob], lambda e: e.scalar_tensor_tensor(
                    ob.t[:, 0:N], o12[1].t[:, 0:N], NEGLAM, o12[0].t[:, 0:N], ALU.mult, ALU.add))
                OA = nx_oast()
                norm_tail(N, None, nx_sc(), sq2, ln2, ones[:], 1.0 / 128, EPS5, cst[:, 7:8], OA, z_sbuf=ob)
                P.dma("sp", None, OA, OAd[h, :, q0:q0 + N], OA.t[:, 0:N])
            if h + 1 < 8:
                load_kv(h + 1, [3])
        P.barrier()
        P.release(KTc[0] + KTc[1] + VTc + QSb + oast + [AT])

    if dbg == 2:
        return nc, es

    QTB = [(t * 512, 512, OWN0 + t * 512, 0, 56) for t in range(8)]
    QTB += [(QL, 1, OWN0 - 1, 0, 56), (QR, 1, OWN1, 0, 56)]
    QTB += [(QS0 + t * 512, 512, t * 512, 56, 72) for t in range(4)]
    with ExitStack() as ph:
        def psb(name, shape, dt):
            return ph.enter_context(nc.sbuf_tensor(name, list(shape), dt))
        KBT = Buf(psb("KBT", [128, 3, NB], BF16))
        VBT = Buf(psb("VBT", [128, 3, 72, 128], BF16))
        QBT = [Buf(psb("QBT%d" % i, [128, 3, 512], BF16)) for i in range(2)]
        dbs = psb("dbs", [128, DB_TOT], BF16)
        vbs = psb("vbs", [128, 72], F32)
        PT = [Buf(psb("PTb%d" % i, [128, 512], BF16)) for i in range(6)]
        SBf = [Buf(psb("SBb%d" % i, [128, 512], F32)) for i in range(4)]
        lrb = Buf(psb("lrb", [128, 512], F32))
        obf = Buf(psb("obf", [128, 512], F32))
        obst = [Buf(psb("obst%d" % i, [128, 512], BF16)) for i in range(2)]
        nx_QB, nx_PT, nx_SB, nx_obst = _rot(QBT), _rot(PT), _rot(SBf), _rot(obst)
        nx_sc = _rot(PS[0:4])
        OBk, LBk = PS[4], PS[6]
        DT = Buf(None)
        P.dma("pool", DT, None, dbs[:], dbt[:, :])
        P.dma("sp", DT, None, vbs[:], valb[:, :])
        KBg = [Buf(KBT.t) for _ in range(3)]
        VBg = [Buf(VBT.t) for _ in range(3)]
        qbitems = [(hh, qq) for hh in range(4) for qq in QTB]
        qbl = {}

        def ens_qb(k):
            if k < len(qbitems) and k not in qbl:
                hh, (q0_, N_, _a, _b, _c) = qbitems[k]
                Qb = nx_QB()
                for g in range(3):
                    P.dma("sp", Qb, None, Qb.t[:, g, 0:N_], QBd[g * 4 + hh, :, q0_:q0_ + N_])
                qbl[k] = Qb
        for h in range(4):
            for g in range(3):
                P.dma("sp", KBg[g], None, KBT.t[:, g, :], KBd[g * 4 + h, :, :])
                for c2 in range(2):
                    P.dma("sp", VBg[g], None, VBT.t[:, g, c2 * 36:(c2 + 1) * 36, :],
                          VBd[g * 4 + h, c2 * 4608:(c2 + 1) * 4608, :].rearrange("(kt p) e -> p kt e", p=128))
            for qi_, (q0, N, qpos, klo, khi) in enumerate(QTB):
                kq = h * len(QTB) + qi_
                ens_qb(kq)
                Q = qbl.pop(kq)
                ens_qb(kq + 1)
                tiles = []
                for g in range(3):
                    wl = 64 * DIL[g]
                    for kt in range(klo, khi):
                        kpos = (kt - klo) * 128
                        dbase = kpos - qpos
                        if dbase - (N - 1) <= wl and dbase + 127 >= -wl:
                            tiles.append((g, kt, DB_OFF[g] + DB_C0[g] - dbase))
                nt = len(tiles)
                DEPTH = 3
                pts = [None] * nt
                for i in range(nt + DEPTH):
                    if i < nt:
                        g, kt, off = tiles[i]
                        SC = nx_sc()
                        P.op("pe", [KBg[g], Q], [SC], lambda e, SC=SC, g=g, kt=kt, Q=Q: e.matmul(
                            SC.t[:, 0:N], KBT.t[:, g, kt * 128:(kt + 1) * 128], Q.t[:, g, 0:N], start=True, stop=True))
                        S2 = nx_SB()
                        sl = -SLOPE_B[g * 4 + h]
                        P.op("dve", [SC, DT], [S2], lambda e, S2=S2, SC=SC, off=off, sl=sl: e.scalar_tensor_tensor(
                            S2.t[:, 0:N], dbs[:, off:off + N], sl, SC.t[:, 0:N], ALU.mult, ALU.add))
                        Pt = nx_PT()
                        P.op("act", [S2, DT], [Pt], lambda e, Pt=Pt, S2=S2, kt=kt: e.activation(
                            Pt.t[:, 0:N], S2.t[:, 0:N], AF.Exp, bias=vbs[:, kt:kt + 1]))
                        pts[i] = Pt
                    j = i - DEPTH
                    if j >= 0:
                        g, kt, off = tiles[j]
                        Pt = pts[j]
                        P.op("pe", [VBg[g], Pt], [OBk], lambda e, g=g, kt=kt, Pt=Pt, j=j: e.matmul(
                            OBk.t[:, 0:N], VBT.t[:, g, kt, :], Pt.t[:, 0:N], start=(j == 0), stop=(j == nt - 1)))
                        P.op("pe", [CB, Pt], [LBk], lambda e, Pt=Pt, j=j: e.matmul(
                            LBk.t[:, 0:N], ones[:], Pt.t[:, 0:N], start=(j == 0), stop=(j == nt - 1)))
                P.op("act", [LBk], [lrb], lambda e: e.activation(lrb.t[:, 0:N], LBk.t[:, 0:N], AF.Copy))
                P.op("dve", [OBk], [obf], lambda e: e.tensor_copy(obf.t[:, 0:N], OBk.t[:, 0:N]))
                P.op("dve", [lrb], [lrb], lambda e: e.reciprocal(lrb.t[:, 0:N], lrb.t[:, 0:N]))
                OO = nx_obst()
                P.op("dve", [obf, lrb], [OO], lambda e, OO=OO: e.tensor_tensor(OO.t[:, 0:N], obf.t[:, 0:N], lrb.t[:, 0:N], ALU.mult))
                P.dma("sp", None, OO, OBd[h, :, q0:q0 + N], OO.t[:, 0:N])
        P.barrier()
        P.release(KBg + VBg + [DT] + QBT + obst)

    if dbg == 3:
        return nc, es

    with ExitStack() as ph:
        def psb(name, shape, dt):
            return ph.enter_context(nc.sbuf_tensor(name, list(shape), dt))
        wo = psb("wo", [128, 16, D], BF16)
        gft = psb("gft", [128, D], F32)
        W3 = Buf(None)
        P.dma("sp", W3, WB["o"], wo[:], wb_o[:, :, :])
        P.dma("sp", W3, None, gft[:], gffnb[:, :])
        wpc = [Buf(psb("wpc%d" % i, [128, 12, 256], BF16)) for i in range(2)]
        oin = [Buf(psb("oin%d" % i, [128, 12, 512], BF16)) for i in range(2)]
        gin = [Buf(psb("gin%d" % i, [128, 2, 512], BF16)) for i in range(2)]
        mT = Buf(psb("mT", [128, 16, 512], BF16))
        t1 = [Buf(psb("t1%d" % i, [128, 512], F32)) for i in range(2)]
        t2 = [Buf(psb("t2%d" % i, [128, 512], F32)) for i in range(2)]
        xb = [Buf(psb("xb%d" % i, [128, D], F32)) for i in range(2)]
        x1b = [Buf(psb("x1b%d" % i, [128, D], F32)) for i in range(2)]
        s3 = [Buf(psb("s3%d" % i, [128, 4], F32)) for i in range(2)]
        h2s = [Buf(psb("h2s%d" % i, [128, D], BF16)) for i in range(2)]
        h2t = [Buf(psb("h2t%d" % i, [128, 16, 128], BF16)) for i in range(2)]
        zt = Buf(psb("zt", [128, 16, 2], BF16))
        nx_wpc, nx_oin, nx_gin, nx_t1, nx_t2 = _rot(wpc), _rot(oin), _rot(gin), _rot(t1), _rot(t2)
        nx_xb, nx_x1b, nx_s3, nx_h2s, nx_h2t = _rot(xb), _rot(x1b), _rot(s3), _rot(h2s), _rot(h2t)
        nx_pa, nx_pbk = _rot(PS[0:2]), _rot(PS[2:4])
        nx_xo = _rot(PS[4:6])
        ptb = [(PS[i], PSt[i].bitcast(BF16)) for i in (6, 7)]
        nx_pt = _rot(ptb)
        P.op("dve", [], [zt], lambda e: e.memset(zt.t[:], 0.0))
        P.dma("sp", None, zt, H2d[:, :, CVS0:CVS0 + 1], zt.t[:, :, 0:1])
        P.dma("sp", None, zt, H2d[:, :, NCV - 1:NCV], zt.t[:, :, 1:2])
        blk_i = [0]

        QT3 = [(t * 512, 512, "p", t) for t in range(8)] + [(QL, 2, "h", 0)] + [(QS0 + t * 512, 512, "s", t) for t in range(4)]
        if _lim2:
            QT3 = [QT3[i] for i in (0, 1, 8, 12)]
        oil = {}

        def ens_oi(k):
            if k < len(QT3) and k not in oil:
                q0_, N_, _k, _t = QT3[k]
                OIb = nx_oin()
                P.dma("sp", OIb, None, OIb.t[:, 0:8, 0:N_], OAd[:, :, q0_:q0_ + N_].rearrange("h p q -> p h q"))
                P.dma("sp", OIb, None, OIb.t[:, 8:12, 0:N_], OBd[:, :, q0_:q0_ + N_].rearrange("h p q -> p h q"))
                oil[k] = OIb

        for qi_, (q0, N, kind, t) in enumerate(QT3):
            ens_oi(qi_)
            OI = oil.pop(qi_)
            ens_oi(qi_ + 1)
            for fo in range(16):
                if fo % 2 == 0:
                    WP = nx_wpc()
                    cg_ = fo // 2
                    P.dma("sp", WP, WB["pa"], WP.t[:, 0:8, :], wb_pa[:, :, cg_ * 256:(cg_ + 1) * 256])
                    P.dma("sp", WP, WB["pb"], WP.t[:, 8:12, :], wb_pb[:, :, cg_ * 256:(cg_ + 1) * 256])
                fl = fo % 2
                GI = nx_gin()
                P.dma("sp", GI, None, GI.t[:, 0, 0:N], GTd[fo, :, q0:q0 + N])
                P.dma("sp", GI, None, GI.t[:, 1, 0:N], GTd[16 + fo, :, q0:q0 + N])
                A = nx_pa()
                Bk = nx_pbk()

                def mma(e, A=A, fl=fl, OI=OI, WP=WP):
                    for kt in range(8):
                        ins = e.matmul(A.t[:, 0:N], WP.t[:, kt, fl * 128:(fl + 1) * 128], OI.t[:, kt, 0:N], start=(kt == 0), stop=(kt == 7))
                    return ins

                def mmb(e, Bk=Bk, fl=fl, OI=OI, WP=WP):
                    for kt in range(4):
                        ins = e.matmul(Bk.t[:, 0:N], WP.t[:, 8 + kt, fl * 128:(fl + 1) * 128], OI.t[:, 8 + kt, 0:N], start=(kt == 0), stop=(kt == 3))
                    return ins
                P.op("pe", [WP, OI], [A], mma)
                P.op("pe", [WP, OI], [Bk], mmb)
                T1, T2 = nx_t1(), nx_t2()
                P.op("dve", [A, GI], [T1], lambda e, T1=T1, A=A, GI=GI: e.tensor_tensor(T1.t[:, 0:N], A.t[:, 0:N], GI.t[:, 0, 0:N], ALU.mult))
                P.op("dve", [Bk, GI], [T2], lambda e, T2=T2, Bk=Bk, GI=GI: e.tensor_tensor(T2.t[:, 0:N], Bk.t[:, 0:N], GI.t[:, 1, 0:N], ALU.mult))
                P.op("pool", [T1, T2], [mT], lambda e, T1=T1, T2=T2, fo=fo: e.tensor_tensor(mT.t[:, fo, 0:N], T1.t[:, 0:N], T2.t[:, 0:N], ALU.add))
            nblk = (N + 127) // 128

            def x1_part(b):
                nb_ = min(128, N - b * 128)
                XB = nx_xb()
                if kind == "p":
                    r0 = OWN0 + t * 512 + b * 128
                    P.dma("sp", XB, None, XB.t[0:nb_, :], xp[r0:r0 + nb_, :])
                    cvs = [(1 + t * 512 + b * 128, 0, nb_)]
                elif kind == "s":
                    r0 = t * 512 + b * 128
                    P.dma("sp", XB, None, XB.t[0:nb_, :], xs[r0:r0 + nb_, :])
                    cvs = [(CVS0 + 1 + t * 512 + b * 128, 0, nb_)]
                else:
                    P.dma("sp", XB, None, XB.t[0:1, :], xp[OWN0 - 1:OWN0, :])
                    P.dma("sp", XB, None, XB.t[1:2, :], xp[OWN1:OWN1 + 1, :])
                    cvs = [(0, 0, 1), (4097, 1, 1)]
                X1 = nx_x1b()
                for cc in range(4):
                    XO = nx_xo()

                    def mmo(e, XO=XO, cc=cc, b=b, nb_=nb_):
                        for kt in range(16):
                            ins = e.matmul(XO.t[0:nb_, :], mT.t[:, kt, b * 128:b * 128 + nb_], wo[:, kt, cc * 512:(cc + 1) * 512], start=(kt == 0), stop=(kt == 15))
                        return ins
                    P.op("pe", [mT, W3], [XO], mmo)
                    P.op("dve", [XO, XB], [X1], lambda e, X1=X1, XO=XO, XB=XB, cc=cc, nb_=nb_: e.tensor_tensor(
                        X1.t[0:nb_, cc * 512:(cc + 1) * 512], XO.t[0:nb_, :], XB.t[0:nb_, cc * 512:(cc + 1) * 512], ALU.add))
                S3 = nx_s3()
                HS = nx_h2s()
                P.op("act", [X1], [S3, HS], lambda e, X1=X1, S3=S3, HS=HS, nb_=nb_: e.activation(HS.t[0:nb_, :], X1.t[0:nb_, :], AF.Square, accum_out=S3.t[0:nb_, 0:1]))
                P.op("act", [S3, CB], [S3], lambda e, S3=S3, nb_=nb_: e.activation(S3.t[0:nb_, 1:2], S3.t[0:nb_, 0:1], AF.Ln, bias=cst[0:nb_, 0:1], scale=1.0 / D))
                P.op("act", [S3], [S3], lambda e, S3=S3, nb_=nb_: e.activation(S3.t[0:nb_, 1:2], S3.t[0:nb_, 1:2], AF.Exp, scale=-0.5))
                if kind != "h":
                    P.dma("sp", None, X1, X1d[cvs[0][0]:cvs[0][0] + nb_, :], X1.t[0:nb_, :])
                P.op("dve", [X1, S3, W3], [HS], lambda e, HS=HS, X1=X1, S3=S3, nb_=nb_: e.scalar_tensor_tensor(
                    HS.t[0:nb_, :], X1.t[0:nb_, :], S3.t[0:nb_, 1:2], gft[0:nb_, :], ALU.mult, ALU.mult))
                return (HS, nb_, cvs)

            def tr_part(st_):
                HS, nb_, cvs = st_
                HT = nx_h2t()
                blk_i[0] += 1
                for kh in range(2):
                    pb, pvw = nx_pt()

                    def tr(e, HS=HS, pvw=pvw, kh=kh, nb_=nb_):
                        for k in range(8):
                            kt = kh * 8 + k
                            ins = e.transpose(pvw[:, k * 128:k * 128 + nb_], HS.t[0:nb_, kt * 128:(kt + 1) * 128], idt[0:nb_, 0:nb_])
                        return ins
                    P.op("pe", [HS, CB], [pb], tr)
                    src = pvw[:, :].rearrange("p (k t) -> p k t", k=8)[:, :, 0:nb_]
                    dst = HT.t[:, kh * 8:kh * 8 + 8, 0:nb_]
                    if kind == "h":
                        for k in range(8):
                            P.op("dve", [pb, CB], [HT], lambda e, k=k, kh=kh, pvw=pvw, HT=HT: e.tensor_tensor(
                                HT.t[:, kh * 8 + k, 0:2], pvw[:, k * 128:k * 128 + 2], pvt[:, PV_HM:PV_HM + 2], ALU.mult))
                    elif blk_i[0] % 2:
                        P.op("dve", [pb], [HT], lambda e, dst=dst, src=src: e.tensor_copy(dst, src))
                    else:
                        P.op("act", [pb], [HT], lambda e, dst=dst, src=src: e.activation(dst, src, AF.Copy))
                for (cv0, c0, n) in cvs:
                    P.dma("sp", None, HT, H2d[:, :, cv0:cv0 + n], HT.t[:, :, c0:c0 + n])

            prev = None
            for b in range(nblk):
                cur = x1_part(b)
                if prev is not None:
                    tr_part(prev)
                prev = cur
            tr_part(prev)
        P.barrier()
        P.release([W3, zt] + wpc + oin + gin + xb + x1b + h2t)

    if dbg == 4:
        return nc, es

    WIN = [(510 * k, min(512, 4098 - 510 * k), "p") for k in range(9)]
    WIN += [(CVS0 + 510 * k, min(512, 2050 - 510 * k), "s") for k in range(5)]
    if _lim2:
        WIN = [WIN[i] for i in (0, 1, 13)]
    with ExitStack() as ph:
        def psb(name, shape, dt):
            return ph.enter_context(nc.sbuf_tensor(name, list(shape), dt))
        hw = [Buf(psb("hw%d" % i, [128, 16, 512], BF16)) for i in range(2)]
        wu = [Buf(psb("wu%d" % i, [128, 2, 16, 128], BF16)) for i in range(3)]
        gT = Buf(psb("gT", [128, 44, 512], BF16))
        wd = [Buf(psb("wd%d" % i, [128, 44, 256], BF16)) for i in range(2)]
        cg = [Buf(psb("cg%d" % i, [128, 512], F32)) for i in range(2)]
        cv_ = [Buf(psb("cv%d" % i, [128, 512], F32)) for i in range(2)]
        ge = [Buf(psb("ge%d" % i, [128, 512], F32)) for i in range(2)]
        x1r = [Buf(psb("x1r%d" % i, [128, 4, 256], F32)) for i in range(2)]
        yb = [Buf(psb("yb%d" % i, [128, 256], F32)) for i in range(3)]
        nx_hw, nx_wu, nx_wd, nx_cg, nx_cv, nx_ge = _rot(hw), _rot(wu), _rot(wd), _rot(cg), _rot(cv_), _rot(ge)
        nx_x1r, nx_yb = _rot(x1r), _rot(yb)
        nx_ug, nx_uv, nx_dn = _rot(PS[0:2]), _rot(PS[2:4]), _rot(PS[4:8])

        def cw(j, ft):
            c = PV_CW + j * 88 + ft
            return pvt[:, c:c + 1]

        def cb(ft):
            c = PV_CB + ft
            return pvt[:, c:c + 1]

        hwl = {}

        def ens_hw(w):
            if w < len(WIN) and w not in hwl:
                c0_, n_, _k = WIN[w]
                b_ = nx_hw()
                P.dma("sp", b_, None, b_.t[:, :, 0:n_], H2d[:, :, c0_:c0_ + n_])
                hwl[w] = b_
        wul = {}

        def ens_wu(k):
            if k < 44 * len(WIN) and k not in wul:
                j_ = k % 44
                s_ = nx_wu()
                P.dma("sp", s_, WB["up"], s_.t[:, 0, :, :], wb_up[:, :, j_ * 128:(j_ + 1) * 128])
                P.dma("sp", s_, WB["up"], s_.t[:, 1, :, :], wb_up[:, :, DFF + j_ * 128:DFF + (j_ + 1) * 128])
                wul[k] = s_
        wdl = {}

        def ens_wd(k):
            if k < 8 * len(WIN) and k not in wdl:
                w_, c8_ = k // 8, k % 8
                c0_, n_, _k = WIN[w_]
                no_ = n_ - 2
                s_ = nx_wd()
                P.dma("sp", s_, WB["dn"], s_.t[:], wb_dn[:, :, c8_ * 256:(c8_ + 1) * 256])
                xr_ = nx_x1r()
                for b_ in range((no_ + 127) // 128):
                    nb2 = min(128, no_ - b_ * 128)
                    cvr_ = c0_ + 1 + b_ * 128
                    P.dma("sp", xr_, None, xr_.t[0:nb2, b_, :], X1d[cvr_:cvr_ + nb2, c8_ * 256:(c8_ + 1) * 256])
                wdl[k] = (s_, xr_)

        for wi, (c0, n, kind) in enumerate(WIN):
            no = n - 2
            ens_hw(wi)
            HW = hwl.pop(wi)
            ens_hw(wi + 1)
            for j in range(44):
                k = wi * 44 + j
                ens_wu(k)
                ens_wu(k + 1)
                ens_wu(k + 2)
                WU = wul.pop(k)
                UG, UV = nx_ug(), nx_uv()
                for half, U in ((0, UG), (1, UV)):
                    def mmu(e, U=U, half=half, WU=WU):
                        for kt in range(16):
                            ins = e.matmul(U.t[:, 0:n], WU.t[:, half, kt, :], HW.t[:, kt, 0:n], start=(kt == 0), stop=(kt == 15))
                        return ins
                    P.op("pe", [WU, HW], [U], mmu)
                CG, CV = nx_cg(), nx_cv()
                for U, C, ft in ((UG, CG, j), (UV, CV, 44 + j)):
                    P.op("dve", [U, CB], [C], lambda e, U=U, C=C, ft=ft: e.tensor_scalar(
                        C.t[:, 0:no], U.t[:, 0:no], cw(0, ft), cb(ft), ALU.mult, ALU.add))
                    P.op("dve", [U, C, CB], [C], lambda e, U=U, C=C, ft=ft: e.scalar_tensor_tensor(
                        C.t[:, 0:no], U.t[:, 1:no + 1], cw(1, ft), C.t[:, 0:no], ALU.mult, ALU.add))
                    P.op("dve", [U, C, CB], [C], lambda e, U=U, C=C, ft=ft: e.scalar_tensor_tensor(
                        C.t[:, 0:no], U.t[:, 2:no + 2], cw(2, ft), C.t[:, 0:no], ALU.mult, ALU.add))
                GE = nx_ge()
                P.op("act", [CG], [GE], lambda e, GE=GE, CG=CG: e.activation(GE.t[:, 0:no], CG.t[:, 0:no], AF.Gelu))
                P.op("pool", [GE, CV], [gT], lambda e, GE=GE, CV=CV, j=j: e.tensor_tensor(gT.t[:, j, 0:no], GE.t[:, 0:no], CV.t[:, 0:no], ALU.mult))
                if j == 40:
                    ens_wd(wi * 8)
            nblk = (no + 127) // 128
            for c8 in range(8):
                k = wi * 8 + c8
                ens_wd(k)
                WD, XR = wdl.pop(k)
                if c8 < 7:
                    ens_wd(k + 1)
                for b in range(nblk):
                    nb_ = min(128, no - b * 128)
                    DN = nx_dn()

                    def mmd(e, DN=DN, WD=WD, b=b, nb_=nb_):
                        for j in range(44):
                            ins = e.matmul(DN.t[0:nb_, 0:256], gT.t[:, j, b * 128:b * 128 + nb_], WD.t[:, j, :], start=(j == 0), stop=(j == 43))
                        return ins
                    P.op("pe", [gT, WD], [DN], mmd)
                    cvr = c0 + 1 + b * 128
                    YB = nx_yb()
                    P.op("dve", [DN, XR], [YB], lambda e, YB=YB, DN=DN, XR=XR, nb_=nb_, b=b: e.tensor_tensor(
                        YB.t[0:nb_, 0:256], DN.t[0:nb_, 0:256], XR.t[0:nb_, b, :], ALU.add))
                    if kind == "p":
                        dst = yp[cvr - 1:cvr - 1 + nb_, c8 * 256:(c8 + 1) * 256]
                    else:
                        dst = ys[cvr - CVS0 - 1:cvr - CVS0 - 1 + nb_, c8 * 256:(c8 + 1) * 256]
                    P.dma("sp", None, YB, dst, YB.t[0:nb_, 0:256])
        P.barrier()
    return nc, es


def _tables(c):
    r = c % 4
    own_lo, own_hi = 4096 * r, 4096 * (r + 1)
    u = np.arange(SEQ)
    pk = (own_lo - OWN0 + u) % SEQ
    is_own = (u >= OWN0) & (u < OWN1)
    sig = np.where(is_own, 1.0, np.where(pk < own_lo, 1.0, -1.0))
    pk_all = np.concatenate([pk, np.arange(SSEQ)]).astype(np.float64)
    sig_all = np.concatenate([sig, np.ones(SSEQ)])
    jlo = pk_all % 128
    jhi = pk_all - jlo
    kaug = np.zeros((8, 4, NCTX), np.float32)
    for h in range(8):
        s = SLOPE_A[h]
        kaug[h, 0] = -sig_all * s
        kaug[h, 1] = -sig_all * s
        kaug[h, 2] = sig_all * s * jlo
        kaug[h, 3] = sig_all * s * jhi
    qpos = np.zeros(NQ, np.float64)
    qpos[0:4096] = own_lo + np.arange(4096)
    qpos[QL] = own_lo - 1
    qpos[QR] = own_hi
    qpos[QS0:] = np.arange(SSEQ)
    ilo = qpos % 128
    ihi = qpos - ilo
    qaugp = np.stack([ilo, ihi, np.ones(NQ), np.ones(NQ)]).astype(np.float32)
    qaugm = -qaugp
    qaugm[:, QR] = qaugp[:, QR]
    valb = np.zeros((128, 72), np.float32)
    ub = np.arange(BWIN)
    pb = own_lo - OWN0 + ub
    bad = (pb < 0) | (pb >= SEQ)
    valb[:, :56] = np.where(bad, -30000.0, 0.0).reshape(56, 128).T
    hm = np.array([1.0 if r > 0 else 0.0, 1.0 if r < 3 else 0.0], np.float32)
    return kaug, qaugp, qaugm, valb, hm


def _shared_tables():
    k = np.arange(128)[:, None]
    absa = np.abs(k - np.arange(896)[None, :] + 384).astype(np.float32)
    dbt = np.zeros((128, DB_TOT), np.float32)
    for g in range(3):
        cc = np.arange(DB_W[g])[None, :]
        dl = k - cc + DB_C0[g]
        ok = (np.abs(dl) <= 64 * DIL[g]) & (dl % DIL[g] == 0)
        dbt[:, DB_OFF[g]:DB_OFF[g] + DB_W[g]] = np.where(ok, np.abs(dl), BIGD)
    return absa, dbt, np.eye(128, dtype=np.float32)


def make_in_maps(inp):
    f = lambda a: np.ascontiguousarray(a, dtype=np.float32)
    absa, dbt, ident = _shared_tables()
    w_in, w_pa, w_pb, w_o = f(inp["w_in"][0]), f(inp["w_pa"][0]), f(inp["w_pb"][0]), f(inp["w_o"][0])
    w_up, w_dn = f(inp["w_up"][0]), f(inp["w_down"][0])
    pvb = np.zeros((128, NPV), np.float32)
    gmixb = np.ascontiguousarray(np.broadcast_to(inp["g_mix_norm"][0][None, :], (128, D)), dtype=np.float32)
    gffnb = np.ascontiguousarray(np.broadcast_to(inp["g_ffn_norm"][0][None, :], (128, D)), dtype=np.float32)
    pvb[:, PV_GQA] = np.tile(inp["g_qa"][0], 2)
    pvb[:, PV_GKA] = np.tile(inp["g_ka"][0], 2)
    pvb[:, PV_GQB] = inp["g_qb"][0]
    pvb[:, PV_GKB] = inp["g_kb"][0]
    pvb[:, PV_GSUB] = inp["g_subln"][0]
    cwv = inp["conv_w"][0]
    for j in range(3):
        pvb[:, PV_CW + j * 88:PV_CW + (j + 1) * 88] = cwv[j].reshape(88, 128).T
    pvb[:, PV_CB:PV_CB + 88] = inp["conv_b"][0].reshape(88, 128).T
    for i, kname in enumerate(("lam_q1", "lam_k1", "lam_q2", "lam_k2")):
        pvb[:, PV_LAM + 64 * i:PV_LAM + 64 * (i + 1)] = inp[kname][0][None, :]
    maps = []
    for c in range(8):
        bp, r, bs = c // 4, c % 4, c // 2
        kaug, qaugp, qaugm, valb, hm = _tables(c)
        pvc = pvb.copy()
        pvc[:, PV_HM:PV_HM + 2] = hm[None, :]
        xpl = np.roll(inp["x_prompt"][bp], -(4096 * r - OWN0), axis=0)
        maps.append({
            "xp": f(xpl), "xs": f(inp["x_sample"][bs]),
            "w_in": w_in, "w_pa": w_pa, "w_pb": w_pb, "w_o": w_o, "w_up": w_up, "w_dn": w_dn,
            "pv": pvc, "gmixb": gmixb, "gffnb": gffnb, "kaug": kaug, "qaugp": qaugp, "qaugm": qaugm,
            "absa": absa, "dbt": dbt, "valb": valb, "ident": ident,
        })
    return maps


def kernel(**inp):
    nc, es = build()
    maps = make_in_maps(inp)
    res = run_bass_kernel_spmd(nc, maps, core_ids=list(range(8)))
    es.close()
    yp = np.zeros((2, SEQ, D), np.float32)
    ys = np.zeros((4, SSEQ, D), np.float32)
    for c in range(8):
        bp, r, bs, hs = c // 4, c % 4, c // 2, c % 2
        yp[bp, 4096 * r:4096 * (r + 1)] = res.results[c]["yp"]
        ys[bs, 1024 * hs:1024 * (hs + 1)] = res.results[c]["ys"][1024 * hs:1024 * (hs + 1)]
    return yp, ys
```

```python
import numpy as np
from contextlib import ExitStack
import concourse.bass as bass
import concourse.mybir as mybir
from concourse.bass_utils import run_bass_kernel_spmd

F32 = mybir.dt.float32
BF16 = mybir.dt.bfloat16
AF = mybir.ActivationFunctionType
ALU = mybir.AluOpType
AX = mybir.AxisListType

D = 2048
SEQ = 16384
SSEQ = 2048
N_IN = 11776
DFF = 5632
NCTX = SEQ + SSEQ
OWN0 = 1536
OWN1 = OWN0 + 4096
BWIN = 7168
NB = BWIN + SSEQ
NQ = 4096 + 2 + SSEQ
QL, QR, QS0 = 4096, 4097, 4098
NCV = 4098 + 2050
CVS0 = 4098
SLOPE_A = [2.0 ** (-(h + 1)) for h in range(8)]
SLOPE_B = [2.0 ** (-8.0 * (i + 1) / 12.0) for i in range(12)]
DIL = [1, 4, 16]
BIGD = 1.0e7
ALIBI_CUT = 94.4
DB_C0 = [64 * d + 512 for d in DIL]
DB_W = [DB_C0[g] + 64 * DIL[g] + 640 for g in range(3)]
DB_OFF = [0, DB_W[0], DB_W[0] + DB_W[1]]
DB_TOT = sum(DB_W)

PV_GQA = 0
PV_GKA = 1
PV_GQB = 2
PV_GKB = 3
PV_GSUB = 4
PV_CW = 5
PV_CB = PV_CW + 264
PV_LAM = PV_CB + 88
PV_HM = PV_LAM + 256
PV_GQK = PV_HM + 2
NPV = PV_GQK + 128

ENGS = ("pe", "act", "dve", "pool", "sp")


class Buf:
    __slots__ = ("t", "w", "r", "ds", "name")

    def __init__(self, t, name=""):
        self.t = t
        self.w = None
        self.r = {}
        self.ds = None
        self.name = name


class Prog:
    def __init__(self, nc, es):
        self.nc = nc
        self.es = es
        self.E = {"pe": nc.tensor, "act": nc.scalar, "dve": nc.vector,
                  "pool": nc.gpsimd, "sp": nc.sync}
        self.sem = {}
        self.cnt = {}
        self.waited = {e: {} for e in ENGS}
        for e in ("pe", "act", "dve", "pool"):
            self._mk("E_" + e)
        self.free_ds = {"sw": [], "hw": []}
        self.nds = 0

    def _mk(self, name):
        self.sem[name] = self.es.enter_context(self.nc.semaphore(name))
        self.cnt[name] = 0

    def get_ds(self, kind):
        if self.free_ds[kind]:
            return self.free_ds[kind].pop()
        name = "D%s%d" % (kind, self.nds)
        self.nds += 1
        self._mk(name)
        return name

    def release(self, bufs):
        for b in bufs:
            if b.ds is not None:
                for kind, nm in b.ds.items():
                    self.free_ds[kind].append(nm)
                b.ds = None

    def wait(self, eng, tok):
        if tok is None:
            return
        s, v = tok
        if eng == "pe" and s == "E_pe":
            return
        if self.waited[eng].get(s, 0) >= v:
            return
        self.waited[eng][s] = v
        self.E[eng].wait_ge(self.sem[s], v)

    def _deps(self, eng, reads, writes):
        for b in reads:
            self.wait(eng, b.w)
        for b in writes:
            self.wait(eng, b.w)
            for t in b.r.values():
                self.wait(eng, t)

    def op(self, eng, reads, writes, fn):
        self._deps(eng, reads, writes)
        ins = fn(self.E[eng])
        s = "E_" + eng
        self.cnt[s] += 1
        ins.then_inc(self.sem[s], 1)
        tok = (s, self.cnt[s])
        for b in reads:
            b.r[eng] = tok
        for b in writes:
            b.w = tok
            b.r = {}
        return tok

    def dma(self, q, out_b, in_b, out_ap, in_ap, track=None):
        reads = [in_b] if in_b is not None else []
        writes = [out_b] if out_b is not None else []
        self._deps(q, reads, writes)
        tb = track if track is not None else (out_b if out_b is not None else in_b)
        if tb.ds is None:
            tb.ds = {}
        kind = "sw" if q == "pool" else "hw"
        if kind not in tb.ds:
            tb.ds[kind] = self.get_ds(kind)
        s = tb.ds[kind]
        self.cnt[s] += 16
        try:
            ins = self.E[q].dma_start(out=out_ap, in_=in_ap)
        except ValueError:
            ins = self.E[q].dma_start(out=out_ap, in_=in_ap, allow_slow_non_contiguous=True)
        ins.then_inc(self.sem[s], 16)
        tok = (s, self.cnt[s])
        if in_b is not None:
            in_b.r["dma_" + s] = tok
        if out_b is not None:
            out_b.w = tok
            out_b.r = {}
        return tok

    def barrier(self):
        for e in ENGS:
            for s, c in self.cnt.items():
                if c > 0:
                    self.wait(e, (s, c))


def _rot(lst):
    i = [0]

    def nxt():
        b = lst[i[0] % len(lst)]
        i[0] += 1
        return b
    return nxt


def build(dbg=False):
    nc = bass.Bass("TRN2", target_bir_lowering=False)
    es = ExitStack()
    P = Prog(nc, es)
    es.enter_context(nc.allow_low_precision(reason="bf16 matmul operands by design, fp32 accumulation"))

    def din(name, shape):
        return nc.dram_tensor(name, list(shape), F32, kind="ExternalInput").ap()

    def dscr(name, shape, dt):
        if dbg:
            return nc.dram_tensor(name, list(shape), dt, kind="ExternalOutput").ap()
        return nc.dram_tensor(name, list(shape), dt).ap()

    xp = din("xp", [SEQ, D])
    xs = din("xs", [SSEQ, D])
    w_in = din("w_in", [D, N_IN])
    w_pa = din("w_pa", [1024, D])
    w_pb = din("w_pb", [512, D])
    w_o = din("w_o", [D, D])
    w_up = din("w_up", [D, 2 * DFF])
    w_dn = din("w_dn", [DFF, D])
    pv = din("pv", [128, NPV])
    gmixb = din("gmixb", [128, D])
    gffnb = din("gffnb", [128, D])
    kaug = din("kaug", [8, 4, NCTX])
    qaugp = din("qaugp", [4, NQ])
    qaugm = din("qaugm", [4, NQ])
    absa = din("absa", [128, 896])
    dbt = din("dbt", [128, DB_TOT])
    valb = din("valb", [128, 72])
    ident = din("ident", [128, 128])
    yp = nc.dram_tensor("yp", [4096, D], F32, kind="ExternalOutput").ap()
    ys = nc.dram_tensor("ys", [SSEQ, D], F32, kind="ExternalOutput").ap()

    wb_in = nc.dram_tensor("wb_in", [128, 16, N_IN], BF16).ap()
    wb_pa = nc.dram_tensor("wb_pa", [128, 8, D], BF16).ap()
    wb_pb = nc.dram_tensor("wb_pb", [128, 4, D], BF16).ap()
    wb_o = nc.dram_tensor("wb_o", [128, 16, D], BF16).ap()
    wb_up = nc.dram_tensor("wb_up", [128, 16, 2 * DFF], BF16).ap()
    wb_dn = nc.dram_tensor("wb_dn", [128, 44, D], BF16).ap()
    kaug_b = nc.dram_tensor("kaug_b", [8, 4, NCTX], BF16).ap()
    qaugp_b = nc.dram_tensor("qaugp_b", [4, NQ], BF16).ap()
    qaugm_b = nc.dram_tensor("qaugm_b", [4, NQ], BF16).ap()
    QAd = dscr("QAd", [8, 128, NQ], BF16)
    KAd = dscr("KAd", [8, 128, NCTX], BF16)
    VAd = dscr("VAd", [8, NCTX, 128], BF16)
    QBd = dscr("QBd", [12, 128, NQ], BF16)
    KBd = dscr("KBd", [12, 128, NB], BF16)
    VBd = dscr("VBd", [12, NB, 128], BF16)
    GTd = dscr("GTd", [32, 128, NQ], BF16)
    OAd = dscr("OAd", [8, 128, NQ], BF16)
    OBd = dscr("OBd", [4, 128, NQ], BF16)
    X1d = dscr("X1d", [NCV, D], F32)
    H2d = dscr("H2d", [128, 16, NCV], BF16)

    def sb(name, shape, dt):
        return es.enter_context(nc.sbuf_tensor(name, list(shape), dt))

    PDt = [es.enter_context(nc.psum_tensor("pd%d" % i, [128, 1024], F32)) for i in range(4)]
    PSt = [PDt[i // 2][:, (i % 2) * 512:(i % 2 + 1) * 512] for i in range(8)]
    PS = [Buf(t, "ps%d" % i) for i, t in enumerate(PSt)]

    pvt = sb("pvt", [128, NPV], F32)
    cst = sb("cst", [128, 16], F32)
    idt = sb("idt", [128, 128], BF16)
    idf = sb("idf", [128, 128], F32)
    ones = sb("ones", [128, 128], BF16)
    blk = sb("blk", [128, 128], BF16)
    lamt = sb("lamt", [128, 64], F32)
    lams = sb("lams", [128, 4], F32)
    CB = Buf(None, "consts")

    WB = {}

    def cast(name, dst, src, ktn):
        b = Buf(None, name)
        for kt in range(ktn):
            P.dma("pool", b, None, dst[:, kt, :], src[kt * 128:(kt + 1) * 128, :])
        WB[name] = b

    TB = Buf(None, "tables")
    P.dma("pool", CB, None, idt[:], ident[:, :])
    P.dma("pool", TB, None, kaug_b[:, :, :], kaug[:, :, :])
    P.dma("pool", TB, None, qaugp_b[:, :], qaugp[:, :])
    P.dma("pool", TB, None, qaugm_b[:, :], qaugm[:, :])
    bkv = Buf(None, "in_kv")
    for kt in range(16):
        P.dma("pool", bkv, None, wb_in[:, kt, 1024:3072], w_in[kt * 128:(kt + 1) * 128, 1024:3072])
    WB["in_kv"] = bkv
    brest = Buf(None, "in")
    for kt in range(16):
        P.dma("pool", brest, None, wb_in[:, kt, 0:1024], w_in[kt * 128:(kt + 1) * 128, 0:1024])
        P.dma("pool", brest, None, wb_in[:, kt, 3072:N_IN], w_in[kt * 128:(kt + 1) * 128, 3072:N_IN])
    WB["in"] = brest

    P.dma("sp", CB, None, pvt[:], pv[:, :])
    P.dma("sp", CB, None, idf[:], ident[:, :])

    def c_op(eng, fn):
        P.op(eng, [CB], [CB], fn)

    c_op("dve", lambda e: e.memset(cst[:, 0:1], 1e-6))
    c_op("dve", lambda e: e.memset(cst[:, 1:2], 1e-5))
    c_op("dve", lambda e: e.memset(cst[:, 8:9], 0.0))
    c_op("dve", lambda e: e.memset(ones[:], 1.0))
    c_op("dve", lambda e: e.memset(blk[:], 0.0))
    c_op("dve", lambda e: e.memset(blk[0:64, 0:64], 1.0))
    c_op("dve", lambda e: e.memset(blk[64:128, 64:128], 1.0))
    c_op("dve", lambda e: e.tensor_scalar(cst[:, 3:4], pvt[:, PV_GQA:PV_GQA + 1], 0.125, None, ALU.mult))
    c_op("dve", lambda e: e.tensor_copy(cst[:, 4:5], pvt[:, PV_GKA:PV_GKA + 1]))
    c_op("dve", lambda e: e.tensor_scalar(cst[:, 5:6], pvt[:, PV_GQB:PV_GQB + 1], 128.0 ** -0.5, None, ALU.mult))
    c_op("dve", lambda e: e.tensor_copy(cst[:, 6:7], pvt[:, PV_GKB:PV_GKB + 1]))
    c_op("dve", lambda e: e.tensor_scalar(cst[:, 7:8], pvt[:, PV_GSUB:PV_GSUB + 1], 0.8, None, ALU.mult))
    for j in range(2):
        a0 = PV_LAM + 128 * j
        c_op("dve", lambda e, a0=a0: e.tensor_tensor(lamt[:], pvt[:, a0:a0 + 64], pvt[:, a0 + 64:a0 + 128], ALU.mult))
        c_op("dve", lambda e, j=j: e.tensor_reduce(lams[:, j:j + 1], lamt[:], AX.X, ALU.add))
        c_op("act", lambda e, j=j: e.activation(lams[:, 2 + j:3 + j], lams[:, j:j + 1], AF.Exp))
    c_op("dve", lambda e: e.tensor_tensor(lams[:, 0:1], lams[:, 3:4], lams[:, 2:3], ALU.subtract))
    c_op("dve", lambda e: e.tensor_scalar(cst[:, 2:3], lams[:, 0:1], -0.2, None, ALU.add))
    c_op("dve", lambda e: e.tensor_reduce(lams[:, 0:1], pvt[:, PV_GQK:PV_GQK + 64], AX.X, ALU.max, apply_absolute_value=True))
    c_op("dve", lambda e: e.tensor_reduce(lams[:, 1:2], pvt[:, PV_GQK + 64:PV_GQK + 128], AX.X, ALU.max, apply_absolute_value=True))
    c_op("dve", lambda e: e.tensor_tensor(lams[:, 2:3], lams[:, 0:1], lams[:, 1:2], ALU.mult))
    c_op("dve", lambda e: e.tensor_scalar(lams[:, 2:3], lams[:, 2:3], 8.0, -32.0, ALU.mult, ALU.add))
    c_op("dve", lambda e: e.tensor_scalar(lams[:, 2:3], lams[:, 2:3], 0.0, 3.0e38, ALU.max, ALU.mult))
    c_op("dve", lambda e: e.tensor_scalar(lams[:, 2:3], lams[:, 2:3], 3.0e38, 0.0, ALU.mult, ALU.mult))
    c_op("dve", lambda e: e.tensor_copy(cst[:, 9:10], lams[:, 2:3]))
    EPS6 = cst[:, 0:1]
    EPS5 = cst[:, 1:2]
    NEGLAM = cst[:, 2:3]

    def norm_tail(N, ps_z, ps_bs, sqb, lnb, onesmat, inv_n, eps_ap, g_ap, out_b, z_sbuf=None):
        zb = z_sbuf if z_sbuf is not None else ps_z
        P.op("act", [zb], [sqb], lambda e: e.activation(sqb.t[:, 0:N], zb.t[:, 0:N], AF.Square))
        P.op("pe", [sqb, CB], [ps_bs], lambda e: e.matmul(ps_bs.t[:, 0:N], onesmat, sqb.t[:, 0:N], start=True, stop=True))
        P.op("act", [ps_bs, CB], [lnb], lambda e: e.activation(lnb.t[:, 0:N], ps_bs.t[:, 0:N], AF.Ln, bias=eps_ap, scale=inv_n))
        P.op("act", [lnb], [lnb], lambda e: e.activation(lnb.t[:, 0:N], lnb.t[:, 0:N], AF.Exp, scale=-0.5))
        P.op("dve", [zb, lnb, CB], [out_b], lambda e: e.scalar_tensor_tensor(
            out_b.t[:, 0:N], zb.t[:, 0:N], g_ap, lnb.t[:, 0:N], ALU.mult, ALU.mult))

    with ExitStack() as ph:
        def psb(name, shape, dt):
            return ph.enter_context(nc.sbuf_tensor(name, list(shape), dt))
        xt = [Buf(psb("xt%d" % i, [128, D], F32)) for i in range(3)]
        junk = psb("junk", [128, D], BF16)
        ssb = [Buf(psb("ssb%d" % i, [128, 8], F32)) for i in range(2)]
        xsb = [Buf(psb("xsb%d" % i, [128, D], BF16)) for i in range(2)]
        hTt = [psb("hT%d" % i, [128, 16, 512], BF16) for i in range(2)]
        hT = [[Buf(t) for _ in range(2)] for t in hTt]
        gmt = psb("gmt", [128, D], F32)
        GM = Buf(None)
        P.dma("sp", GM, None, gmt[:], gmixb[:, :])
        wch = [Buf(psb("wch%d" % i, [128, 16, 512], BF16)) for i in range(4)]
        zst = [Buf(psb("zst%d" % i, [128, 512], BF16)) for i in range(4)]
        vst = [Buf(psb("vst%d" % i, [128, 512], BF16)) for i in range(3)]
        sqb = [Buf(psb("sqb%d" % i, [128, 512], BF16)) for i in range(2)]
        lnb = [Buf(psb("lnb%d" % i, [128, 512], F32)) for i in range(2)]
        gex = [Buf(psb("gex%d" % i, [128, 512], F32)) for i in range(2)]
        nx_xt, nx_ss, nx_xs, nx_hT = _rot(xt), _rot(ssb), _rot(xsb), _rot(hT)
        nx_zst, nx_vst, nx_sq, nx_ln, nx_gex = _rot(zst), _rot(vst), _rot(sqb), _rot(lnb), _rot(gex)
        ptb = [(PS[i], PSt[i].bitcast(BF16)) for i in range(2)]
        nx_pt = _rot(ptb)
        nx_z = _rot(PS[2:6])
        nx_bs = _rot(PS[6:8])
        evac_i = [0]
        wslot = _rot(wch)

        def load_chunk(ci, slot):
            P.dma("sp", slot, WB["in_kv" if ci in (2, 3, 4, 5) else "in"], slot.t[:], wb_in[:, :, ci * 512:(ci + 1) * 512])

        import os
        _lim = os.environ.get("KDBG", "")

        def prologue(xsrc, r0):
            Hs = nx_hT()
            Ht = Hs[0].t
            ss = nx_ss()
            for b in range(4):
                X = nx_xt()
                P.dma("sp", X, None, X.t[:], xsrc[r0 + b * 128:r0 + (b + 1) * 128, :])
                P.op("act", [X], [ss], lambda e, X=X, b=b: e.activation(
                    junk[:], X.t[:], AF.Square, accum_out=ss.t[:, b:b + 1]))
                P.op("act", [ss, CB], [ss], lambda e, b=b: e.activation(ss.t[:, 4 + b:5 + b], ss.t[:, b:b + 1], AF.Ln, bias=EPS6, scale=1.0 / D))
                P.op("act", [ss], [ss], lambda e, b=b: e.activation(ss.t[:, 4 + b:5 + b], ss.t[:, 4 + b:5 + b], AF.Exp, scale=-0.5))
                S = nx_xs()
                P.op("dve", [X, ss, GM], [S], lambda e, X=X, S=S, b=b: e.scalar_tensor_tensor(
                    S.t[:], X.t[:], ss.t[:, 4 + b:5 + b], gmt[:], ALU.mult, ALU.mult))
                for kh in range(2):
                    pb, pvw = nx_pt()

                    def tr(e, S=S, pvw=pvw, kh=kh):
                        for k in range(8):
                            kt = kh * 8 + k
                            ins = e.transpose(pvw[:, k * 128:(k + 1) * 128], S.t[:, kt * 128:(kt + 1) * 128], idt[:])
                        return ins
                    P.op("pe", [S, CB], [pb], tr)
                    src = pvw[:, :].rearrange("p (k t) -> p k t", k=8)
                    dst = Ht[:, kh * 8:kh * 8 + 8, b * 128:(b + 1) * 128]
                    if kh % 2:
                        P.op("dve", [pb], [Hs[kh]], lambda e, dst=dst, src=src: e.tensor_copy(dst, src))
                    else:
                        P.op("act", [pb], [Hs[kh]], lambda e, dst=dst, src=src: e.activation(dst, src, AF.Copy))
            return Hs

        stream = []
        loaded = {}
        spos = [0]

        last_res = [-1]

        def ensure(k):
            if k < len(stream) and k not in loaded:
                cj, resident = stream[k]
                if cj in resident:
                    loaded[k] = resident[cj]
                else:
                    if spos[0] <= last_res[0] + 1:
                        return
                    sl = wslot()
                    load_chunk(cj, sl)
                    loaded[k] = sl

        def project(Hs, ctx0, b0, q0, qc, chunks, hoist=None, hoist_at=0):
            Ht = Hs[0].t
            for j, ci in enumerate(chunks):
                k = spos[0]
                spos[0] += 1
                ensure(k)
                ensure(k + 1)
                ensure(k + 2)
                W = loaded.pop(k)
                if hoist is not None and j == hoist_at:
                    hoist()
                if ci in (4, 5, 12, 13, 14):
                    for b in range(4):
                        Z = nx_z()

                        def mm(e, W=W, Z=Z, b=b):
                            for kt in range(16):
                                ins = e.matmul(Z.t[:, :], Ht[:, kt, b * 128:(b + 1) * 128], W.t[:, kt, :], start=(kt == 0), stop=(kt == 15))
                            return ins
                        P.op("pe", Hs + [W], [Z], mm)
                        V = nx_vst()
                        evac_i[0] += 1
                        if evac_i[0] % 2:
                            P.op("dve", [Z], [V], lambda e, V=V, Z=Z: e.tensor_copy(V.t[:], Z.t[:]))
                        else:
                            P.op("act", [Z], [V], lambda e, V=V, Z=Z: e.activation(V.t[:], Z.t[:], AF.Copy))
                        if ci in (4, 5):
                            h0 = 4 * (ci - 4)
                            dst = VAd[h0:h0 + 4, ctx0 + b * 128:ctx0 + (b + 1) * 128, :].rearrange("h t e -> t h e")
                        else:
                            h0 = 4 * (ci - 12)
                            dst = VBd[h0:h0 + 4, b0 + b * 128:b0 + (b + 1) * 128, :].rearrange("h t e -> t h e")
                        P.dma("sp", None, V, dst, V.t[:].rearrange("p (h e) -> p h e", h=4))
                    continue
                for f in range(4):
                    ft = ci * 4 + f
                    isq = ft < 8 or 24 <= ft < 36 or ft >= 60
                    c0, c1 = qc if isq else (0, 512)
                    N = c1 - c0
                    Z = nx_z()

                    def mm(e, W=W, Z=Z, f=f, c0=c0, c1=c1, N=N):
                        for kt in range(16):
                            ins = e.matmul(Z.t[:, 0:N], W.t[:, kt, f * 128:(f + 1) * 128], Ht[:, kt, c0:c1], start=(kt == 0), stop=(kt == 15))
                        return ins
                    P.op("pe", Hs + [W], [Z], mm)
                    O = nx_zst()
                    if ft >= 60:
                        G = nx_gex()
                        P.op("act", [Z], [G], lambda e, G=G, Z=Z, N=N: e.activation(G.t[:, 0:N], Z.t[:, 0:N], AF.Exp, scale=-1.0))
                        P.op("dve", [G], [G], lambda e, G=G, N=N: e.tensor_scalar(G.t[:, 0:N], G.t[:, 0:N], 1.0, None, ALU.add))
                        P.op("dve", [G], [O], lambda e, G=G, O=O, N=N: e.reciprocal(O.t[:, 0:N], G.t[:, 0:N]))
                        dst = GTd[ft - 60, :, q0:q0 + N]
                    else:
                        if ft < 8:
                            om, inv, g, dst = blk[:], 1.0 / 64, cst[:, 3:4], QAd[ft, :, q0:q0 + N]
                        elif ft < 16:
                            om, inv, g, dst = blk[:], 1.0 / 64, cst[:, 4:5], KAd[ft - 8, :, ctx0:ctx0 + 512]
                        elif ft < 36:
                            om, inv, g, dst = ones[:], 1.0 / 128, cst[:, 5:6], QBd[ft - 24, :, q0:q0 + N]
                        else:
                            om, inv, g, dst = ones[:], 1.0 / 128, cst[:, 6:7], KBd[ft - 36, :, b0:b0 + 512]
                        norm_tail(N, Z, nx_bs(), nx_sq(), nx_ln(), om, inv, EPS6, g, O)
                    P.dma("sp", None, O, dst, O.t[:, 0:N])

        KV = [2, 3, 4, 5]
        KVB = [2, 3, 4, 5, 9, 10, 11, 12, 13, 14]
        ALLC = list(range(23))
        res = {}
        for ci in KV:
            sl = wslot()
            load_chunk(ci, sl)
            res[ci] = sl
        TD = []
        for T in range(14, 32):
            TD.append((xp, T * 512, T * 512, None, None, None, KV, res))
        for T in range(0, 14):
            if 3 <= T <= 10:
                TD.append((xp, T * 512, T * 512, T * 512, (T - 3) * 512, (0, 512), ALLC, {}))
            elif T == 2:
                TD.append((xp, T * 512, T * 512, T * 512, QL, (511, 512), ALLC, {}))
            elif T == 11:
                TD.append((xp, T * 512, T * 512, T * 512, QR, (0, 1), ALLC, {}))
            else:
                TD.append((xp, T * 512, T * 512, T * 512, None, None, KVB, {}))
        for T in range(4):
            TD.append((xs, T * 512, SEQ + T * 512, BWIN + T * 512, QS0 + T * 512, (0, 512), ALLC, {}))
        for d in TD:
            for ci in d[6]:
                if ci in d[7]:
                    last_res[0] = len(stream)
                stream.append((ci, d[7]))
        nextH = [prologue(TD[0][0], TD[0][1])]
        for i, d in enumerate(TD):
            Hcur = nextH[0]

            def hoist(i=i):
                if i + 1 < len(TD):
                    nextH[0] = prologue(TD[i + 1][0], TD[i + 1][1])
            project(Hcur, d[2], d[3], d[4], d[5], d[6], hoist=hoist, hoist_at=min(1, len(d[6]) - 1))
        P.barrier()
        P.release(xt + wch + zst + vst + [GM])

    cast("pa", wb_pa, w_pa, 8)
    cast("pb", wb_pb, w_pb, 4)
    cast("o", wb_o, w_o, 16)
    cast("up", wb_up, w_up, 16)
    cast("dn", wb_dn, w_dn, 44)

    if dbg == 1:
        return nc, es

    import os
    _lim2 = os.environ.get("KDBG2", "")
    QT = [("p", t, t * 512, 512) for t in range(8)] + [("h", 0, QL, 2)] + [("s", t, QS0 + t * 512, 512) for t in range(4)]
    with ExitStack() as ph:
        def psb(name, shape, dt):
            return ph.enter_context(nc.sbuf_tensor(name, list(shape), dt))
        KT = [Buf(psb("KT%d" % m, [68, NCTX], BF16)) for m in range(2)]
        VT = Buf(psb("VT", [128, 144, 128], BF16))
        QSb = [Buf(psb("QS%d" % i, [68, 6, 512], BF16)) for i in range(2)]
        PT = [Buf(psb("PT%d" % i, [128, 2, 512], BF16)) for i in range(6)]
        PSm = [Buf(psb("PSm%d" % i, [128, 2, 512], BF16)) for i in range(3)]
        nx_PSm = _rot(PSm)
        SBf = [Buf(psb("SBf%d" % i, [128, 2, 512], F32)) for i in range(2)]
        absat = psb("absat", [128, 896], F32)
        lr = [Buf(psb("lr%d" % i, [128, 512], F32)) for i in range(2)]
        o12 = [Buf(psb("o12%d" % i, [128, 512], F32)) for i in range(2)]
        ob = Buf(psb("ob", [128, 512], F32))
        sq2 = Buf(psb("sq2", [128, 512], BF16))
        ln2 = Buf(psb("ln2", [128, 512], F32))
        oast = [Buf(psb("oast%d" % i, [128, 512], BF16)) for i in range(2)]
        nx_QS, nx_PT, nx_SB, nx_oast = _rot(QSb), _rot(PT), _rot(SBf), _rot(oast)
        nx_sc = _rot(PS[0:4])
        nx_scp = _rot([(PS[0], PS[1], PDt[0]), (PS[2], PS[3], PDt[1])])
        OB_, LB_ = PS[4:6], PS[6:8]
        AT = Buf(None)
        P.dma("sp", AT, None, absat[:], absa[:, :])
        for q in QSb:
            P.op("dve", [], [q], lambda e, q=q: e.memset(q.t[64:68, 4:6, :], 0.0))

        QTL = [q for i, q in enumerate(QT) if not (_lim2 and i not in (0, 1, 8, 12))]
        KTc = [[Buf(KT[m].t) for _ in range(4)] for m in range(2)]
        VTc = [Buf(VT.t) for _ in range(4)]

        def load_kv(hh, cs):
            for c4 in cs:
                k0, k1 = c4 * 4608, (c4 + 1) * 4608
                for m in range(2):
                    P.dma("sp", KTc[m][c4], None, KT[m].t[0:64, k0:k1], KAd[hh, m * 64:(m + 1) * 64, k0:k1])
                    P.dma("sp", KTc[m][c4], TB, KT[m].t[64:68, k0:k1], kaug_b[hh, :, k0:k1])
                P.dma("sp", VTc[c4], None, VT.t[:, c4 * 36:(c4 + 1) * 36, :],
                      VAd[hh, k0:k1, :].rearrange("(kt p) e -> p kt e", p=128))
        qitems = [(hh, qq) for hh in range(8) for qq in QTL]
        qld = {}

        def ens_q(k):
            if k < len(qitems) and k not in qld:
                hh, (kind_, t_, q0_, N_) = qitems[k]
                Qb = nx_QS()
                qsrc = QAd[hh, :, q0_:q0_ + N_].rearrange("(m d) q -> d m q", m=2)
                for v in range(3):
                    P.dma("sp", Qb, None, Qb.t[0:64, 2 * v:2 * v + 2, 0:N_], qsrc)
                for m in range(2):
                    P.dma("sp", Qb, TB, Qb.t[64:68, m, 0:N_], qaugp_b[:, q0_:q0_ + N_])
                    P.dma("sp", Qb, TB, Qb.t[64:68, 2 + m, 0:N_], qaugm_b[:, q0_:q0_ + N_])
                qld[k] = Qb
        load_kv(0, [0, 1, 2, 3])
        for h in range(8):
            for qi_, (kind, t, q0, N) in enumerate(QTL):
                kq = h * len(QTL) + qi_
                ens_q(kq)
                Q = qld.pop(kq)
                ens_q(kq + 1)
                if kind == "s" and t == (3 if _lim2 else 0) and h + 1 < 8:
                    load_kv(h + 1, [0, 1, 2])
                if kind == "s":
                    kts = list(range(128, 144))
                else:
                    kts = list(range(0, 128))
                tiles = []
                for kt in kts:
                    if ALIBI_CUT is not None:
                        if kind == "s":
                            klo, qlo, qhi, per = (kt - 128) * 128, t * 512, t * 512 + 511, None
                            qs_ = [(qlo, qhi)]
                        elif kind == "p":
                            klo, per = kt * 128, SEQ
                            qs_ = [(OWN0 + t * 512, OWN0 + t * 512 + 511)]
                        else:
                            klo, per = kt * 128, SEQ
                            qs_ = [(OWN0 - 1, OWN0 - 1), (OWN1, OWN1)]
                        khi = klo + 127
                        gap = None
                        for (qlo, qhi) in qs_:
                            for sh in ((0,) if per is None else (0, per, -per)):
                                g_ = max(0, klo + sh - qhi, qlo - (khi + sh))
                                gap = g_ if gap is None else min(gap, g_)
                        if SLOPE_A[h] * gap > ALIBI_CUT:
                            continue
                    if kind == "p":
                        d0 = 12 + 4 * t
                        if kt < 12 or kt >= 44 or kt < d0:
                            var, dk = 0, None
                        elif kt < d0 + 4:
                            var, dk = 2, kt - d0
                        else:
                            var, dk = 1, None
                    elif kind == "h":
                        var, dk = (1 if 12 <= kt < 44 else 0), None
                    else:
                        d0 = 128 + 4 * t
                        if kt < d0:
                            var, dk = 0, None
                        elif kt < d0 + 4:
                            var, dk = 2, kt - d0
                        else:
                            var, dk = 1, None
                    tiles.append((kt, var, dk))
                nt = len(tiles)
                DEPTH = 2
                pts = [None] * nt
                sums = [None] * nt
                for i in range(nt + DEPTH):
                    if i < nt:
                        kt, var, dk = tiles[i]
                        SCa, SCb, PDp = nx_scp()
                        SCm = (SCa, SCb)
                        for m in range(2):
                            P.op("pe", [KTc[m][kt // 36], Q], [SCm[m]], lambda e, m=m, kt=kt, var=var, Q=Q, SCm=SCm: e.matmul(
                                SCm[m].t[:, 0:N], KT[m].t[0:68, kt * 128:(kt + 1) * 128], Q.t[0:68, 2 * var + m, 0:N], start=True, stop=True))
                        if dk is not None:
                            S2 = nx_SB()
                            off = 384 - 128 * dk
                            for m in range(2):
                                P.op("dve", [SCm[m], AT], [S2], lambda e, S2=S2, m=m, SCm=SCm, off=off: e.scalar_tensor_tensor(
                                    S2.t[:, m, 0:N], absat[:, off:off + N], -SLOPE_A[h], SCm[m].t[:, 0:N], ALU.mult, ALU.add))
                            srcb = [S2]
                            src_ap = S2.t[:, :, 0:N]
                        else:
                            srcb = [SCa, SCb]
                            src_ap = PDp[:, :].rearrange("p (m n) -> p m n", m=2)[:, :, 0:N]
                        Pt = nx_PT()
                        P.op("act", srcb, [Pt], lambda e, Pt=Pt, src_ap=src_ap: e.activation(Pt.t[:, :, 0:N], src_ap, AF.Exp))
                        pts[i] = Pt
                        if i % 2 == 1:
                            Sm = nx_PSm()
                            eng = "dve"
                            Pa = pts[i - 1]
                            P.op(eng, [Pa, Pt], [Sm], lambda e, Sm=Sm, Pa=Pa, Pt=Pt: e.tensor_tensor(
                                Sm.t[:, :, 0:N], Pa.t[:, :, 0:N], Pt.t[:, :, 0:N], ALU.add))
                            sums[i] = Sm
                    j = i - DEPTH
                    if j >= 0:
                        kt, var, dk = tiles[j]
                        Pt = pts[j]
                        st = (j == 0)
                        sp_ = (j == nt - 1)
                        for m in range(2):
                            P.op("pe", [VTc[kt // 36], Pt], [OB_[m]], lambda e, m=m, kt=kt, Pt=Pt, st=st, sp_=sp_: e.matmul(
                                OB_[m].t[:, 0:N], VT.t[:, kt, :], Pt.t[:, m, 0:N], start=st, stop=sp_))
                        if j % 2 == 1 or j == nt - 1:
                            Ls = sums[j] if j % 2 == 1 else Pt
                            lst = (j <= 1)
                            for m in range(2):
                                P.op("pe", [CB, Ls], [LB_[m]], lambda e, m=m, Ls=Ls, lst=lst, sp_=sp_: e.matmul(
                                    LB_[m].t[:, 0:N], ones[:], Ls.t[:, m, 0:N], start=lst, stop=sp_))
                for m in range(2):
                    P.op("act", [LB_[m]], [lr[m]], lambda e, m=m: e.activation(lr[m].t[:, 0:N], LB_[m].t[:, 0:N], AF.Copy))
                    P.op("dve", [OB_[m]], [o12[m]], lambda e, m=m: e.tensor_copy(o12[m].t[:, 0:N], OB_[m].t[:, 0:N]))
                for m in range(2):
                    P.op("dve", [lr[m]], [lr[m]], lambda e, m=m: e.reciprocal(lr[m].t[:, 0:N], lr[m].t[:, 0:N]))
                    P.op("dve", [o12[m], lr[m]], [o12[m]], lambda e, m=m: e.tensor_tensor(
                        o12[m].t[:, 0:N], o12[m].t[:, 0:N], lr[m].t[:, 0:N], ALU.mult))
                P.op("dve", [o12[0], o12[1], CB], [ob], lambda e: e.scalar_tensor_tensor(
                    ob.t[:, 0:N], o12[1].t[:, 0:N], NEGLAM, o12[0].t[:, 0:N], ALU.mult, ALU.add))
                OA = nx_oast()
                norm_tail(N, None, nx_sc(), sq2, ln2, ones[:], 1.0 / 128, EPS5, cst[:, 7:8], OA, z_sbuf=ob)
                P.dma("sp", None, OA, OAd[h, :, q0:q0 + N], OA.t[:, 0:N])
            if h + 1 < 8:
                load_kv(h + 1, [3])
        P.barrier()
        P.release(KTc[0] + KTc[1] + VTc + QSb + oast + [AT])

    if dbg == 2:
        return nc, es

    QTB = [(t * 512, 512, OWN0 + t * 512, 0, 56) for t in range(8)]
    QTB += [(QL, 1, OWN0 - 1, 0, 56), (QR, 1, OWN1, 0, 56)]
    QTB += [(QS0 + t * 512, 512, t * 512, 56, 72) for t in range(4)]
    with ExitStack() as ph:
        def psb(name, shape, dt):
            return ph.enter_context(nc.sbuf_tensor(name, list(shape), dt))
        KBT = Buf(psb("KBT", [128, 3, NB], BF16))
        VBT = Buf(psb("VBT", [128, 3, 72, 128], BF16))
        QBT = [Buf(psb("QBT%d" % i, [128, 3, 512], BF16)) for i in range(2)]
        dbs = psb("dbs", [128, DB_TOT], BF16)
        vbs = psb("vbs", [128, 72], F32)
        PT = [Buf(psb("PTb%d" % i, [128, 512], BF16)) for i in range(6)]
        SBf = [Buf(psb("SBb%d" % i, [128, 512], F32)) for i in range(4)]
        lrb = Buf(psb("lrb", [128, 512], F32))
        obf = Buf(psb("obf", [128, 512], F32))
        obst = [Buf(psb("obst%d" % i, [128, 512], BF16)) for i in range(2)]
        nx_QB, nx_PT, nx_SB, nx_obst = _rot(QBT), _rot(PT), _rot(SBf), _rot(obst)
        nx_sc = _rot(PS[0:4])
        OBk, LBk = PS[4], PS[6]
        DT = Buf(None)
        P.dma("pool", DT, None, dbs[:], dbt[:, :])
        P.dma("sp", DT, None, vbs[:], valb[:, :])
        KBg = [Buf(KBT.t) for _ in range(3)]
        VBg = [Buf(VBT.t) for _ in range(3)]
        qbitems = [(hh, qq) for hh in range(4) for qq in QTB]
        qbl = {}

        def ens_qb(k):
            if k < len(qbitems) and k not in qbl:
                hh, (q0_, N_, _a, _b, _c) = qbitems[k]
                Qb = nx_QB()
                for g in range(3):
                    P.dma("sp", Qb, None, Qb.t[:, g, 0:N_], QBd[g * 4 + hh, :, q0_:q0_ + N_])
                qbl[k] = Qb
        for h in range(4):
            for g in range(3):
                P.dma("sp", KBg[g], None, KBT.t[:, g, :], KBd[g * 4 + h, :, :])
                for c2 in range(2):
                    P.dma("sp", VBg[g], None, VBT.t[:, g, c2 * 36:(c2 + 1) * 36, :],
                          VBd[g * 4 + h, c2 * 4608:(c2 + 1) * 4608, :].rearrange("(kt p) e -> p kt e", p=128))
            for qi_, (q0, N, qpos, klo, khi) in enumerate(QTB):
                kq = h * len(QTB) + qi_
                ens_qb(kq)
                Q = qbl.pop(kq)
                ens_qb(kq + 1)
                tiles = []
                for g in range(3):
                    wl = 64 * DIL[g]
                    for kt in range(klo, khi):
                        kpos = (kt - klo) * 128
                        dbase = kpos - qpos
                        if dbase - (N - 1) <= wl and dbase + 127 >= -wl:
                            tiles.append((g, kt, DB_OFF[g] + DB_C0[g] - dbase))
                nt = len(tiles)
                DEPTH = 3
                pts = [None] * nt
                for i in range(nt + DEPTH):
                    if i < nt:
                        g, kt, off = tiles[i]
                        SC = nx_sc()
                        P.op("pe", [KBg[g], Q], [SC], lambda e, SC=SC, g=g, kt=kt, Q=Q: e.matmul(
                            SC.t[:, 0:N], KBT.t[:, g, kt * 128:(kt + 1) * 128], Q.t[:, g, 0:N], start=True, stop=True))
                        S2 = nx_SB()
                        sl = -SLOPE_B[g * 4 + h]
                        P.op("dve", [SC, DT], [S2], lambda e, S2=S2, SC=SC, off=off, sl=sl: e.scalar_tensor_tensor(
                            S2.t[:, 0:N], dbs[:, off:off + N], sl, SC.t[:, 0:N], ALU.mult, ALU.add))
                        Pt = nx_PT()
                        P.op("act", [S2, DT], [Pt], lambda e, Pt=Pt, S2=S2, kt=kt: e.activation(
                            Pt.t[:, 0:N], S2.t[:, 0:N], AF.Exp, bias=vbs[:, kt:kt + 1]))
                        pts[i] = Pt
                    j = i - DEPTH
                    if j >= 0:
                        g, kt, off = tiles[j]
                        Pt = pts[j]
                        P.op("pe", [VBg[g], Pt], [OBk], lambda e, g=g, kt=kt, Pt=Pt, j=j: e.matmul(
                            OBk.t[:, 0:N], VBT.t[:, g, kt, :], Pt.t[:, 0:N], start=(j == 0), stop=(j == nt - 1)))
                        P.op("pe", [CB, Pt], [LBk], lambda e, Pt=Pt, j=j: e.matmul(
                            LBk.t[:, 0:N], ones[:], Pt.t[:, 0:N], start=(j == 0), stop=(j == nt - 1)))
                P.op("act", [LBk], [lrb], lambda e: e.activation(lrb.t[:, 0:N], LBk.t[:, 0:N], AF.Copy))
                P.op("dve", [OBk], [obf], lambda e: e.tensor_copy(obf.t[:, 0:N], OBk.t[:, 0:N]))
                P.op("dve", [lrb], [lrb], lambda e: e.reciprocal(lrb.t[:, 0:N], lrb.t[:, 0:N]))
                OO = nx_obst()
                P.op("dve", [obf, lrb], [OO], lambda e, OO=OO: e.tensor_tensor(OO.t[:, 0:N], obf.t[:, 0:N], lrb.t[:, 0:N], ALU.mult))
                P.dma("sp", None, OO, OBd[h, :, q0:q0 + N], OO.t[:, 0:N])
        P.barrier()
        P.release(KBg + VBg + [DT] + QBT + obst)

    if dbg == 3:
        return nc, es

    with ExitStack() as ph:
        def psb(name, shape, dt):
            return ph.enter_context(nc.sbuf_tensor(name, list(shape), dt))
        wo = psb("wo", [128, 16, D], BF16)
        gft = psb("gft", [128, D], F32)
        W3 = Buf(None)
        P.dma("sp", W3, WB["o"], wo[:], wb_o[:, :, :])
        P.dma("sp", W3, None, gft[:], gffnb[:, :])
        wpc = [Buf(psb("wpc%d" % i, [128, 12, 256], BF16)) for i in range(2)]
        oin = [Buf(psb("oin%d" % i, [128, 12, 512], BF16)) for i in range(2)]
        gin = [Buf(psb("gin%d" % i, [128, 2, 512], BF16)) for i in range(2)]
        mT = Buf(psb("mT", [128, 16, 512], BF16))
        t1 = [Buf(psb("t1%d" % i, [128, 512], F32)) for i in range(2)]
        t2 = [Buf(psb("t2%d" % i, [128, 512], F32)) for i in range(2)]
        xb = [Buf(psb("xb%d" % i, [128, D], F32)) for i in range(2)]
        x1b = [Buf(psb("x1b%d" % i, [128, D], F32)) for i in range(2)]
        s3 = [Buf(psb("s3%d" % i, [128, 4], F32)) for i in range(2)]
        h2s = [Buf(psb("h2s%d" % i, [128, D], BF16)) for i in range(2)]
        h2t = [Buf(psb("h2t%d" % i, [128, 16, 128], BF16)) for i in range(2)]
        zt = Buf(psb("zt", [128, 16, 2], BF16))
        nx_wpc, nx_oin, nx_gin, nx_t1, nx_t2 = _rot(wpc), _rot(oin), _rot(gin), _rot(t1), _rot(t2)
        nx_xb, nx_x1b, nx_s3, nx_h2s, nx_h2t = _rot(xb), _rot(x1b), _rot(s3), _rot(h2s), _rot(h2t)
        nx_pa, nx_pbk = _rot(PS[0:2]), _rot(PS[2:4])
        nx_xo = _rot(PS[4:6])
        ptb = [(PS[i], PSt[i].bitcast(BF16)) for i in (6, 7)]
        nx_pt = _rot(ptb)
        P.op("dve", [], [zt], lambda e: e.memset(zt.t[:], 0.0))
        P.dma("sp", None, zt, H2d[:, :, CVS0:CVS0 + 1], zt.t[:, :, 0:1])
        P.dma("sp", None, zt, H2d[:, :, NCV - 1:NCV], zt.t[:, :, 1:2])
        blk_i = [0]

        QT3 = [(t * 512, 512, "p", t) for t in range(8)] + [(QL, 2, "h", 0)] + [(QS0 + t * 512, 512, "s", t) for t in range(4)]
        if _lim2:
            QT3 = [QT3[i] for i in (0, 1, 8, 12)]
        oil = {}

        def ens_oi(k):
            if k < len(QT3) and k not in oil:
                q0_, N_, _k, _t = QT3[k]
                OIb = nx_oin()
                P.dma("sp", OIb, None, OIb.t[:, 0:8, 0:N_], OAd[:, :, q0_:q0_ + N_].rearrange("h p q -> p h q"))
                P.dma("sp", OIb, None, OIb.t[:, 8:12, 0:N_], OBd[:, :, q0_:q0_ + N_].rearrange("h p q -> p h q"))
                oil[k] = OIb

        for qi_, (q0, N, kind, t) in enumerate(QT3):
            ens_oi(qi_)
            OI = oil.pop(qi_)
            ens_oi(qi_ + 1)
            for fo in range(16):
                if fo % 2 == 0:
                    WP = nx_wpc()
                    cg_ = fo // 2
                    P.dma("sp", WP, WB["pa"], WP.t[:, 0:8, :], wb_pa[:, :, cg_ * 256:(cg_ + 1) * 256])
                    P.dma("sp", WP, WB["pb"], WP.t[:, 8:12, :], wb_pb[:, :, cg_ * 256:(cg_ + 1) * 256])
                fl = fo % 2
                GI = nx_gin()
                P.dma("sp", GI, None, GI.t[:, 0, 0:N], GTd[fo, :, q0:q0 + N])
                P.dma("sp", GI, None, GI.t[:, 1, 0:N], GTd[16 + fo, :, q0:q0 + N])
                A = nx_pa()
                Bk = nx_pbk()

                def mma(e, A=A, fl=fl, OI=OI, WP=WP):
                    for kt in range(8):
                        ins = e.matmul(A.t[:, 0:N], WP.t[:, kt, fl * 128:(fl + 1) * 128], OI.t[:, kt, 0:N], start=(kt == 0), stop=(kt == 7))
                    return ins

                def mmb(e, Bk=Bk, fl=fl, OI=OI, WP=WP):
                    for kt in range(4):
                        ins = e.matmul(Bk.t[:, 0:N], WP.t[:, 8 + kt, fl * 128:(fl + 1) * 128], OI.t[:, 8 + kt, 0:N], start=(kt == 0), stop=(kt == 3))
                    return ins
                P.op("pe", [WP, OI], [A], mma)
                P.op("pe", [WP, OI], [Bk], mmb)
                T1, T2 = nx_t1(), nx_t2()
                P.op("dve", [A, GI], [T1], lambda e, T1=T1, A=A, GI=GI: e.tensor_tensor(T1.t[:, 0:N], A.t[:, 0:N], GI.t[:, 0, 0:N], ALU.mult))
                P.op("dve", [Bk, GI], [T2], lambda e, T2=T2, Bk=Bk, GI=GI: e.tensor_tensor(T2.t[:, 0:N], Bk.t[:, 0:N], GI.t[:, 1, 0:N], ALU.mult))
                P.op("pool", [T1, T2], [mT], lambda e, T1=T1, T2=T2, fo=fo: e.tensor_tensor(mT.t[:, fo, 0:N], T1.t[:, 0:N], T2.t[:, 0:N], ALU.add))
            nblk = (N + 127) // 128

            def x1_part(b):
                nb_ = min(128, N - b * 128)
                XB = nx_xb()
                if kind == "p":
                    r0 = OWN0 + t * 512 + b * 128
                    P.dma("sp", XB, None, XB.t[0:nb_, :], xp[r0:r0 + nb_, :])
                    cvs = [(1 + t * 512 + b * 128, 0, nb_)]
                elif kind == "s":
                    r0 = t * 512 + b * 128
                    P.dma("sp", XB, None, XB.t[0:nb_, :], xs[r0:r0 + nb_, :])
                    cvs = [(CVS0 + 1 + t * 512 + b * 128, 0, nb_)]
                else:
                    P.dma("sp", XB, None, XB.t[0:1, :], xp[OWN0 - 1:OWN0, :])
                    P.dma("sp", XB, None, XB.t[1:2, :], xp[OWN1:OWN1 + 1, :])
                    cvs = [(0, 0, 1), (4097, 1, 1)]
                X1 = nx_x1b()
                for cc in range(4):
                    XO = nx_xo()

                    def mmo(e, XO=XO, cc=cc, b=b, nb_=nb_):
                        for kt in range(16):
                            ins = e.matmul(XO.t[0:nb_, :], mT.t[:, kt, b * 128:b * 128 + nb_], wo[:, kt, cc * 512:(cc + 1) * 512], start=(kt == 0), stop=(kt == 15))
                        return ins
                    P.op("pe", [mT, W3], [XO], mmo)
                    P.op("dve", [XO, XB], [X1], lambda e, X1=X1, XO=XO, XB=XB, cc=cc, nb_=nb_: e.tensor_tensor(
                        X1.t[0:nb_, cc * 512:(cc + 1) * 512], XO.t[0:nb_, :], XB.t[0:nb_, cc * 512:(cc + 1) * 512], ALU.add))
                S3 = nx_s3()
                HS = nx_h2s()
                P.op("act", [X1], [S3, HS], lambda e, X1=X1, S3=S3, HS=HS, nb_=nb_: e.activation(HS.t[0:nb_, :], X1.t[0:nb_, :], AF.Square, accum_out=S3.t[0:nb_, 0:1]))
                P.op("act", [S3, CB], [S3], lambda e, S3=S3, nb_=nb_: e.activation(S3.t[0:nb_, 1:2], S3.t[0:nb_, 0:1], AF.Ln, bias=cst[0:nb_, 0:1], scale=1.0 / D))
                P.op("act", [S3], [S3], lambda e, S3=S3, nb_=nb_: e.activation(S3.t[0:nb_, 1:2], S3.t[0:nb_, 1:2], AF.Exp, scale=-0.5))
                if kind != "h":
                    P.dma("sp", None, X1, X1d[cvs[0][0]:cvs[0][0] + nb_, :], X1.t[0:nb_, :])
                P.op("dve", [X1, S3, W3], [HS], lambda e, HS=HS, X1=X1, S3=S3, nb_=nb_: e.scalar_tensor_tensor(
                    HS.t[0:nb_, :], X1.t[0:nb_, :], S3.t[0:nb_, 1:2], gft[0:nb_, :], ALU.mult, ALU.mult))
                return (HS, nb_, cvs)

            def tr_part(st_):
                HS, nb_, cvs = st_
                HT = nx_h2t()
                blk_i[0] += 1
                for kh in range(2):
                    pb, pvw = nx_pt()

                    def tr(e, HS=HS, pvw=pvw, kh=kh, nb_=nb_):
                        for k in range(8):
                            kt = kh * 8 + k
                            ins = e.transpose(pvw[:, k * 128:k * 128 + nb_], HS.t[0:nb_, kt * 128:(kt + 1) * 128], idt[0:nb_, 0:nb_])
                        return ins
                    P.op("pe", [HS, CB], [pb], tr)
                    src = pvw[:, :].rearrange("p (k t) -> p k t", k=8)[:, :, 0:nb_]
                    dst = HT.t[:, kh * 8:kh * 8 + 8, 0:nb_]
                    if kind == "h":
                        for k in range(8):
                            P.op("dve", [pb, CB], [HT], lambda e, k=k, kh=kh, pvw=pvw, HT=HT: e.tensor_tensor(
                                HT.t[:, kh * 8 + k, 0:2], pvw[:, k * 128:k * 128 + 2], pvt[:, PV_HM:PV_HM + 2], ALU.mult))
                    elif blk_i[0] % 2:
                        P.op("dve", [pb], [HT], lambda e, dst=dst, src=src: e.tensor_copy(dst, src))
                    else:
                        P.op("act", [pb], [HT], lambda e, dst=dst, src=src: e.activation(dst, src, AF.Copy))
                for (cv0, c0, n) in cvs:
                    P.dma("sp", None, HT, H2d[:, :, cv0:cv0 + n], HT.t[:, :, c0:c0 + n])

            prev = None
            for b in range(nblk):
                cur = x1_part(b)
                if prev is not None:
                    tr_part(prev)
                prev = cur
            tr_part(prev)
        P.barrier()
        P.release([W3, zt] + wpc + oin + gin + xb + x1b + h2t)

    if dbg == 4:
        return nc, es

    WIN = [(510 * k, min(512, 4098 - 510 * k), "p") for k in range(9)]
    WIN += [(CVS0 + 510 * k, min(512, 2050 - 510 * k), "s") for k in range(5)]
    if _lim2:
        WIN = [WIN[i] for i in (0, 1, 13)]
    with ExitStack() as ph:
        def psb(name, shape, dt):
            return ph.enter_context(nc.sbuf_tensor(name, list(shape), dt))
        hw = [Buf(psb("hw%d" % i, [128, 16, 512], BF16)) for i in range(2)]
        wu = [Buf(psb("wu%d" % i, [128, 2, 16, 128], BF16)) for i in range(3)]
        gT = Buf(psb("gT", [128, 44, 512], BF16))
        wd = [Buf(psb("wd%d" % i, [128, 44, 256], BF16)) for i in range(2)]
        cg = [Buf(psb("cg%d" % i, [128, 512], F32)) for i in range(2)]
        cv_ = [Buf(psb("cv%d" % i, [128, 512], F32)) for i in range(2)]
        ge = [Buf(psb("ge%d" % i, [128, 512], F32)) for i in range(2)]
        x1r = [Buf(psb("x1r%d" % i, [128, 4, 256], F32)) for i in range(2)]
        yb = [Buf(psb("yb%d" % i, [128, 256], F32)) for i in range(3)]
        nx_hw, nx_wu, nx_wd, nx_cg, nx_cv, nx_ge = _rot(hw), _rot(wu), _rot(wd), _rot(cg), _rot(cv_), _rot(ge)
        nx_x1r, nx_yb = _rot(x1r), _rot(yb)
        nx_ug, nx_uv, nx_dn = _rot(PS[0:2]), _rot(PS[2:4]), _rot(PS[4:8])

        def cw(j, ft):
            c = PV_CW + j * 88 + ft
            return pvt[:, c:c + 1]

        def cb(ft):
            c = PV_CB + ft
            return pvt[:, c:c + 1]

        hwl = {}

        def ens_hw(w):
            if w < len(WIN) and w not in hwl:
                c0_, n_, _k = WIN[w]
                b_ = nx_hw()
                P.dma("sp", b_, None, b_.t[:, :, 0:n_], H2d[:, :, c0_:c0_ + n_])
                hwl[w] = b_
        wul = {}

        def ens_wu(k):
            if k < 44 * len(WIN) and k not in wul:
                j_ = k % 44
                s_ = nx_wu()
                P.dma("sp", s_, WB["up"], s_.t[:, 0, :, :], wb_up[:, :, j_ * 128:(j_ + 1) * 128])
                P.dma("sp", s_, WB["up"], s_.t[:, 1, :, :], wb_up[:, :, DFF + j_ * 128:DFF + (j_ + 1) * 128])
                wul[k] = s_
        wdl = {}

        def ens_wd(k):
            if k < 8 * len(WIN) and k not in wdl:
                w_, c8_ = k // 8, k % 8
                c0_, n_, _k = WIN[w_]
                no_ = n_ - 2
                s_ = nx_wd()
                P.dma("sp", s_, WB["dn"], s_.t[:], wb_dn[:, :, c8_ * 256:(c8_ + 1) * 256])
                xr_ = nx_x1r()
                for b_ in range((no_ + 127) // 128):
                    nb2 = min(128, no_ - b_ * 128)
                    cvr_ = c0_ + 1 + b_ * 128
                    P.dma("sp", xr_, None, xr_.t[0:nb2, b_, :], X1d[cvr_:cvr_ + nb2, c8_ * 256:(c8_ + 1) * 256])
                wdl[k] = (s_, xr_)

        for wi, (c0, n, kind) in enumerate(WIN):
            no = n - 2
            ens_hw(wi)
            HW = hwl.pop(wi)
            ens_hw(wi + 1)
            for j in range(44):
                k = wi * 44 + j
                ens_wu(k)
                ens_wu(k + 1)
                ens_wu(k + 2)
                WU = wul.pop(k)
                UG, UV = nx_ug(), nx_uv()
                for half, U in ((0, UG), (1, UV)):
                    def mmu(e, U=U, half=half, WU=WU):
                        for kt in range(16):
                            ins = e.matmul(U.t[:, 0:n], WU.t[:, half, kt, :], HW.t[:, kt, 0:n], start=(kt == 0), stop=(kt == 15))
                        return ins
                    P.op("pe", [WU, HW], [U], mmu)
                CG, CV = nx_cg(), nx_cv()
                for U, C, ft in ((UG, CG, j), (UV, CV, 44 + j)):
                    P.op("dve", [U, CB], [C], lambda e, U=U, C=C, ft=ft: e.tensor_scalar(
                        C.t[:, 0:no], U.t[:, 0:no], cw(0, ft), cb(ft), ALU.mult, ALU.add))
                    P.op("dve", [U, C, CB], [C], lambda e, U=U, C=C, ft=ft: e.scalar_tensor_tensor(
                        C.t[:, 0:no], U.t[:, 1:no + 1], cw(1, ft), C.t[:, 0:no], ALU.mult, ALU.add))
                    P.op("dve", [U, C, CB], [C], lambda e, U=U, C=C, ft=ft: e.scalar_tensor_tensor(
                        C.t[:, 0:no], U.t[:, 2:no + 2], cw(2, ft), C.t[:, 0:no], ALU.mult, ALU.add))
                GE = nx_ge()
                P.op("act", [CG], [GE], lambda e, GE=GE, CG=CG: e.activation(GE.t[:, 0:no], CG.t[:, 0:no], AF.Gelu))
                P.op("pool", [GE, CV], [gT], lambda e, GE=GE, CV=CV, j=j: e.tensor_tensor(gT.t[:, j, 0:no], GE.t[:, 0:no], CV.t[:, 0:no], ALU.mult))
                if j == 40:
                    ens_wd(wi * 8)
            nblk = (no + 127) // 128
            for c8 in range(8):
                k = wi * 8 + c8
                ens_wd(k)
                WD, XR = wdl.pop(k)
                if c8 < 7:
                    ens_wd(k + 1)
                for b in range(nblk):
                    nb_ = min(128, no - b * 128)
                    DN = nx_dn()

                    def mmd(e, DN=DN, WD=WD, b=b, nb_=nb_):
                        for j in range(44):
                            ins = e.matmul(DN.t[0:nb_, 0:256], gT.t[:, j, b * 128:b * 128 + nb_], WD.t[:, j, :], start=(j == 0), stop=(j == 43))
                        return ins
                    P.op("pe", [gT, WD], [DN], mmd)
                    cvr = c0 + 1 + b * 128
                    YB = nx_yb()
                    P.op("dve", [DN, XR, CB], [YB], lambda e, YB=YB, DN=DN, XR=XR, nb_=nb_, b=b: e.scalar_tensor_tensor(
                        YB.t[0:nb_, 0:256], DN.t[0:nb_, 0:256], cst[0:nb_, 9:10], XR.t[0:nb_, b, :], ALU.add, ALU.add))
                    if kind == "p":
                        dst = yp[cvr - 1:cvr - 1 + nb_, c8 * 256:(c8 + 1) * 256]
                    else:
                        dst = ys[cvr - CVS0 - 1:cvr - CVS0 - 1 + nb_, c8 * 256:(c8 + 1) * 256]
                    P.dma("sp", None, YB, dst, YB.t[0:nb_, 0:256])
        P.barrier()
    return nc, es


def _tables(c):
    r = c % 4
    own_lo, own_hi = 4096 * r, 4096 * (r + 1)
    u = np.arange(SEQ)
    pk = (own_lo - OWN0 + u) % SEQ
    is_own = (u >= OWN0) & (u < OWN1)
    sig = np.where(is_own, 1.0, np.where(pk < own_lo, 1.0, -1.0))
    pk_all = np.concatenate([pk, np.arange(SSEQ)]).astype(np.float64)
    sig_all = np.concatenate([sig, np.ones(SSEQ)])
    jlo = pk_all % 128
    jhi = pk_all - jlo
    kaug = np.zeros((8, 4, NCTX), np.float32)
    for h in range(8):
        s = SLOPE_A[h]
        kaug[h, 0] = -sig_all * s
        kaug[h, 1] = -sig_all * s
        kaug[h, 2] = sig_all * s * jlo
        kaug[h, 3] = sig_all * s * jhi
    qpos = np.zeros(NQ, np.float64)
    qpos[0:4096] = own_lo + np.arange(4096)
    qpos[QL] = own_lo - 1
    qpos[QR] = own_hi
    qpos[QS0:] = np.arange(SSEQ)
    ilo = qpos % 128
    ihi = qpos - ilo
    qaugp = np.stack([ilo, ihi, np.ones(NQ), np.ones(NQ)]).astype(np.float32)
    qaugm = -qaugp
    qaugm[:, QR] = qaugp[:, QR]
    valb = np.zeros((128, 72), np.float32)
    ub = np.arange(BWIN)
    pb = own_lo - OWN0 + ub
    bad = (pb < 0) | (pb >= SEQ)
    valb[:, :56] = np.where(bad, -30000.0, 0.0).reshape(56, 128).T
    hm = np.array([1.0 if r > 0 else 0.0, 1.0 if r < 3 else 0.0], np.float32)
    return kaug, qaugp, qaugm, valb, hm


def _shared_tables():
    k = np.arange(128)[:, None]
    absa = np.abs(k - np.arange(896)[None, :] + 384).astype(np.float32)
    dbt = np.zeros((128, DB_TOT), np.float32)
    for g in range(3):
        cc = np.arange(DB_W[g])[None, :]
        dl = k - cc + DB_C0[g]
        ok = (np.abs(dl) <= 64 * DIL[g]) & (dl % DIL[g] == 0)
        dbt[:, DB_OFF[g]:DB_OFF[g] + DB_W[g]] = np.where(ok, np.abs(dl), BIGD)
    return absa, dbt, np.eye(128, dtype=np.float32)


def make_in_maps(inp):
    f = lambda a: np.ascontiguousarray(a, dtype=np.float32)
    absa, dbt, ident = _shared_tables()
    w_in, w_pa, w_pb, w_o = f(inp["w_in"][0]), f(inp["w_pa"][0]), f(inp["w_pb"][0]), f(inp["w_o"][0])
    w_up, w_dn = f(inp["w_up"][0]), f(inp["w_down"][0])
    pvb = np.zeros((128, NPV), np.float32)
    gmixb = np.ascontiguousarray(np.broadcast_to(inp["g_mix_norm"][0][None, :], (128, D)), dtype=np.float32)
    gffnb = np.ascontiguousarray(np.broadcast_to(inp["g_ffn_norm"][0][None, :], (128, D)), dtype=np.float32)
    pvb[:, PV_GQA] = np.tile(inp["g_qa"][0], 2)
    pvb[:, PV_GKA] = np.tile(inp["g_ka"][0], 2)
    pvb[:, PV_GQB] = inp["g_qb"][0]
    pvb[:, PV_GKB] = inp["g_kb"][0]
    pvb[:, PV_GSUB] = inp["g_subln"][0]
    cwv = inp["conv_w"][0]
    for j in range(3):
        pvb[:, PV_CW + j * 88:PV_CW + (j + 1) * 88] = cwv[j].reshape(88, 128).T
    pvb[:, PV_CB:PV_CB + 88] = inp["conv_b"][0].reshape(88, 128).T
    for i, kname in enumerate(("lam_q1", "lam_k1", "lam_q2", "lam_k2")):
        pvb[:, PV_LAM + 64 * i:PV_LAM + 64 * (i + 1)] = inp[kname][0][None, :]
    pvb[:, PV_GQK:PV_GQK + 64] = inp["g_qa"][0][None, :]
    pvb[:, PV_GQK + 64:PV_GQK + 128] = inp["g_ka"][0][None, :]
    maps = []
    for c in range(8):
        bp, r, bs = c // 4, c % 4, c // 2
        kaug, qaugp, qaugm, valb, hm = _tables(c)
        pvc = pvb.copy()
        pvc[:, PV_HM:PV_HM + 2] = hm[None, :]
        xpl = np.roll(inp["x_prompt"][bp], -(4096 * r - OWN0), axis=0)
        maps.append({
            "xp": f(xpl), "xs": f(inp["x_sample"][bs]),
            "w_in": w_in, "w_pa": w_pa, "w_pb": w_pb, "w_o": w_o, "w_up": w_up, "w_dn": w_dn,
            "pv": pvc, "gmixb": gmixb, "gffnb": gffnb, "kaug": kaug, "qaugp": qaugp, "qaugm": qaugm,
            "absa": absa, "dbt": dbt, "valb": valb, "ident": ident,
        })
    return maps


def kernel(**inp):
    nc, es = build()
    maps = make_in_maps(inp)
    res = run_bass_kernel_spmd(nc, maps, core_ids=list(range(8)))
    es.close()
    yp = np.zeros((2, SEQ, D), np.float32)
    ys = np.zeros((4, SSEQ, D), np.float32)
    for c in range(8):
        bp, r, bs, hs = c // 4, c % 4, c // 2, c % 2
        yp[bp, 4096 * r:4096 * (r + 1)] = res.results[c]["yp"]
        ys[bs, 1024 * hs:1024 * (hs + 1)] = res.results[c]["ys"][1024 * hs:1024 * (hs + 1)]
    return yp, ys
```

```python
import numpy as np
from contextlib import ExitStack
import concourse.bass as bass
import concourse.mybir as mybir
from concourse.bass_utils import run_bass_kernel_spmd

F32 = mybir.dt.float32
BF16 = mybir.dt.bfloat16
AF = mybir.ActivationFunctionType
ALU = mybir.AluOpType
AX = mybir.AxisListType

D = 2048
SEQ = 16384
SSEQ = 2048
N_IN = 11776
DFF = 5632
NCTX = SEQ + SSEQ
OWN0 = 1536
OWN1 = OWN0 + 4096
BWIN = 7168
NB = BWIN + SSEQ
NQ = 4096 + 2 + SSEQ
QL, QR, QS0 = 4096, 4097, 4098
NCV = 4098 + 2050
CVS0 = 4098
SLOPE_A = [2.0 ** (-(h + 1)) for h in range(8)]
SLOPE_B = [2.0 ** (-8.0 * (i + 1) / 12.0) for i in range(12)]
DIL = [1, 4, 16]
BIGD = 1.0e7
ALIBI_CUT = 94.4
DB_C0 = [64 * d + 512 for d in DIL]
DB_W = [DB_C0[g] + 64 * DIL[g] + 640 for g in range(3)]
DB_OFF = [0, DB_W[0], DB_W[0] + DB_W[1]]
DB_TOT = sum(DB_W)

PV_GQA = 0
PV_GKA = 1
PV_GQB = 2
PV_GKB = 3
PV_GSUB = 4
PV_CW = 5
PV_CB = PV_CW + 264
PV_LAM = PV_CB + 88
PV_HM = PV_LAM + 256
PV_GQK = PV_HM + 2
NPV = PV_GQK + 128

ENGS = ("pe", "act", "dve", "pool", "sp")


class Buf:
    __slots__ = ("t", "w", "r", "ds", "name")

    def __init__(self, t, name=""):
        self.t = t
        self.w = None
        self.r = {}
        self.ds = None
        self.name = name


class Prog:
    def __init__(self, nc, es):
        self.nc = nc
        self.es = es
        self.E = {"pe": nc.tensor, "act": nc.scalar, "dve": nc.vector,
                  "pool": nc.gpsimd, "sp": nc.sync}
        self.sem = {}
        self.cnt = {}
        self.waited = {e: {} for e in ENGS}
        for e in ("pe", "act", "dve", "pool"):
            self._mk("E_" + e)
        self.free_ds = {"sw": [], "hw": []}
        self.nds = 0

    def _mk(self, name):
        self.sem[name] = self.es.enter_context(self.nc.semaphore(name))
        self.cnt[name] = 0

    def get_ds(self, kind):
        if self.free_ds[kind]:
            return self.free_ds[kind].pop()
        name = "D%s%d" % (kind, self.nds)
        self.nds += 1
        self._mk(name)
        return name

    def release(self, bufs):
        for b in bufs:
            if b.ds is not None:
                for kind, nm in b.ds.items():
                    self.free_ds[kind].append(nm)
                b.ds = None

    def wait(self, eng, tok):
        if tok is None:
            return
        s, v = tok
        if eng == "pe" and s == "E_pe":
            return
        if self.waited[eng].get(s, 0) >= v:
            return
        self.waited[eng][s] = v
        self.E[eng].wait_ge(self.sem[s], v)

    def _deps(self, eng, reads, writes):
        for b in reads:
            self.wait(eng, b.w)
        for b in writes:
            self.wait(eng, b.w)
            for t in b.r.values():
                self.wait(eng, t)

    def op(self, eng, reads, writes, fn):
        self._deps(eng, reads, writes)
        ins = fn(self.E[eng])
        s = "E_" + eng
        self.cnt[s] += 1
        ins.then_inc(self.sem[s], 1)
        tok = (s, self.cnt[s])
        for b in reads:
            b.r[eng] = tok
        for b in writes:
            b.w = tok
            b.r = {}
        return tok

    def dma(self, q, out_b, in_b, out_ap, in_ap, track=None):
        reads = [in_b] if in_b is not None else []
        writes = [out_b] if out_b is not None else []
        self._deps(q, reads, writes)
        tb = track if track is not None else (out_b if out_b is not None else in_b)
        if tb.ds is None:
            tb.ds = {}
        kind = "sw" if q == "pool" else "hw"
        if kind not in tb.ds:
            tb.ds[kind] = self.get_ds(kind)
        s = tb.ds[kind]
        self.cnt[s] += 16
        try:
            ins = self.E[q].dma_start(out=out_ap, in_=in_ap)
        except ValueError:
            ins = self.E[q].dma_start(out=out_ap, in_=in_ap, allow_slow_non_contiguous=True)
        ins.then_inc(self.sem[s], 16)
        tok = (s, self.cnt[s])
        if in_b is not None:
            in_b.r["dma_" + s] = tok
        if out_b is not None:
            out_b.w = tok
            out_b.r = {}
        return tok

    def barrier(self):
        for e in ENGS:
            for s, c in self.cnt.items():
                if c > 0:
                    self.wait(e, (s, c))


def _rot(lst):
    i = [0]

    def nxt():
        b = lst[i[0] % len(lst)]
        i[0] += 1
        return b
    return nxt


def build(dbg=False):
    nc = bass.Bass("TRN2", target_bir_lowering=False)
    es = ExitStack()
    P = Prog(nc, es)
    es.enter_context(nc.allow_low_precision(reason="bf16 matmul operands by design, fp32 accumulation"))

    def din(name, shape):
        return nc.dram_tensor(name, list(shape), F32, kind="ExternalInput").ap()

    def dscr(name, shape, dt):
        if dbg:
            return nc.dram_tensor(name, list(shape), dt, kind="ExternalOutput").ap()
        return nc.dram_tensor(name, list(shape), dt).ap()

    xp = din("xp", [SEQ, D])
    xs = din("xs", [SSEQ, D])
    w_in = din("w_in", [D, N_IN])
    w_pa = din("w_pa", [1024, D])
    w_pb = din("w_pb", [512, D])
    w_o = din("w_o", [D, D])
    w_up = din("w_up", [D, 2 * DFF])
    w_dn = din("w_dn", [DFF, D])
    pv = din("pv", [128, NPV])
    gmixb = din("gmixb", [128, D])
    gffnb = din("gffnb", [128, D])
    kaug = din("kaug", [8, 4, NCTX])
    qaugp = din("qaugp", [4, NQ])
    qaugm = din("qaugm", [4, NQ])
    absa = din("absa", [128, 896])
    dbt = din("dbt", [128, DB_TOT])
    valb = din("valb", [128, 72])
    ident = din("ident", [128, 128])
    yp = nc.dram_tensor("yp", [4096, D], F32, kind="ExternalOutput").ap()
    ys = nc.dram_tensor("ys", [SSEQ, D], F32, kind="ExternalOutput").ap()

    wb_in = nc.dram_tensor("wb_in", [128, 16, N_IN], BF16).ap()
    wb_pa = nc.dram_tensor("wb_pa", [128, 8, D], BF16).ap()
    wb_pb = nc.dram_tensor("wb_pb", [128, 4, D], BF16).ap()
    wb_o = nc.dram_tensor("wb_o", [128, 16, D], BF16).ap()
    wb_up = nc.dram_tensor("wb_up", [128, 16, 2 * DFF], BF16).ap()
    wb_dn = nc.dram_tensor("wb_dn", [128, 44, D], BF16).ap()
    kaug_b = nc.dram_tensor("kaug_b", [8, 4, NCTX], BF16).ap()
    qaugp_b = nc.dram_tensor("qaugp_b", [4, NQ], BF16).ap()
    qaugm_b = nc.dram_tensor("qaugm_b", [4, NQ], BF16).ap()
    QAd = dscr("QAd", [8, 128, NQ], BF16)
    KAd = dscr("KAd", [8, 128, NCTX], BF16)
    VAd = dscr("VAd", [8, NCTX, 128], BF16)
    QBd = dscr("QBd", [12, 128, NQ], BF16)
    KBd = dscr("KBd", [12, 128, NB], BF16)
    VBd = dscr("VBd", [12, NB, 128], BF16)
    GTd = dscr("GTd", [32, 128, NQ], BF16)
    OAd = dscr("OAd", [8, 128, NQ], BF16)
    OBd = dscr("OBd", [4, 128, NQ], BF16)
    X1d = dscr("X1d", [NCV, D], F32)
    H2d = dscr("H2d", [128, 16, NCV], BF16)

    def sb(name, shape, dt):
        return es.enter_context(nc.sbuf_tensor(name, list(shape), dt))

    PDt = [es.enter_context(nc.psum_tensor("pd%d" % i, [128, 1024], F32)) for i in range(4)]
    PSt = [PDt[i // 2][:, (i % 2) * 512:(i % 2 + 1) * 512] for i in range(8)]
    PS = [Buf(t, "ps%d" % i) for i, t in enumerate(PSt)]

    pvt = sb("pvt", [128, NPV], F32)
    cst = sb("cst", [128, 16], F32)
    idt = sb("idt", [128, 128], BF16)
    idf = sb("idf", [128, 128], F32)
    ones = sb("ones", [128, 128], BF16)
    blk = sb("blk", [128, 128], BF16)
    lamt = sb("lamt", [128, 64], F32)
    lams = sb("lams", [128, 4], F32)
    CB = Buf(None, "consts")

    WB = {}

    def cast(name, dst, src, ktn):
        b = Buf(None, name)
        for kt in range(ktn):
            P.dma("pool", b, None, dst[:, kt, :], src[kt * 128:(kt + 1) * 128, :])
        WB[name] = b

    TB = Buf(None, "tables")
    P.dma("pool", CB, None, idt[:], ident[:, :])
    P.dma("pool", TB, None, kaug_b[:, :, :], kaug[:, :, :])
    P.dma("pool", TB, None, qaugp_b[:, :], qaugp[:, :])
    P.dma("pool", TB, None, qaugm_b[:, :], qaugm[:, :])
    bkv = Buf(None, "in_kv")
    for kt in range(16):
        P.dma("pool", bkv, None, wb_in[:, kt, 1024:3072], w_in[kt * 128:(kt + 1) * 128, 1024:3072])
    WB["in_kv"] = bkv
    brest = Buf(None, "in")
    for kt in range(16):
        P.dma("pool", brest, None, wb_in[:, kt, 0:1024], w_in[kt * 128:(kt + 1) * 128, 0:1024])
        P.dma("pool", brest, None, wb_in[:, kt, 3072:N_IN], w_in[kt * 128:(kt + 1) * 128, 3072:N_IN])
    WB["in"] = brest
    cast("pa", wb_pa, w_pa, 8)
    cast("pb", wb_pb, w_pb, 4)
    cast("o", wb_o, w_o, 16)
    cast("up", wb_up, w_up, 16)
    cast("dn", wb_dn, w_dn, 44)

    P.dma("sp", CB, None, pvt[:], pv[:, :])
    P.dma("sp", CB, None, idf[:], ident[:, :])

    def c_op(eng, fn):
        P.op(eng, [CB], [CB], fn)

    c_op("dve", lambda e: e.memset(cst[:, 0:1], 1e-6))
    c_op("dve", lambda e: e.memset(cst[:, 1:2], 1e-5))
    c_op("dve", lambda e: e.memset(cst[:, 8:9], 0.0))
    c_op("dve", lambda e: e.memset(ones[:], 1.0))
    c_op("dve", lambda e: e.memset(blk[:], 0.0))
    c_op("dve", lambda e: e.memset(blk[0:64, 0:64], 1.0))
    c_op("dve", lambda e: e.memset(blk[64:128, 64:128], 1.0))
    c_op("dve", lambda e: e.tensor_scalar(cst[:, 3:4], pvt[:, PV_GQA:PV_GQA + 1], 0.125, None, ALU.mult))
    c_op("dve", lambda e: e.tensor_copy(cst[:, 4:5], pvt[:, PV_GKA:PV_GKA + 1]))
    c_op("dve", lambda e: e.tensor_scalar(cst[:, 5:6], pvt[:, PV_GQB:PV_GQB + 1], 128.0 ** -0.5, None, ALU.mult))
    c_op("dve", lambda e: e.tensor_copy(cst[:, 6:7], pvt[:, PV_GKB:PV_GKB + 1]))
    c_op("dve", lambda e: e.tensor_scalar(cst[:, 7:8], pvt[:, PV_GSUB:PV_GSUB + 1], 0.8, None, ALU.mult))
    for j in range(2):
        a0 = PV_LAM + 128 * j
        c_op("dve", lambda e, a0=a0: e.tensor_tensor(lamt[:], pvt[:, a0:a0 + 64], pvt[:, a0 + 64:a0 + 128], ALU.mult))
        c_op("dve", lambda e, j=j: e.tensor_reduce(lams[:, j:j + 1], lamt[:], AX.X, ALU.add))
        c_op("act", lambda e, j=j: e.activation(lams[:, 2 + j:3 + j], lams[:, j:j + 1], AF.Exp))
    c_op("dve", lambda e: e.tensor_tensor(lams[:, 0:1], lams[:, 3:4], lams[:, 2:3], ALU.subtract))
    c_op("dve", lambda e: e.tensor_scalar(cst[:, 2:3], lams[:, 0:1], -0.2, None, ALU.add))
    c_op("dve", lambda e: e.tensor_reduce(lams[:, 0:1], pvt[:, PV_GQK:PV_GQK + 64], AX.X, ALU.max, apply_absolute_value=True))
    c_op("dve", lambda e: e.tensor_reduce(lams[:, 1:2], pvt[:, PV_GQK + 64:PV_GQK + 128], AX.X, ALU.max, apply_absolute_value=True))
    c_op("dve", lambda e: e.tensor_tensor(lams[:, 2:3], lams[:, 0:1], lams[:, 1:2], ALU.mult))
    c_op("dve", lambda e: e.tensor_scalar(lams[:, 2:3], lams[:, 2:3], 8.0, -32.0, ALU.mult, ALU.add))
    c_op("dve", lambda e: e.tensor_scalar(lams[:, 2:3], lams[:, 2:3], 0.0, 3.0e38, ALU.max, ALU.mult))
    c_op("dve", lambda e: e.tensor_scalar(lams[:, 2:3], lams[:, 2:3], 3.0e38, 0.0, ALU.mult, ALU.mult))
    c_op("dve", lambda e: e.tensor_copy(cst[:, 9:10], lams[:, 2:3]))
    EPS6 = cst[:, 0:1]
    EPS5 = cst[:, 1:2]
    NEGLAM = cst[:, 2:3]

    def norm_tail(N, ps_z, ps_bs, sqb, lnb, onesmat, inv_n, eps_ap, g_ap, out_b, z_sbuf=None):
        zb = z_sbuf if z_sbuf is not None else ps_z
        P.op("act", [zb], [sqb], lambda e: e.activation(sqb.t[:, 0:N], zb.t[:, 0:N], AF.Square))
        P.op("pe", [sqb, CB], [ps_bs], lambda e: e.matmul(ps_bs.t[:, 0:N], onesmat, sqb.t[:, 0:N], start=True, stop=True))
        P.op("act", [ps_bs, CB], [lnb], lambda e: e.activation(lnb.t[:, 0:N], ps_bs.t[:, 0:N], AF.Ln, bias=eps_ap, scale=inv_n))
        P.op("act", [lnb], [lnb], lambda e: e.activation(lnb.t[:, 0:N], lnb.t[:, 0:N], AF.Exp, scale=-0.5))
        P.op("dve", [zb, lnb, CB], [out_b], lambda e: e.scalar_tensor_tensor(
            out_b.t[:, 0:N], zb.t[:, 0:N], g_ap, lnb.t[:, 0:N], ALU.mult, ALU.mult))

    with ExitStack() as ph:
        def psb(name, shape, dt):
            return ph.enter_context(nc.sbuf_tensor(name, list(shape), dt))
        xt = [Buf(psb("xt%d" % i, [128, D], F32)) for i in range(3)]
        junk = psb("junk", [128, D], BF16)
        ssb = [Buf(psb("ssb%d" % i, [128, 8], F32)) for i in range(2)]
        xsb = [Buf(psb("xsb%d" % i, [128, D], BF16)) for i in range(2)]
        hTt = [psb("hT%d" % i, [128, 16, 512], BF16) for i in range(2)]
        hT = [[Buf(t) for _ in range(2)] for t in hTt]
        gmt = psb("gmt", [128, D], F32)
        GM = Buf(None)
        P.dma("sp", GM, None, gmt[:], gmixb[:, :])
        wch = [Buf(psb("wch%d" % i, [128, 16, 512], BF16)) for i in range(4)]
        zst = [Buf(psb("zst%d" % i, [128, 512], BF16)) for i in range(4)]
        vst = [Buf(psb("vst%d" % i, [128, 512], BF16)) for i in range(3)]
        sqb = [Buf(psb("sqb%d" % i, [128, 512], BF16)) for i in range(2)]
        lnb = [Buf(psb("lnb%d" % i, [128, 512], F32)) for i in range(2)]
        gex = [Buf(psb("gex%d" % i, [128, 512], F32)) for i in range(2)]
        nx_xt, nx_ss, nx_xs, nx_hT = _rot(xt), _rot(ssb), _rot(xsb), _rot(hT)
        nx_zst, nx_vst, nx_sq, nx_ln, nx_gex = _rot(zst), _rot(vst), _rot(sqb), _rot(lnb), _rot(gex)
        ptb = [(PS[i], PSt[i].bitcast(BF16)) for i in range(2)]
        nx_pt = _rot(ptb)
        nx_z = _rot(PS[2:6])
        nx_bs = _rot(PS[6:8])
        evac_i = [0]
        wslot = _rot(wch)

        def load_chunk(ci, slot):
            P.dma("sp", slot, WB["in_kv" if ci in (2, 3, 4, 5) else "in"], slot.t[:], wb_in[:, :, ci * 512:(ci + 1) * 512])

        import os
        _lim = os.environ.get("KDBG", "")

        def prologue(xsrc, r0):
            Hs = nx_hT()
            Ht = Hs[0].t
            ss = nx_ss()
            for b in range(4):
                X = nx_xt()
                P.dma("sp", X, None, X.t[:], xsrc[r0 + b * 128:r0 + (b + 1) * 128, :])
                P.op("act", [X], [ss], lambda e, X=X, b=b: e.activation(
                    junk[:], X.t[:], AF.Square, accum_out=ss.t[:, b:b + 1]))
                P.op("act", [ss, CB], [ss], lambda e, b=b: e.activation(ss.t[:, 4 + b:5 + b], ss.t[:, b:b + 1], AF.Ln, bias=EPS6, scale=1.0 / D))
                P.op("act", [ss], [ss], lambda e, b=b: e.activation(ss.t[:, 4 + b:5 + b], ss.t[:, 4 + b:5 + b], AF.Exp, scale=-0.5))
                S = nx_xs()
                P.op("dve", [X, ss, GM], [S], lambda e, X=X, S=S, b=b: e.scalar_tensor_tensor(
                    S.t[:], X.t[:], ss.t[:, 4 + b:5 + b], gmt[:], ALU.mult, ALU.mult))
                for kh in range(2):
                    pb, pvw = nx_pt()

                    def tr(e, S=S, pvw=pvw, kh=kh):
                        for k in range(8):
                            kt = kh * 8 + k
                            ins = e.transpose(pvw[:, k * 128:(k + 1) * 128], S.t[:, kt * 128:(kt + 1) * 128], idt[:])
                        return ins
                    P.op("pe", [S, CB], [pb], tr)
                    src = pvw[:, :].rearrange("p (k t) -> p k t", k=8)
                    dst = Ht[:, kh * 8:kh * 8 + 8, b * 128:(b + 1) * 128]
                    if kh % 2:
                        P.op("dve", [pb], [Hs[kh]], lambda e, dst=dst, src=src: e.tensor_copy(dst, src))
                    else:
                        P.op("act", [pb], [Hs[kh]], lambda e, dst=dst, src=src: e.activation(dst, src, AF.Copy))
            return Hs

        stream = []
        loaded = {}
        spos = [0]

        last_res = [-1]

        def ensure(k):
            if k < len(stream) and k not in loaded:
                cj, resident = stream[k]
                if cj in resident:
                    loaded[k] = resident[cj]
                else:
                    if spos[0] <= last_res[0] + 1:
                        return
                    sl = wslot()
                    load_chunk(cj, sl)
                    loaded[k] = sl

        def project(Hs, ctx0, b0, q0, qc, chunks, hoist=None, hoist_at=0):
            Ht = Hs[0].t
            pend = [None]

            def flush():
                if pend[0] is not None:
                    f_ = pend[0]
                    pend[0] = None
                    f_()
            for j, ci in enumerate(chunks):
                k = spos[0]
                spos[0] += 1
                ensure(k)
                ensure(k + 1)
                ensure(k + 2)
                W = loaded.pop(k)
                if hoist is not None and j == hoist_at:
                    hoist()
                if ci in (4, 5, 12, 13, 14):
                    flush()
                    for b in range(4):
                        Z = nx_z()

                        def mm(e, W=W, Z=Z, b=b):
                            for kt in range(16):
                                ins = e.matmul(Z.t[:, :], Ht[:, kt, b * 128:(b + 1) * 128], W.t[:, kt, :], start=(kt == 0), stop=(kt == 15))
                            return ins
                        P.op("pe", Hs + [W], [Z], mm)
                        V = nx_vst()
                        evac_i[0] += 1
                        if evac_i[0] % 2:
                            P.op("dve", [Z], [V], lambda e, V=V, Z=Z: e.tensor_copy(V.t[:], Z.t[:]))
                        else:
                            P.op("act", [Z], [V], lambda e, V=V, Z=Z: e.activation(V.t[:], Z.t[:], AF.Copy))
                        if ci in (4, 5):
                            h0 = 4 * (ci - 4)
                            dst = VAd[h0:h0 + 4, ctx0 + b * 128:ctx0 + (b + 1) * 128, :].rearrange("h t e -> t h e")
                        else:
                            h0 = 4 * (ci - 12)
                            dst = VBd[h0:h0 + 4, b0 + b * 128:b0 + (b + 1) * 128, :].rearrange("h t e -> t h e")
                        P.dma("sp", None, V, dst, V.t[:].rearrange("p (h e) -> p h e", h=4))
                    continue
                for f in range(4):
                    ft = ci * 4 + f
                    isq = ft < 8 or 24 <= ft < 36 or ft >= 60
                    c0, c1 = qc if isq else (0, 512)
                    N = c1 - c0
                    Z = nx_z()

                    def mm(e, W=W, Z=Z, f=f, c0=c0, c1=c1, N=N):
                        for kt in range(16):
                            ins = e.matmul(Z.t[:, 0:N], W.t[:, kt, f * 128:(f + 1) * 128], Ht[:, kt, c0:c1], start=(kt == 0), stop=(kt == 15))
                        return ins
                    P.op("pe", Hs + [W], [Z], mm)
                    O = nx_zst()
                    if ft >= 60:
                        G = nx_gex()
                        P.op("act", [Z], [G], lambda e, G=G, Z=Z, N=N: e.activation(G.t[:, 0:N], Z.t[:, 0:N], AF.Exp, scale=-1.0))
                        P.op("dve", [G], [G], lambda e, G=G, N=N: e.tensor_scalar(G.t[:, 0:N], G.t[:, 0:N], 1.0, None, ALU.add))
                        P.op("dve", [G], [O], lambda e, G=G, O=O, N=N: e.reciprocal(O.t[:, 0:N], G.t[:, 0:N]))
                        dst = GTd[ft - 60, :, q0:q0 + N]
                    else:
                        if ft < 8:
                            om, inv, g, dst = blk[:], 1.0 / 64, cst[:, 3:4], QAd[ft, :, q0:q0 + N]
                        elif ft < 16:
                            om, inv, g, dst = blk[:], 1.0 / 64, cst[:, 4:5], KAd[ft - 8, :, ctx0:ctx0 + 512]
                        elif ft < 36:
                            om, inv, g, dst = ones[:], 1.0 / 128, cst[:, 5:6], QBd[ft - 24, :, q0:q0 + N]
                        else:
                            om, inv, g, dst = ones[:], 1.0 / 128, cst[:, 6:7], KBd[ft - 36, :, b0:b0 + 512]
                        SQ_, BS_, LN_ = nx_sq(), nx_bs(), nx_ln()
                        P.op("act", [Z], [SQ_], lambda e, SQ_=SQ_, Z=Z, N=N: e.activation(SQ_.t[:, 0:N], Z.t[:, 0:N], AF.Square))
                        flush()

                        def tail(N=N, Z=Z, SQ_=SQ_, BS_=BS_, LN_=LN_, om=om, inv=inv, g=g, O=O, dst=dst):
                            P.op("pe", [SQ_, CB], [BS_], lambda e: e.matmul(BS_.t[:, 0:N], om, SQ_.t[:, 0:N], start=True, stop=True))
                            P.op("act", [BS_, CB], [LN_], lambda e: e.activation(LN_.t[:, 0:N], BS_.t[:, 0:N], AF.Ln, bias=EPS6, scale=inv))
                            P.op("act", [LN_], [LN_], lambda e: e.activation(LN_.t[:, 0:N], LN_.t[:, 0:N], AF.Exp, scale=-0.5))
                            P.op("dve", [Z, LN_, CB], [O], lambda e: e.scalar_tensor_tensor(
                                O.t[:, 0:N], Z.t[:, 0:N], g, LN_.t[:, 0:N], ALU.mult, ALU.mult))
                            P.dma("sp", None, O, dst, O.t[:, 0:N])
                        pend[0] = tail
                        continue
                    flush()
                    P.dma("sp", None, O, dst, O.t[:, 0:N])
            flush()

        KV = [2, 3, 4, 5]
        KVB = [2, 3, 4, 5, 9, 10, 11, 12, 13, 14]
        ALLC = list(range(23))
        res = {}
        for ci in KV:
            sl = wslot()
            load_chunk(ci, sl)
            res[ci] = sl
        TD = []
        for T in range(14, 32):
            TD.append((xp, T * 512, T * 512, None, None, None, KV, res))
        for T in range(0, 14):
            if 3 <= T <= 10:
                TD.append((xp, T * 512, T * 512, T * 512, (T - 3) * 512, (0, 512), ALLC, {}))
            elif T == 2:
                TD.append((xp, T * 512, T * 512, T * 512, QL, (511, 512), ALLC, {}))
            elif T == 11:
                TD.append((xp, T * 512, T * 512, T * 512, QR, (0, 1), ALLC, {}))
            else:
                TD.append((xp, T * 512, T * 512, T * 512, None, None, KVB, {}))
        for T in range(4):
            TD.append((xs, T * 512, SEQ + T * 512, BWIN + T * 512, QS0 + T * 512, (0, 512), ALLC, {}))
        for d in TD:
            for ci in d[6]:
                if ci in d[7]:
                    last_res[0] = len(stream)
                stream.append((ci, d[7]))
        nextH = [prologue(TD[0][0], TD[0][1])]
        for i, d in enumerate(TD):
            Hcur = nextH[0]

            def hoist(i=i):
                if i + 1 < len(TD):
                    nextH[0] = prologue(TD[i + 1][0], TD[i + 1][1])
            project(Hcur, d[2], d[3], d[4], d[5], d[6], hoist=hoist, hoist_at=min(1, len(d[6]) - 1))
        P.barrier()
        P.release(xt + wch + zst + vst + [GM])

    if dbg == 1:
        return nc, es

    import os
    _lim2 = os.environ.get("KDBG2", "")
    QT = [("p", t, t * 512, 512) for t in range(8)] + [("h", 0, QL, 2)] + [("s", t, QS0 + t * 512, 512) for t in range(4)]
    with ExitStack() as ph:
        def psb(name, shape, dt):
            return ph.enter_context(nc.sbuf_tensor(name, list(shape), dt))
        KT = [Buf(psb("KT%d" % m, [68, NCTX], BF16)) for m in range(2)]
        VT = Buf(psb("VT", [128, 144, 128], BF16))
        QSb = [Buf(psb("QS%d" % i, [68, 6, 512], BF16)) for i in range(2)]
        PT = [Buf(psb("PT%d" % i, [128, 2, 512], BF16)) for i in range(6)]
        PSm = [Buf(psb("PSm%d" % i, [128, 2, 512], BF16)) for i in range(3)]
        nx_PSm = _rot(PSm)
        SBf = [Buf(psb("SBf%d" % i, [128, 2, 512], F32)) for i in range(2)]
        absat = psb("absat", [128, 896], F32)
        lr = [Buf(psb("lr%d" % i, [128, 512], F32)) for i in range(2)]
        o12 = [Buf(psb("o12%d" % i, [128, 512], F32)) for i in range(2)]
        ob = Buf(psb("ob", [128, 512], F32))
        sq2 = Buf(psb("sq2", [128, 512], BF16))
        ln2 = Buf(psb("ln2", [128, 512], F32))
        oast = [Buf(psb("oast%d" % i, [128, 512], BF16)) for i in range(2)]
        nx_QS, nx_PT, nx_SB, nx_oast = _rot(QSb), _rot(PT), _rot(SBf), _rot(oast)
        nx_sc = _rot(PS[0:4])
        nx_scp = _rot([(PS[0], PS[1], PDt[0]), (PS[2], PS[3], PDt[1])])
        OB_, LB_ = PS[4:6], PS[6:8]
        AT = Buf(None)
        P.dma("sp", AT, None, absat[:], absa[:, :])
        for q in QSb:
            P.op("dve", [], [q], lambda e, q=q: e.memset(q.t[64:68, 4:6, :], 0.0))

        QTL = [q for i, q in enumerate(QT) if not (_lim2 and i not in (0, 1, 8, 12))]
        KTc = [[Buf(KT[m].t) for _ in range(4)] for m in range(2)]
        VTc = [Buf(VT.t) for _ in range(4)]

        def load_kv(hh, cs):
            for c4 in cs:
                k0, k1 = c4 * 4608, (c4 + 1) * 4608
                for m in range(2):
                    P.dma("sp", KTc[m][c4], None, KT[m].t[0:64, k0:k1], KAd[hh, m * 64:(m + 1) * 64, k0:k1])
                    P.dma("sp", KTc[m][c4], TB, KT[m].t[64:68, k0:k1], kaug_b[hh, :, k0:k1])
                P.dma("sp", VTc[c4], None, VT.t[:, c4 * 36:(c4 + 1) * 36, :],
                      VAd[hh, k0:k1, :].rearrange("(kt p) e -> p kt e", p=128))
        qitems = [(hh, qq) for hh in range(8) for qq in QTL]
        qld = {}

        def ens_q(k):
            if k < len(qitems) and k not in qld:
                hh, (kind_, t_, q0_, N_) = qitems[k]
                Qb = nx_QS()
                qsrc = QAd[hh, :, q0_:q0_ + N_].rearrange("(m d) q -> d m q", m=2)
                for v in range(3):
                    P.dma("sp", Qb, None, Qb.t[0:64, 2 * v:2 * v + 2, 0:N_], qsrc)
                for m in range(2):
                    P.dma("sp", Qb, TB, Qb.t[64:68, m, 0:N_], qaugp_b[:, q0_:q0_ + N_])
                    P.dma("sp", Qb, TB, Qb.t[64:68, 2 + m, 0:N_], qaugm_b[:, q0_:q0_ + N_])
                qld[k] = Qb
        load_kv(0, [0, 1, 2, 3])
        for h in range(8):
            for qi_, (kind, t, q0, N) in enumerate(QTL):
                kq = h * len(QTL) + qi_
                ens_q(kq)
                Q = qld.pop(kq)
                ens_q(kq + 1)
                if kind == "s" and t == (3 if _lim2 else 0) and h + 1 < 8:
                    load_kv(h + 1, [0, 1, 2])
                if kind == "s":
                    kts = list(range(128, 144))
                else:
                    kts = list(range(0, 128))
                tiles = []
                for kt in kts:
                    if ALIBI_CUT is not None:
                        if kind == "s":
                            klo, qlo, qhi, per = (kt - 128) * 128, t * 512, t * 512 + 511, None
                            qs_ = [(qlo, qhi)]
                        elif kind == "p":
                            klo, per = kt * 128, SEQ
                            qs_ = [(OWN0 + t * 512, OWN0 + t * 512 + 511)]
                        else:
                            klo, per = kt * 128, SEQ
                            qs_ = [(OWN0 - 1, OWN0 - 1), (OWN1, OWN1)]
                        khi = klo + 127
                        gap = None
                        for (qlo, qhi) in qs_:
                            for sh in ((0,) if per is None else (0, per, -per)):
                                g_ = max(0, klo + sh - qhi, qlo - (khi + sh))
                                gap = g_ if gap is None else min(gap, g_)
                        if SLOPE_A[h] * gap > ALIBI_CUT:
                            continue
                    if kind == "p":
                        d0 = 12 + 4 * t
                        if kt < 12 or kt >= 44 or kt < d0:
                            var, dk = 0, None
                        elif kt < d0 + 4:
                            var, dk = 2, kt - d0
                        else:
                            var, dk = 1, None
                    elif kind == "h":
                        var, dk = (1 if 12 <= kt < 44 else 0), None
                    else:
                        d0 = 128 + 4 * t
                        if kt < d0:
                            var, dk = 0, None
                        elif kt < d0 + 4:
                            var, dk = 2, kt - d0
                        else:
                            var, dk = 1, None
                    tiles.append((kt, var, dk))
                nt = len(tiles)
                DEPTH = 2
                pts = [None] * nt
                sums = [None] * nt
                for i in range(nt + DEPTH):
                    if i < nt:
                        kt, var, dk = tiles[i]
                        SCa, SCb, PDp = nx_scp()
                        SCm = (SCa, SCb)
                        for m in range(2):
                            P.op("pe", [KTc[m][kt // 36], Q], [SCm[m]], lambda e, m=m, kt=kt, var=var, Q=Q, SCm=SCm: e.matmul(
                                SCm[m].t[:, 0:N], KT[m].t[0:68, kt * 128:(kt + 1) * 128], Q.t[0:68, 2 * var + m, 0:N], start=True, stop=True))
                        if dk is not None:
                            S2 = nx_SB()
                            off = 384 - 128 * dk
                            for m in range(2):
                                P.op("dve", [SCm[m], AT], [S2], lambda e, S2=S2, m=m, SCm=SCm, off=off: e.scalar_tensor_tensor(
                                    S2.t[:, m, 0:N], absat[:, off:off + N], -SLOPE_A[h], SCm[m].t[:, 0:N], ALU.mult, ALU.add))
                            srcb = [S2]
                            src_ap = S2.t[:, :, 0:N]
                        else:
                            srcb = [SCa, SCb]
                            src_ap = PDp[:, :].rearrange("p (m n) -> p m n", m=2)[:, :, 0:N]
                        Pt = nx_PT()
                        P.op("act", srcb, [Pt], lambda e, Pt=Pt, src_ap=src_ap: e.activation(Pt.t[:, :, 0:N], src_ap, AF.Exp))
                        pts[i] = Pt
                        if i % 2 == 1:
                            Sm = nx_PSm()
                            eng = "dve"
                            Pa = pts[i - 1]
                            P.op(eng, [Pa, Pt], [Sm], lambda e, Sm=Sm, Pa=Pa, Pt=Pt: e.tensor_tensor(
                                Sm.t[:, :, 0:N], Pa.t[:, :, 0:N], Pt.t[:, :, 0:N], ALU.add))
                            sums[i] = Sm
                    j = i - DEPTH
                    if j >= 0:
                        kt, var, dk = tiles[j]
                        Pt = pts[j]
                        st = (j == 0)
                        sp_ = (j == nt - 1)
                        for m in range(2):
                            P.op("pe", [VTc[kt // 36], Pt], [OB_[m]], lambda e, m=m, kt=kt, Pt=Pt, st=st, sp_=sp_: e.matmul(
                                OB_[m].t[:, 0:N], VT.t[:, kt, :], Pt.t[:, m, 0:N], start=st, stop=sp_))
                        if j % 2 == 1 or j == nt - 1:
                            Ls = sums[j] if j % 2 == 1 else Pt
                            lst = (j <= 1)
                            for m in range(2):
                                P.op("pe", [CB, Ls], [LB_[m]], lambda e, m=m, Ls=Ls, lst=lst, sp_=sp_: e.matmul(
                                    LB_[m].t[:, 0:N], ones[:], Ls.t[:, m, 0:N], start=lst, stop=sp_))
                for m in range(2):
                    P.op("act", [LB_[m]], [lr[m]], lambda e, m=m: e.activation(lr[m].t[:, 0:N], LB_[m].t[:, 0:N], AF.Copy))
                    P.op("dve", [OB_[m]], [o12[m]], lambda e, m=m: e.tensor_copy(o12[m].t[:, 0:N], OB_[m].t[:, 0:N]))
                for m in range(2):
                    P.op("dve", [lr[m]], [lr[m]], lambda e, m=m: e.reciprocal(lr[m].t[:, 0:N], lr[m].t[:, 0:N]))
                    P.op("dve", [o12[m], lr[m]], [o12[m]], lambda e, m=m: e.tensor_tensor(
                        o12[m].t[:, 0:N], o12[m].t[:, 0:N], lr[m].t[:, 0:N], ALU.mult))
                P.op("dve", [o12[0], o12[1], CB], [ob], lambda e: e.scalar_tensor_tensor(
                    ob.t[:, 0:N], o12[1].t[:, 0:N], NEGLAM, o12[0].t[:, 0:N], ALU.mult, ALU.add))
                OA = nx_oast()
                norm_tail(N, None, nx_sc(), sq2, ln2, ones[:], 1.0 / 128, EPS5, cst[:, 7:8], OA, z_sbuf=ob)
                P.dma("sp", None, OA, OAd[h, :, q0:q0 + N], OA.t[:, 0:N])
            if h + 1 < 8:
                load_kv(h + 1, [3])
        P.barrier()
        P.release(KTc[0] + KTc[1] + VTc + QSb + oast + [AT])

    if dbg == 2:
        return nc, es

    QTB = [(t * 512, 512, OWN0 + t * 512, 0, 56) for t in range(8)]
    QTB += [(QL, 1, OWN0 - 1, 0, 56), (QR, 1, OWN1, 0, 56)]
    QTB += [(QS0 + t * 512, 512, t * 512, 56, 72) for t in range(4)]
    with ExitStack() as ph:
        def psb(name, shape, dt):
            return ph.enter_context(nc.sbuf_tensor(name, list(shape), dt))
        KBT = Buf(psb("KBT", [128, 3, NB], BF16))
        VBT = Buf(psb("VBT", [128, 3, 72, 128], BF16))
        QBT = [Buf(psb("QBT%d" % i, [128, 3, 512], BF16)) for i in range(2)]
        dbs = psb("dbs", [128, DB_TOT], BF16)
        vbs = psb("vbs", [128, 72], F32)
        PT = [Buf(psb("PTb%d" % i, [128, 512], BF16)) for i in range(8)]
        SBf = [Buf(psb("SBb%d" % i, [128, 512], F32)) for i in range(6)]
        lrb = Buf(psb("lrb", [128, 512], F32))
        obf = Buf(psb("obf", [128, 512], F32))
        obst = [Buf(psb("obst%d" % i, [128, 512], BF16)) for i in range(2)]
        nx_QB, nx_PT, nx_SB, nx_obst = _rot(QBT), _rot(PT), _rot(SBf), _rot(obst)
        nx_sc = _rot(PS[0:4])
        OBk, LBk = PS[4], PS[6]
        DT = Buf(None)
        P.dma("pool", DT, None, dbs[:], dbt[:, :])
        P.dma("sp", DT, None, vbs[:], valb[:, :])
        KBg = [Buf(KBT.t) for _ in range(3)]
        VBg = [Buf(VBT.t) for _ in range(3)]
        qbitems = [(hh, qq) for hh in range(4) for qq in QTB]
        qbl = {}

        def ens_qb(k):
            if k < len(qbitems) and k not in qbl:
                hh, (q0_, N_, _a, _b, _c) = qbitems[k]
                Qb = nx_QB()
                for g in range(3):
                    P.dma("sp", Qb, None, Qb.t[:, g, 0:N_], QBd[g * 4 + hh, :, q0_:q0_ + N_])
                qbl[k] = Qb
        for h in range(4):
            for g in range(3):
                P.dma("sp", KBg[g], None, KBT.t[:, g, :], KBd[g * 4 + h, :, :])
                for c2 in range(2):
                    P.dma("sp", VBg[g], None, VBT.t[:, g, c2 * 36:(c2 + 1) * 36, :],
                          VBd[g * 4 + h, c2 * 4608:(c2 + 1) * 4608, :].rearrange("(kt p) e -> p kt e", p=128))
            for qi_, (q0, N, qpos, klo, khi) in enumerate(QTB):
                kq = h * len(QTB) + qi_
                ens_qb(kq)
                Q = qbl.pop(kq)
                ens_qb(kq + 1)
                tiles = []
                for g in range(3):
                    wl = 64 * DIL[g]
                    for kt in range(klo, khi):
                        kpos = (kt - klo) * 128
                        dbase = kpos - qpos
                        if dbase - (N - 1) <= wl and dbase + 127 >= -wl:
                            tiles.append((g, kt, DB_OFF[g] + DB_C0[g] - dbase))
                nt = len(tiles)
                DEPTH = 5
                pts = [None] * nt
                for i in range(nt + DEPTH):
                    if i < nt:
                        g, kt, off = tiles[i]
                        SC = nx_sc()
                        P.op("pe", [KBg[g], Q], [SC], lambda e, SC=SC, g=g, kt=kt, Q=Q: e.matmul(
                            SC.t[:, 0:N], KBT.t[:, g, kt * 128:(kt + 1) * 128], Q.t[:, g, 0:N], start=True, stop=True))
                        S2 = nx_SB()
                        sl = -SLOPE_B[g * 4 + h]
                        P.op("dve", [SC, DT], [S2], lambda e, S2=S2, SC=SC, off=off, sl=sl: e.scalar_tensor_tensor(
                            S2.t[:, 0:N], dbs[:, off:off + N], sl, SC.t[:, 0:N], ALU.mult, ALU.add))
                        Pt = nx_PT()
                        P.op("act", [S2, DT], [Pt], lambda e, Pt=Pt, S2=S2, kt=kt: e.activation(
                            Pt.t[:, 0:N], S2.t[:, 0:N], AF.Exp, bias=vbs[:, kt:kt + 1]))
                        pts[i] = Pt
                    j = i - DEPTH
                    if j >= 0:
                        g, kt, off = tiles[j]
                        Pt = pts[j]
                        P.op("pe", [VBg[g], Pt], [OBk], lambda e, g=g, kt=kt, Pt=Pt, j=j: e.matmul(
                            OBk.t[:, 0:N], VBT.t[:, g, kt, :], Pt.t[:, 0:N], start=(j == 0), stop=(j == nt - 1)))
                        P.op("pe", [CB, Pt], [LBk], lambda e, Pt=Pt, j=j: e.matmul(
                            LBk.t[:, 0:N], ones[:], Pt.t[:, 0:N], start=(j == 0), stop=(j == nt - 1)))
                P.op("act", [LBk], [lrb], lambda e: e.activation(lrb.t[:, 0:N], LBk.t[:, 0:N], AF.Copy))
                P.op("dve", [OBk], [obf], lambda e: e.tensor_copy(obf.t[:, 0:N], OBk.t[:, 0:N]))
                P.op("dve", [lrb], [lrb], lambda e: e.reciprocal(lrb.t[:, 0:N], lrb.t[:, 0:N]))
                OO = nx_obst()
                P.op("dve", [obf, lrb], [OO], lambda e, OO=OO: e.tensor_tensor(OO.t[:, 0:N], obf.t[:, 0:N], lrb.t[:, 0:N], ALU.mult))
                P.dma("sp", None, OO, OBd[h, :, q0:q0 + N], OO.t[:, 0:N])
        P.barrier()
        P.release(KBg + VBg + [DT] + QBT + obst)

    if dbg == 3:
        return nc, es

    with ExitStack() as ph:
        def psb(name, shape, dt):
            return ph.enter_context(nc.sbuf_tensor(name, list(shape), dt))
        wo = psb("wo", [128, 16, D], BF16)
        gft = psb("gft", [128, D], F32)
        W3 = Buf(None)
        P.dma("sp", W3, WB["o"], wo[:], wb_o[:, :, :])
        P.dma("sp", W3, None, gft[:], gffnb[:, :])
        wpc = [Buf(psb("wpc%d" % i, [128, 12, 256], BF16)) for i in range(2)]
        oin = [Buf(psb("oin%d" % i, [128, 12, 512], BF16)) for i in range(2)]
        gin = [Buf(psb("gin%d" % i, [128, 2, 512], BF16)) for i in range(2)]
        mT = Buf(psb("mT", [128, 16, 512], BF16))
        t1 = [Buf(psb("t1%d" % i, [128, 512], F32)) for i in range(2)]
        t2 = [Buf(psb("t2%d" % i, [128, 512], F32)) for i in range(2)]
        xb = [Buf(psb("xb%d" % i, [128, D], F32)) for i in range(2)]
        x1b = [Buf(psb("x1b%d" % i, [128, D], F32)) for i in range(2)]
        s3 = [Buf(psb("s3%d" % i, [128, 4], F32)) for i in range(2)]
        h2s = [Buf(psb("h2s%d" % i, [128, D], BF16)) for i in range(2)]
        h2t = [Buf(psb("h2t%d" % i, [128, 16, 128], BF16)) for i in range(2)]
        zt = Buf(psb("zt", [128, 16, 2], BF16))
        nx_wpc, nx_oin, nx_gin, nx_t1, nx_t2 = _rot(wpc), _rot(oin), _rot(gin), _rot(t1), _rot(t2)
        nx_xb, nx_x1b, nx_s3, nx_h2s, nx_h2t = _rot(xb), _rot(x1b), _rot(s3), _rot(h2s), _rot(h2t)
        nx_pa, nx_pbk = _rot(PS[0:2]), _rot(PS[2:4])
        nx_xo = _rot(PS[4:6])
        ptb = [(PS[i], PSt[i].bitcast(BF16)) for i in (6, 7)]
        nx_pt = _rot(ptb)
        P.op("dve", [], [zt], lambda e: e.memset(zt.t[:], 0.0))
        P.dma("sp", None, zt, H2d[:, :, CVS0:CVS0 + 1], zt.t[:, :, 0:1])
        P.dma("sp", None, zt, H2d[:, :, NCV - 1:NCV], zt.t[:, :, 1:2])
        blk_i = [0]

        QT3 = [(t * 512, 512, "p", t) for t in range(8)] + [(QL, 2, "h", 0)] + [(QS0 + t * 512, 512, "s", t) for t in range(4)]
        if _lim2:
            QT3 = [QT3[i] for i in (0, 1, 8, 12)]
        oil = {}

        def ens_oi(k):
            if k < len(QT3) and k not in oil:
                q0_, N_, _k, _t = QT3[k]
                OIb = nx_oin()
                P.dma("sp", OIb, None, OIb.t[:, 0:8, 0:N_], OAd[:, :, q0_:q0_ + N_].rearrange("h p q -> p h q"))
                P.dma("sp", OIb, None, OIb.t[:, 8:12, 0:N_], OBd[:, :, q0_:q0_ + N_].rearrange("h p q -> p h q"))
                oil[k] = OIb

        for qi_, (q0, N, kind, t) in enumerate(QT3):
            ens_oi(qi_)
            OI = oil.pop(qi_)
            ens_oi(qi_ + 1)
            for fo in range(16):
                if fo % 2 == 0:
                    WP = nx_wpc()
                    cg_ = fo // 2
                    P.dma("sp", WP, WB["pa"], WP.t[:, 0:8, :], wb_pa[:, :, cg_ * 256:(cg_ + 1) * 256])
                    P.dma("sp", WP, WB["pb"], WP.t[:, 8:12, :], wb_pb[:, :, cg_ * 256:(cg_ + 1) * 256])
                fl = fo % 2
                GI = nx_gin()
                P.dma("sp", GI, None, GI.t[:, 0, 0:N], GTd[fo, :, q0:q0 + N])
                P.dma("sp", GI, None, GI.t[:, 1, 0:N], GTd[16 + fo, :, q0:q0 + N])
                A = nx_pa()
                Bk = nx_pbk()

                def mma(e, A=A, fl=fl, OI=OI, WP=WP):
                    for kt in range(8):
                        ins = e.matmul(A.t[:, 0:N], WP.t[:, kt, fl * 128:(fl + 1) * 128], OI.t[:, kt, 0:N], start=(kt == 0), stop=(kt == 7))
                    return ins

                def mmb(e, Bk=Bk, fl=fl, OI=OI, WP=WP):
                    for kt in range(4):
                        ins = e.matmul(Bk.t[:, 0:N], WP.t[:, 8 + kt, fl * 128:(fl + 1) * 128], OI.t[:, 8 + kt, 0:N], start=(kt == 0), stop=(kt == 3))
                    return ins
                P.op("pe", [WP, OI], [A], mma)
                P.op("pe", [WP, OI], [Bk], mmb)
                T1, T2 = nx_t1(), nx_t2()
                P.op("dve", [A, GI], [T1], lambda e, T1=T1, A=A, GI=GI: e.tensor_tensor(T1.t[:, 0:N], A.t[:, 0:N], GI.t[:, 0, 0:N], ALU.mult))
                P.op("dve", [Bk, GI], [T2], lambda e, T2=T2, Bk=Bk, GI=GI: e.tensor_tensor(T2.t[:, 0:N], Bk.t[:, 0:N], GI.t[:, 1, 0:N], ALU.mult))
                P.op("pool", [T1, T2], [mT], lambda e, T1=T1, T2=T2, fo=fo: e.tensor_tensor(mT.t[:, fo, 0:N], T1.t[:, 0:N], T2.t[:, 0:N], ALU.add))
            nblk = (N + 127) // 128

            def x1_part(b):
                nb_ = min(128, N - b * 128)
                XB = nx_xb()
                if kind == "p":
                    r0 = OWN0 + t * 512 + b * 128
                    P.dma("sp", XB, None, XB.t[0:nb_, :], xp[r0:r0 + nb_, :])
                    cvs = [(1 + t * 512 + b * 128, 0, nb_)]
                elif kind == "s":
                    r0 = t * 512 + b * 128
                    P.dma("sp", XB, None, XB.t[0:nb_, :], xs[r0:r0 + nb_, :])
                    cvs = [(CVS0 + 1 + t * 512 + b * 128, 0, nb_)]
                else:
                    P.dma("sp", XB, None, XB.t[0:1, :], xp[OWN0 - 1:OWN0, :])
                    P.dma("sp", XB, None, XB.t[1:2, :], xp[OWN1:OWN1 + 1, :])
                    cvs = [(0, 0, 1), (4097, 1, 1)]
                X1 = nx_x1b()
                for cc in range(4):
                    XO = nx_xo()

                    def mmo(e, XO=XO, cc=cc, b=b, nb_=nb_):
                        for kt in range(16):
                            ins = e.matmul(XO.t[0:nb_, :], mT.t[:, kt, b * 128:b * 128 + nb_], wo[:, kt, cc * 512:(cc + 1) * 512], start=(kt == 0), stop=(kt == 15))
                        return ins
                    P.op("pe", [mT, W3], [XO], mmo)
                    P.op("dve", [XO, XB], [X1], lambda e, X1=X1, XO=XO, XB=XB, cc=cc, nb_=nb_: e.tensor_tensor(
                        X1.t[0:nb_, cc * 512:(cc + 1) * 512], XO.t[0:nb_, :], XB.t[0:nb_, cc * 512:(cc + 1) * 512], ALU.add))
                S3 = nx_s3()
                HS = nx_h2s()
                P.op("act", [X1], [S3, HS], lambda e, X1=X1, S3=S3, HS=HS, nb_=nb_: e.activation(HS.t[0:nb_, :], X1.t[0:nb_, :], AF.Square, accum_out=S3.t[0:nb_, 0:1]))
                P.op("act", [S3, CB], [S3], lambda e, S3=S3, nb_=nb_: e.activation(S3.t[0:nb_, 1:2], S3.t[0:nb_, 0:1], AF.Ln, bias=cst[0:nb_, 0:1], scale=1.0 / D))
                P.op("act", [S3], [S3], lambda e, S3=S3, nb_=nb_: e.activation(S3.t[0:nb_, 1:2], S3.t[0:nb_, 1:2], AF.Exp, scale=-0.5))
                if kind != "h":
                    P.dma("sp", None, X1, X1d[cvs[0][0]:cvs[0][0] + nb_, :], X1.t[0:nb_, :])
                P.op("dve", [X1, S3, W3], [HS], lambda e, HS=HS, X1=X1, S3=S3, nb_=nb_: e.scalar_tensor_tensor(
                    HS.t[0:nb_, :], X1.t[0:nb_, :], S3.t[0:nb_, 1:2], gft[0:nb_, :], ALU.mult, ALU.mult))
                return (HS, nb_, cvs)

            def tr_part(st_):
                HS, nb_, cvs = st_
                HT = nx_h2t()
                blk_i[0] += 1
                for kh in range(2):
                    pb, pvw = nx_pt()

                    def tr(e, HS=HS, pvw=pvw, kh=kh, nb_=nb_):
                        for k in range(8):
                            kt = kh * 8 + k
                            ins = e.transpose(pvw[:, k * 128:k * 128 + nb_], HS.t[0:nb_, kt * 128:(kt + 1) * 128], idt[0:nb_, 0:nb_])
                        return ins
                    P.op("pe", [HS, CB], [pb], tr)
                    src = pvw[:, :].rearrange("p (k t) -> p k t", k=8)[:, :, 0:nb_]
                    dst = HT.t[:, kh * 8:kh * 8 + 8, 0:nb_]
                    if kind == "h":
                        for k in range(8):
                            P.op("dve", [pb, CB], [HT], lambda e, k=k, kh=kh, pvw=pvw, HT=HT: e.tensor_tensor(
                                HT.t[:, kh * 8 + k, 0:2], pvw[:, k * 128:k * 128 + 2], pvt[:, PV_HM:PV_HM + 2], ALU.mult))
                    elif blk_i[0] % 2:
                        P.op("dve", [pb], [HT], lambda e, dst=dst, src=src: e.tensor_copy(dst, src))
                    else:
                        P.op("act", [pb], [HT], lambda e, dst=dst, src=src: e.activation(dst, src, AF.Copy))
                for (cv0, c0, n) in cvs:
                    P.dma("sp", None, HT, H2d[:, :, cv0:cv0 + n], HT.t[:, :, c0:c0 + n])

            prev = None
            for b in range(nblk):
                cur = x1_part(b)
                if prev is not None:
                    tr_part(prev)
                prev = cur
            tr_part(prev)
        P.barrier()
        P.release([W3, zt] + wpc + oin + gin + xb + x1b + h2t)

    if dbg == 4:
        return nc, es

    WIN = [(510 * k, min(512, 4098 - 510 * k), "p") for k in range(9)]
    WIN += [(CVS0 + 510 * k, min(512, 2050 - 510 * k), "s") for k in range(5)]
    if _lim2:
        WIN = [WIN[i] for i in (0, 1, 13)]
    with ExitStack() as ph:
        def psb(name, shape, dt):
            return ph.enter_context(nc.sbuf_tensor(name, list(shape), dt))
        hw = [Buf(psb("hw%d" % i, [128, 16, 512], BF16)) for i in range(2)]
        wu = [Buf(psb("wu%d" % i, [128, 2, 16, 128], BF16)) for i in range(3)]
        gT = Buf(psb("gT", [128, 44, 512], BF16))
        wd = [Buf(psb("wd%d" % i, [128, 44, 256], BF16)) for i in range(2)]
        cg = [Buf(psb("cg%d" % i, [128, 512], F32)) for i in range(2)]
        cv_ = [Buf(psb("cv%d" % i, [128, 512], F32)) for i in range(2)]
        ge = [Buf(psb("ge%d" % i, [128, 512], F32)) for i in range(2)]
        x1r = [Buf(psb("x1r%d" % i, [128, 4, 256], F32)) for i in range(2)]
        yb = [Buf(psb("yb%d" % i, [128, 256], F32)) for i in range(3)]
        nx_hw, nx_wu, nx_wd, nx_cg, nx_cv, nx_ge = _rot(hw), _rot(wu), _rot(wd), _rot(cg), _rot(cv_), _rot(ge)
        nx_x1r, nx_yb = _rot(x1r), _rot(yb)
        nx_ug, nx_uv, nx_dn = _rot(PS[0:2]), _rot(PS[2:4]), _rot(PS[4:8])

        def cw(j, ft):
            c = PV_CW + j * 88 + ft
            return pvt[:, c:c + 1]

        def cb(ft):
            c = PV_CB + ft
            return pvt[:, c:c + 1]

        hwl = {}

        def ens_hw(w):
            if w < len(WIN) and w not in hwl:
                c0_, n_, _k = WIN[w]
                b_ = nx_hw()
                P.dma("sp", b_, None, b_.t[:, :, 0:n_], H2d[:, :, c0_:c0_ + n_])
                hwl[w] = b_
        wul = {}

        def ens_wu(k):
            if k < 44 * len(WIN) and k not in wul:
                j_ = k % 44
                s_ = nx_wu()
                P.dma("sp", s_, WB["up"], s_.t[:, 0, :, :], wb_up[:, :, j_ * 128:(j_ + 1) * 128])
                P.dma("sp", s_, WB["up"], s_.t[:, 1, :, :], wb_up[:, :, DFF + j_ * 128:DFF + (j_ + 1) * 128])
                wul[k] = s_
        wdl = {}

        def ens_wd(k):
            if k < 8 * len(WIN) and k not in wdl:
                w_, c8_ = k // 8, k % 8
                c0_, n_, _k = WIN[w_]
                no_ = n_ - 2
                s_ = nx_wd()
                P.dma("sp", s_, WB["dn"], s_.t[:], wb_dn[:, :, c8_ * 256:(c8_ + 1) * 256])
                xr_ = nx_x1r()
                for b_ in range((no_ + 127) // 128):
                    nb2 = min(128, no_ - b_ * 128)
                    cvr_ = c0_ + 1 + b_ * 128
                    P.dma("sp", xr_, None, xr_.t[0:nb2, b_, :], X1d[cvr_:cvr_ + nb2, c8_ * 256:(c8_ + 1) * 256])
                wdl[k] = (s_, xr_)

        for wi, (c0, n, kind) in enumerate(WIN):
            no = n - 2
            ens_hw(wi)
            HW = hwl.pop(wi)
            ens_hw(wi + 1)
            for j in range(44):
                k = wi * 44 + j
                ens_wu(k)
                ens_wu(k + 1)
                ens_wu(k + 2)
                WU = wul.pop(k)
                UG, UV = nx_ug(), nx_uv()
                for half, U in ((0, UG), (1, UV)):
                    def mmu(e, U=U, half=half, WU=WU):
                        for kt in range(16):
                            ins = e.matmul(U.t[:, 0:n], WU.t[:, half, kt, :], HW.t[:, kt, 0:n], start=(kt == 0), stop=(kt == 15))
                        return ins
                    P.op("pe", [WU, HW], [U], mmu)
                CG, CV = nx_cg(), nx_cv()
                for U, C, ft in ((UG, CG, j), (UV, CV, 44 + j)):
                    P.op("dve", [U, CB], [C], lambda e, U=U, C=C, ft=ft: e.tensor_scalar(
                        C.t[:, 0:no], U.t[:, 0:no], cw(0, ft), cb(ft), ALU.mult, ALU.add))
                    P.op("dve", [U, C, CB], [C], lambda e, U=U, C=C, ft=ft: e.scalar_tensor_tensor(
                        C.t[:, 0:no], U.t[:, 1:no + 1], cw(1, ft), C.t[:, 0:no], ALU.mult, ALU.add))
                    P.op("dve", [U, C, CB], [C], lambda e, U=U, C=C, ft=ft: e.scalar_tensor_tensor(
                        C.t[:, 0:no], U.t[:, 2:no + 2], cw(2, ft), C.t[:, 0:no], ALU.mult, ALU.add))
                GE = nx_ge()
                P.op("act", [CG], [GE], lambda e, GE=GE, CG=CG: e.activation(GE.t[:, 0:no], CG.t[:, 0:no], AF.Gelu))
                P.op("pool", [GE, CV], [gT], lambda e, GE=GE, CV=CV, j=j: e.tensor_tensor(gT.t[:, j, 0:no], GE.t[:, 0:no], CV.t[:, 0:no], ALU.mult))
                if j == 40:
                    ens_wd(wi * 8)
            nblk = (no + 127) // 128
            for c8 in range(8):
                k = wi * 8 + c8
                ens_wd(k)
                WD, XR = wdl.pop(k)
                if c8 < 7:
                    ens_wd(k + 1)
                for b in range(nblk):
                    nb_ = min(128, no - b * 128)
                    DN = nx_dn()

                    def mmd(e, DN=DN, WD=WD, b=b, nb_=nb_):
                        for j in range(44):
                            ins = e.matmul(DN.t[0:nb_, 0:256], gT.t[:, j, b * 128:b * 128 + nb_], WD.t[:, j, :], start=(j == 0), stop=(j == 43))
                        return ins
                    P.op("pe", [gT, WD], [DN], mmd)
                    cvr = c0 + 1 + b * 128
                    YB = nx_yb()
                    P.op("dve", [DN, XR, CB], [YB], lambda e, YB=YB, DN=DN, XR=XR, nb_=nb_, b=b: e.scalar_tensor_tensor(
                        YB.t[0:nb_, 0:256], DN.t[0:nb_, 0:256], cst[0:nb_, 9:10], XR.t[0:nb_, b, :], ALU.add, ALU.add))
                    if kind == "p":
                        dst = yp[cvr - 1:cvr - 1 + nb_, c8 * 256:(c8 + 1) * 256]
                    else:
                        dst = ys[cvr - CVS0 - 1:cvr - CVS0 - 1 + nb_, c8 * 256:(c8 + 1) * 256]
                    P.dma("sp", None, YB, dst, YB.t[0:nb_, 0:256])
        P.barrier()
    return nc, es


def _tables(c):
    r = c % 4
    own_lo, own_hi = 4096 * r, 4096 * (r + 1)
    u = np.arange(SEQ)
    pk = (own_lo - OWN0 + u) % SEQ
    is_own = (u >= OWN0) & (u < OWN1)
    sig = np.where(is_own, 1.0, np.where(pk < own_lo, 1.0, -1.0))
    pk_all = np.concatenate([pk, np.arange(SSEQ)]).astype(np.float64)
    sig_all = np.concatenate([sig, np.ones(SSEQ)])
    jlo = pk_all % 128
    jhi = pk_all - jlo
    kaug = np.zeros((8, 4, NCTX), np.float32)
    for h in range(8):
        s = SLOPE_A[h]
        kaug[h, 0] = -sig_all * s
        kaug[h, 1] = -sig_all * s
        kaug[h, 2] = sig_all * s * jlo
        kaug[h, 3] = sig_all * s * jhi
    qpos = np.zeros(NQ, np.float64)
    qpos[0:4096] = own_lo + np.arange(4096)
    qpos[QL] = own_lo - 1
    qpos[QR] = own_hi
    qpos[QS0:] = np.arange(SSEQ)
    ilo = qpos % 128
    ihi = qpos - ilo
    qaugp = np.stack([ilo, ihi, np.ones(NQ), np.ones(NQ)]).astype(np.float32)
    qaugm = -qaugp
    qaugm[:, QR] = qaugp[:, QR]
    valb = np.zeros((128, 72), np.float32)
    ub = np.arange(BWIN)
    pb = own_lo - OWN0 + ub
    bad = (pb < 0) | (pb >= SEQ)
    valb[:, :56] = np.where(bad, -30000.0, 0.0).reshape(56, 128).T
    hm = np.array([1.0 if r > 0 else 0.0, 1.0 if r < 3 else 0.0], np.float32)
    return kaug, qaugp, qaugm, valb, hm


def _shared_tables():
    k = np.arange(128)[:, None]
    absa = np.abs(k - np.arange(896)[None, :] + 384).astype(np.float32)
    dbt = np.zeros((128, DB_TOT), np.float32)
    for g in range(3):
        cc = np.arange(DB_W[g])[None, :]
        dl = k - cc + DB_C0[g]
        ok = (np.abs(dl) <= 64 * DIL[g]) & (dl % DIL[g] == 0)
        dbt[:, DB_OFF[g]:DB_OFF[g] + DB_W[g]] = np.where(ok, np.abs(dl), BIGD)
    return absa, dbt, np.eye(128, dtype=np.float32)


def make_in_maps(inp):
    f = lambda a: np.ascontiguousarray(a, dtype=np.float32)
    absa, dbt, ident = _shared_tables()
    w_in, w_pa, w_pb, w_o = f(inp["w_in"][0]), f(inp["w_pa"][0]), f(inp["w_pb"][0]), f(inp["w_o"][0])
    w_up, w_dn = f(inp["w_up"][0]), f(inp["w_down"][0])
    pvb = np.zeros((128, NPV), np.float32)
    gmixb = np.ascontiguousarray(np.broadcast_to(inp["g_mix_norm"][0][None, :], (128, D)), dtype=np.float32)
    gffnb = np.ascontiguousarray(np.broadcast_to(inp["g_ffn_norm"][0][None, :], (128, D)), dtype=np.float32)
    pvb[:, PV_GQA] = np.tile(inp["g_qa"][0], 2)
    pvb[:, PV_GKA] = np.tile(inp["g_ka"][0], 2)
    pvb[:, PV_GQB] = inp["g_qb"][0]
    pvb[:, PV_GKB] = inp["g_kb"][0]
    pvb[:, PV_GSUB] = inp["g_subln"][0]
    cwv = inp["conv_w"][0]
    for j in range(3):
        pvb[:, PV_CW + j * 88:PV_CW + (j + 1) * 88] = cwv[j].reshape(88, 128).T
    pvb[:, PV_CB:PV_CB + 88] = inp["conv_b"][0].reshape(88, 128).T
    for i, kname in enumerate(("lam_q1", "lam_k1", "lam_q2", "lam_k2")):
        pvb[:, PV_LAM + 64 * i:PV_LAM + 64 * (i + 1)] = inp[kname][0][None, :]
    pvb[:, PV_GQK:PV_GQK + 64] = inp["g_qa"][0][None, :]
    pvb[:, PV_GQK + 64:PV_GQK + 128] = inp["g_ka"][0][None, :]
    maps = []
    for c in range(8):
        bp, r, bs = c // 4, c % 4, c // 2
        kaug, qaugp, qaugm, valb, hm = _tables(c)
        pvc = pvb.copy()
        pvc[:, PV_HM:PV_HM + 2] = hm[None, :]
        xpl = np.roll(inp["x_prompt"][bp], -(4096 * r - OWN0), axis=0)
        maps.append({
            "xp": f(xpl), "xs": f(inp["x_sample"][bs]),
            "w_in": w_in, "w_pa": w_pa, "w_pb": w_pb, "w_o": w_o, "w_up": w_up, "w_dn": w_dn,
            "pv": pvc, "gmixb": gmixb, "gffnb": gffnb, "kaug": kaug, "qaugp": qaugp, "qaugm": qaugm,
            "absa": absa, "dbt": dbt, "valb": valb, "ident": ident,
        })
    return maps


def kernel(**inp):
    nc, es = build()
    maps = make_in_maps(inp)
    res = run_bass_kernel_spmd(nc, maps, core_ids=list(range(8)))
    es.close()
    yp = np.zeros((2, SEQ, D), np.float32)
    ys = np.zeros((4, SSEQ, D), np.float32)
    for c in range(8):
        bp, r, bs, hs = c // 4, c % 4, c // 2, c % 2
        yp[bp, 4096 * r:4096 * (r + 1)] = res.results[c]["yp"]
        ys[bs, 1024 * hs:1024 * (hs + 1)] = res.results[c]["ys"][1024 * hs:1024 * (hs + 1)]
    return yp, ys
```
